# Optimizing a Trainium2 kernel written in Bass

```python
import jax, jax.numpy as jnp
from jax import lax
import numpy as np

D_MODEL = 1024
BATCH = 32
SEQ = 256
DEPTH = 4
DEC_BATCH = 4
DEC_SEQ = 2048
PAST_LEN = 512

GRID_W = 64
N_MIXERS = 3
N_MLA = (DEPTH + 2) // 3
N_CONV = (DEPTH + 1) // 3
N_RET = DEPTH // 3

MLA_HEADS = 8
MLA_NOPE = 128
MLA_ROPE = 64
MLA_V = 128
MLA_Q_RANK = 384
MLA_KV_RANK = 256
MLA_SCALE = (MLA_NOPE + MLA_ROPE) ** -0.5
ROPE_THETA = 10000.0
ROPE_AXIS_FREQS = MLA_ROPE // 4
Q_BLOCK = 128

CONV_WIDTH = 3

RET_HEADS = 4
RET_DK = D_MODEL // RET_HEADS
RET_DV = 2 * RET_DK
RET_CHUNK = 128

FFN_HIDDEN = -(-8 * D_MODEL // (3 * 256)) * 256
EPS = 1e-6

kernel_name = 'hybrid_diffusion_mla_conv_retention_step'


def rmsnorm(x, g):
    xf = x.astype(jnp.float32)
    y = xf * lax.rsqrt(jnp.mean(xf * xf, axis=-1, keepdims=True) + EPS)
    return (y * g.astype(jnp.float32)).astype(x.dtype)


def modulate(x, g, shift, scale):
    return rmsnorm(x, g) * (1 + scale) + shift


def axial_rope_tables(n_tokens):
    rows = n_tokens // GRID_W
    r = jnp.repeat(jnp.arange(rows, dtype=jnp.float32), GRID_W)
    col = jnp.tile(jnp.arange(GRID_W, dtype=jnp.float32), rows)
    inv = ROPE_THETA ** (-jnp.arange(ROPE_AXIS_FREQS, dtype=jnp.float32) / ROPE_AXIS_FREQS)
    ang = jnp.stack([r[:, None] * inv, col[:, None] * inv], axis=1)
    return jnp.cos(ang), jnp.sin(ang)


def apply_axial_rope(x, cos, sin):
    xr = x.reshape(x.shape[:-1] + (2, 2, ROPE_AXIS_FREQS))
    x1, x2 = xr[..., 0, :], xr[..., 1, :]
    cos = cos.astype(x.dtype)
    sin = sin.astype(x.dtype)
    out = jnp.stack([x1 * cos - x2 * sin, x2 * cos + x1 * sin], axis=-2)
    return out.reshape(x.shape)


def mla_project(h, w_a, q_norm_g, kv_norm_g, w_q_b):
    B, S, _ = h.shape
    a = h @ w_a
    q_a, ckv, k_pe = jnp.split(a, [MLA_Q_RANK, MLA_Q_RANK + MLA_KV_RANK], axis=-1)
    q = (rmsnorm(q_a, q_norm_g) @ w_q_b).reshape(B, S, MLA_HEADS, MLA_NOPE + MLA_ROPE)
    return q[..., :MLA_NOPE], q[..., MLA_NOPE:], rmsnorm(ckv, kv_norm_g), k_pe


def mla_expand(ckv, w_kv_b):
    B, S, _ = ckv.shape
    kv = (ckv @ w_kv_b).reshape(B, S, MLA_HEADS, MLA_NOPE + MLA_V)
    return kv[..., :MLA_NOPE], kv[..., MLA_NOPE:]


def mla_attend(q_nope, q_pe, k_nope, k_pe, v, w_o):
    B, Sq = q_nope.shape[:2]
    nb = Sq // Q_BLOCK

    def to_blocks(t):
        return t.reshape((B, nb, Q_BLOCK) + t.shape[2:]).swapaxes(0, 1)

    def one_block(args):
        qn, qp = args
        s = (jnp.einsum('bqhd,bkhd->bhqk', qn, k_nope)
             + jnp.einsum('bqhr,bkr->bhqk', qp, k_pe))
        p = jax.nn.softmax(s.astype(jnp.float32) * MLA_SCALE, axis=-1).astype(v.dtype)
        return jnp.einsum('bhqk,bkhd->bqhd', p, v)

    o = lax.map(one_block, (to_blocks(q_nope), to_blocks(q_pe)))
    return o.swapaxes(0, 1).reshape(B, Sq, MLA_HEADS * MLA_V) @ w_o


def short_conv_mixer(h, w_in, conv_w, w_out):
    b_gate, c_gate, u = jnp.split(h @ w_in, 3, axis=-1)
    z = c_gate * u
    S = z.shape[1]
    zp = jnp.pad(z, ((0, 0), (1, 1), (0, 0)))
    conv = zp[:, :S] * conv_w[0] + zp[:, 1:S + 1] * conv_w[1] + zp[:, 2:] * conv_w[2]
    return (b_gate * conv) @ w_out


def retention_scan(q, k, v, log_gamma, s0):
    B, S, H, _ = q.shape
    n = S // RET_CHUNK
    idx = jnp.arange(RET_CHUNK, dtype=jnp.float32)
    diff = idx[:, None] - idx[None, :]
    intra = jnp.where(diff[None] >= 0,
                      jnp.exp(jnp.maximum(diff, 0.0)[None] * log_gamma[:, None, None]), 0.0)
    q_decay = jnp.exp((idx[:, None] + 1.0) * log_gamma[None])
    k_decay = jnp.exp((RET_CHUNK - 1.0 - idx)[:, None] * log_gamma[None])
    chunk_decay = jnp.exp(RET_CHUNK * log_gamma)

    def chunks(t):
        return t.reshape((B, n, RET_CHUNK) + t.shape[2:]).swapaxes(0, 1)

    def step(state, qkv):
        qc, kc, vc = qkv
        scores = jnp.einsum('bihd,bjhd->bhij', qc, kc) * intra
        inner = jnp.einsum('bhij,bjhe->bihe', scores, vc)
        cross = jnp.einsum('bihd,bhde->bihe', qc, state) * q_decay[None, :, :, None]
        new_state = (state * chunk_decay[None, :, None, None]
                     + jnp.einsum('bjhd,bjhe->bhde', kc * k_decay[None, :, :, None], vc))
        return new_state, inner + cross

    s_final, o = lax.scan(step, s0, (chunks(q), chunks(k), chunks(v)))
    return o.swapaxes(0, 1).reshape(B, S, H, v.shape[-1]), s_final


def retention_mixer(h, w_in, log_rate, gn_g, w_out, s0_fwd, s0_bwd):
    B, S, _ = h.shape
    f32 = jnp.float32
    hk, hv = RET_HEADS * RET_DK, RET_HEADS * RET_DV
    q, k, v, g = jnp.split(h @ w_in, [hk, 2 * hk, 2 * hk + hv], axis=-1)
    q = q.reshape(B, S, RET_HEADS, RET_DK).astype(f32)
    k = k.reshape(B, S, RET_HEADS, RET_DK).astype(f32) * (RET_DK ** -0.5)
    v = v.reshape(B, S, RET_HEADS, RET_DV).astype(f32)
    log_gamma = -jnp.exp(log_rate.astype(f32))
    o_f, s_f = retention_scan(q, k, v, log_gamma[0], s0_fwd.astype(f32))
    o_b, s_b = retention_scan(q[:, ::-1], k[:, ::-1], v[:, ::-1], log_gamma[1], s0_bwd.astype(f32))
    o = o_f + o_b[:, ::-1]
    mu = jnp.mean(o, axis=-1, keepdims=True)
    var = jnp.mean(jnp.square(o - mu), axis=-1, keepdims=True)
    o = ((o - mu) * lax.rsqrt(var + EPS)).reshape(B, S, hv) * gn_g.astype(f32)
    y = jax.nn.silu(g) * o.astype(h.dtype)
    return y @ w_out, s_f.astype(h.dtype), s_b.astype(h.dtype)


def swiglu(h, w_in, w_out):
    a, b = jnp.split(h @ w_in, 2, axis=-1)
    return (jax.nn.silu(a) * b) @ w_out


def setup_inputs(seed: int = 0) -> dict:
    key = jax.random.key(seed)
    ks = jax.random.split(key, 32)
    f32 = jnp.float32

    def nrm(k, shape, scale):
        return jax.random.normal(k, shape, f32) * scale

    def gain(k, shape):
        return 1.0 + 0.05 * jax.random.normal(k, shape, f32)

    D = D_MODEL
    hk, hv = RET_HEADS * RET_DK, RET_HEADS * RET_DV
    base_rate = jnp.log(-jnp.log1p(-(2.0 ** (-5.0 - jnp.arange(RET_HEADS, dtype=f32)))))
    ret_log_rate = base_rate[None, None, :] + 0.1 * jax.random.normal(ks[25], (N_RET, 2, RET_HEADS), f32)
    return {
        'x_prompt': nrm(ks[0], (BATCH, SEQ, D), 1.0),
        'x_sample': nrm(ks[1], (DEC_BATCH, DEC_SEQ, D), 1.0),
        'c': nrm(ks[2], (DEC_BATCH, D), 1.0),
        'c_ctx': nrm(ks[3], (D,), 1.0),
        'cache_mla_ckv': nrm(ks[4], (DEC_BATCH, N_MLA, PAST_LEN, MLA_KV_RANK), 1.0),
        'cache_mla_kpe': nrm(ks[5], (DEC_BATCH, N_MLA, PAST_LEN, MLA_ROPE), 1.0),
        'state_ret': nrm(ks[6], (DEC_BATCH, N_RET, 2, RET_HEADS, RET_DK, RET_DV), 0.5),
        'ada_w': nrm(ks[7], (DEPTH, D, 6 * D), 0.5 * D ** -0.5),
        'ada_b': nrm(ks[8], (DEPTH, 6 * D), 0.02),
        'norm_mix_g': gain(ks[9], (DEPTH, D)),
        'norm_ffn_g': gain(ks[10], (DEPTH, D)),
        'mla_w_a': nrm(ks[11], (N_MLA, D, MLA_Q_RANK + MLA_KV_RANK + MLA_ROPE), D ** -0.5),
        'mla_q_norm_g': gain(ks[12], (N_MLA, MLA_Q_RANK)),
        'mla_kv_norm_g': gain(ks[13], (N_MLA, MLA_KV_RANK)),
        'mla_w_q_b': nrm(ks[14], (N_MLA, MLA_Q_RANK, MLA_HEADS * (MLA_NOPE + MLA_ROPE)), MLA_Q_RANK ** -0.5),
        'mla_w_kv_b': nrm(ks[15], (N_MLA, MLA_KV_RANK, MLA_HEADS * (MLA_NOPE + MLA_V)), MLA_KV_RANK ** -0.5),
        'mla_w_o': nrm(ks[16], (N_MLA, MLA_HEADS * MLA_V, D), (MLA_HEADS * MLA_V) ** -0.5),
        'conv_w_in': nrm(ks[17], (N_CONV, D, 3 * D), D ** -0.5),
        'conv_w': nrm(ks[18], (N_CONV, CONV_WIDTH, D), CONV_WIDTH ** -0.5),
        'conv_w_out': nrm(ks[19], (N_CONV, D, D), D ** -0.5),
        'ret_w_in': nrm(ks[20], (N_RET, D, 2 * hk + 2 * hv), D ** -0.5),
        'ret_log_rate': ret_log_rate,
        'ret_gn_g': gain(ks[21], (N_RET, hv)),
        'ret_w_out': nrm(ks[22], (N_RET, hv, D), hv ** -0.5),
        'ffn_w_in': nrm(ks[23], (DEPTH, D, 2 * FFN_HIDDEN), D ** -0.5),
        'ffn_w_out': nrm(ks[24], (DEPTH, FFN_HIDDEN, D), FFN_HIDDEN ** -0.5),
        'final_norm_g': gain(ks[26], (D,)),
    }


def reference(x_prompt, x_sample, c, c_ctx, cache_mla_ckv, cache_mla_kpe, state_ret,
              ada_w, ada_b, norm_mix_g, norm_ffn_g,
              mla_w_a, mla_q_norm_g, mla_kv_norm_g, mla_w_q_b, mla_w_kv_b, mla_w_o,
              conv_w_in, conv_w, conv_w_out,
              ret_w_in, ret_log_rate, ret_gn_g, ret_w_out,
              ffn_w_in, ffn_w_out, final_norm_g):
    cos, sin = axial_rope_tables(x_sample.shape[1])
    xc, xs = x_prompt, x_sample
    new_ckv, new_kpe, new_ret = [], [], []
    for i in range(DEPTH):
        kind, j = i % N_MIXERS, i // N_MIXERS
        mod_c = jnp.split((jax.nn.silu(c_ctx) @ ada_w[i] + ada_b[i])[None, None, :], 6, axis=-1)
        mod_s = jnp.split((jax.nn.silu(c) @ ada_w[i] + ada_b[i])[:, None, :], 6, axis=-1)
        hc = modulate(xc, norm_mix_g[i], mod_c[0], mod_c[1])
        hs = modulate(xs, norm_mix_g[i], mod_s[0], mod_s[1])
        if kind == 0:
            qn, qp, ckv, kpe = mla_project(hc, mla_w_a[j], mla_q_norm_g[j], mla_kv_norm_g[j], mla_w_q_b[j])
            kn, vv = mla_expand(ckv, mla_w_kv_b[j])
            oc = mla_attend(qn, qp, kn, kpe, vv, mla_w_o[j])
            new_ckv.append(ckv)
            new_kpe.append(kpe)
            qn_s, qp_s, ckv_s, kpe_s = mla_project(hs, mla_w_a[j], mla_q_norm_g[j], mla_kv_norm_g[j], mla_w_q_b[j])
            qp_s = apply_axial_rope(qp_s, cos[:, None], sin[:, None])
            kpe_s = apply_axial_rope(kpe_s, cos, sin)
            kn_s, v_s = mla_expand(ckv_s, mla_w_kv_b[j])
            kn_p, v_p = mla_expand(cache_mla_ckv[:, j], mla_w_kv_b[j])
            os_ = mla_attend(qn_s, qp_s,
                             jnp.concatenate([kn_p, kn_s], axis=1),
                             jnp.concatenate([cache_mla_kpe[:, j], kpe_s], axis=1),
                             jnp.concatenate([v_p, v_s], axis=1), mla_w_o[j])
        elif kind == 1:
            oc = short_conv_mixer(hc, conv_w_in[j], conv_w[j], conv_w_out[j])
            os_ = short_conv_mixer(hs, conv_w_in[j], conv_w[j], conv_w_out[j])
        else:
            zeros = jnp.zeros((xc.shape[0], RET_HEADS, RET_DK, RET_DV), xc.dtype)
            oc, s_f, s_b = retention_mixer(hc, ret_w_in[j], ret_log_rate[j], ret_gn_g[j], ret_w_out[j], zeros, zeros)
            new_ret.append(jnp.stack([s_f, s_b], axis=1))
            os_, _, _ = retention_mixer(hs, ret_w_in[j], ret_log_rate[j], ret_gn_g[j], ret_w_out[j],
                                        state_ret[:, j, 0], state_ret[:, j, 1])
        xc = xc + mod_c[2] * oc
        xs = xs + mod_s[2] * os_
        hc = modulate(xc, norm_ffn_g[i], mod_c[3], mod_c[4])
        hs = modulate(xs, norm_ffn_g[i], mod_s[3], mod_s[4])
        xc = xc + mod_c[5] * swiglu(hc, ffn_w_in[i], ffn_w_out[i])
        xs = xs + mod_s[5] * swiglu(hs, ffn_w_in[i], ffn_w_out[i])
    y_prompt = rmsnorm(xc, final_norm_g)
    y_sample = rmsnorm(xs, final_norm_g)
    return (y_prompt, y_sample, jnp.stack(new_ckv, axis=1), jnp.stack(new_kpe, axis=1), jnp.stack(new_ret, axis=1))
```

```python
import numpy as np
from contextlib import ExitStack
import concourse.bass as bass
import concourse.mybir as mybir
from concourse.bass_utils import run_bass_kernel_spmd

F32 = mybir.dt.float32
BF16 = mybir.dt.bfloat16
AF = mybir.ActivationFunctionType
ALU = mybir.AluOpType

D = 1024
NT = 2048
TB = 4
FFN_H = 2816
EPS = 1e-6
MLA_SCALE = 192 ** -0.5
NEG = -30000.0

_CO = {}
_n = 0
for _name, _w in [("cv", 8), ("adab", 192), ("gmix", 32), ("gffn", 32), ("gfin", 8), ("qng", 6),
                  ("kvg", 4), ("convw", 24), ("lr", 8), ("coef", 8), ("cmf", 16), ("cmb", 16),
                  ("btab", 160), ("identf", 128), ("ones", 128), ("maskf", 128), ("maskb", 128)]:
    _CO[_name] = _n
    _n += _w
NCST = _n


class DS:
    def __init__(self, sem, key):
        self.sem = sem
        self.cnt = 0
        self.key = key


class Buf:
    __slots__ = ("ap", "w", "r", "ds")

    def __init__(self, ap):
        self.ap = ap
        self.w = None
        self.r = {}
        self.ds = None


class Eng:
    def __init__(self, eng, sem):
        self.eng = eng
        self.sem = sem
        self.cnt = 0
        self.seen = {}


class K:
    def __init__(self, nc, es):
        self.nc = nc
        self.E = {}
        for name, eng in [("pe", nc.tensor), ("act", nc.scalar), ("dve", nc.vector),
                          ("pool", nc.gpsimd), ("sp", nc.sync)]:
            self.E[name] = Eng(eng, es.enter_context(nc.semaphore("s_" + name)))
        self.free_ds = {"sp": [], "pool": []}
        self.all_ds = {}
        for i in range(64):
            d = DS(es.enter_context(nc.semaphore("d%d" % i)), ("d", i))
            d.q = "sp" if i < 24 else "pool"
            self.free_ds[d.q].append(d)
            self.all_ds[d.key] = d
        self.dirty = set()

    def _sem_of(self, key):
        if isinstance(key, tuple):
            d = self.all_ds[key]
            return d.sem, 16
        return self.E[key].sem, 1

    def _waits(self, en, reads, writes):
        e = self.E[en]
        need = {}
        for b in reads:
            if b.w is not None and b.w[1] > need.get(b.w[0], 0):
                need[b.w[0]] = b.w[1]
        for b in writes:
            if b.w is not None and b.w[1] > need.get(b.w[0], 0):
                need[b.w[0]] = b.w[1]
            for k, v in b.r.items():
                if k == en and en == "pe":
                    continue
                if v > need.get(k, 0):
                    need[k] = v
        for key, val in need.items():
            if key == en and en == "pe":
                continue
            if e.seen.get(key, 0) >= val:
                continue
            sem, mult = self._sem_of(key)
            e.eng.wait_ge(sem, val * mult)
            e.seen[key] = val

    def op(self, en, fn, reads=(), writes=()):
        e = self.E[en]
        self._waits(en, reads, writes)
        inst = fn(e.eng)
        e.cnt += 1
        inst.then_inc(e.sem, 1)
        for b in writes:
            b.w = (en, e.cnt)
            b.r = {}
        for b in reads:
            b.r[en] = e.cnt

    def dma(self, q, out, in_, reads=(), writes=(), **kw):
        owner = writes[0] if writes else reads[0]
        if owner.ds is None:
            owner.ds = self.free_ds[q].pop()
        ds = owner.ds
        assert ds.q == q
        e = self.E[q]
        self._waits(q, reads, writes)
        if q == "pool":
            kw.setdefault("max_dma_last_dim", 8192)
        inst = e.eng.dma_start(out=out, in_=in_, **kw)
        ds.cnt += 1
        inst.then_inc(ds.sem, 16)
        self.dirty.add(ds.key)
        for b in writes:
            b.w = (ds.key, ds.cnt)
            b.r = {}
        for b in reads:
            b.r[ds.key] = ds.cnt

    def barrier(self):
        keys = [(k, self.E[k].cnt) for k in ("pe", "act", "dve", "pool") if self.E[k].cnt > 0]
        keys += [(k, self.all_ds[k].cnt) for k in sorted(self.dirty)]
        self.dirty = set()
        for en, e in self.E.items():
            for key, val in keys:
                if key == en:
                    continue
                if e.seen.get(key, 0) >= val:
                    continue
                sem, mult = self._sem_of(key)
                e.eng.wait_ge(sem, val * mult)
                e.seen[key] = val

    def release(self, bufs):
        for b in bufs:
            if b.ds is not None:
                self.free_ds[b.ds.q].append(b.ds)
                b.ds = None


_DBG = set()
RSUM_ENG = ("dve", "pool")


def build(layers=(0, 1, 2, 3)):
    nc = bass.Bass("TRN2", target_bir_lowering=False)

    def din(name, shape):
        return nc.dram_tensor(name, list(shape), F32, kind="ExternalInput").ap()

    def dout(name, shape):
        return nc.dram_tensor(name, list(shape), F32, kind="ExternalOutput").ap()

    x_d = din("x", [NT, D])
    cst_d = din("cst", [128, NCST])
    rope_d = din("rope", [2, 64, NT])
    cmask_d = din("cmask", [2, 128, NT])
    gng_d = din("gng", [128, 2048])
    st0_d = din("st0", [2, 4, 256, 512])
    cckv_d = din("cckv", [2, 256, 512])
    ckpe_d = din("ckpe", [2, 64, 512])
    ada_w = din("ada_w", [4, D, 6 * D])
    mla_w_a = din("mla_w_a", [2, D, 704])
    mla_w_q_b = din("mla_w_q_b", [2, 384, 1536])
    mla_w_kv_b = din("mla_w_kv_b", [2, 256, 2048])
    mla_w_o = din("mla_w_o", [2, 1024, D])
    conv_w_in = din("conv_w_in", [1, D, 3 * D])
    conv_w_out = din("conv_w_out", [1, D, D])
    ret_w_in = din("ret_w_in", [1, D, 6144])
    ret_w_out = din("ret_w_out", [1, 2048, D])
    ffn_w_in = din("ffn_w_in", [4, D, 2 * FFN_H])
    ffn_w_out = din("ffn_w_out", [4, FFN_H, D])

    y_d = dout("y", [NT, D])
    ockv_d = dout("ockv", [2, NT, 256])
    okpe_d = dout("okpe", [2, NT, 64])
    ost_d = dout("ost", [8, 2, 4, 256, 512])

    es = ExitStack()
    with es:
        k = K(nc, es)
        op, dma = k.op, k.dma

        uid = [0]

        def sb(scope, name, shape, dt=F32):
            uid[0] += 1
            return scope.enter_context(nc.sbuf_tensor("%s_%d" % (name, uid[0]), list(shape), dt))

        X = sb(es, "X", [128, 8, NT])
        Xb = [Buf(X[:, :, t * 512:(t + 1) * 512]) for t in range(TB)]
        CST = sb(es, "CST", [128, NCST])
        cstB = Buf(CST)
        CB = sb(es, "CB", [128, 4, 128], BF16)
        cbB = Buf(CB)
        MODS = sb(es, "MODS", [128, 4, 48])
        G12 = sb(es, "G12", [128, 4, 16])
        modB = Buf(MODS)
        SCV = sb(es, "SCV", [128, 8], BF16)
        scvB = Buf(SCV)
        DEC = sb(es, "DEC", [128, 4, 8])
        decB = Buf(DEC)
        PSt = [es.enter_context(nc.psum_tensor("ps%d" % i, [128, 512], F32)) for i in range(8)]
        PS = [Buf(t) for t in PSt]
        rr = [0]

        def ps_next(banks=range(8)):
            banks = list(banks)
            b = banks[rr[0] % len(banks)]
            rr[0] += 1
            return PS[b]

        def c_(name, a=0, b=None):
            o = _CO[name]
            if b is None:
                b = a + 1
            return CST[:, o + a:o + b]

        identf = c_("identf", 0, 128)
        identb = CB[:, 0, :]
        onesb = CB[:, 1, :]
        maskf = CB[:, 2, :]
        maskb = CB[:, 3, :]

        def ada_dma(i, b, AW, awB):
            src = ada_w[i].rearrange("(c p) n -> p c n", p=128)
            dma("pool", AW[b % 2][:], src[:, :, b * 384:(b + 1) * 384], writes=[awB[b % 2]])

        def ada_pe(b, pm, AW, awB):
            s_ = b % 2
            for n in range(3):
                col = b * 3 + n
                for kc in range(8):
                    op("pe", lambda e: e.matmul(pm.ap[:, col:col + 1], lhsT=AW[s_][:, kc, n * 128:(n + 1) * 128],
                                                rhs=SCV[:, kc:kc + 1], start=(kc == 0), stop=(kc == 7)),
                       [awB[s_], scvB], [pm])

        def ada_finish(i, pm):
            op("dve", lambda e: e.tensor_tensor(out=MODS[:, i, :], in0=pm.ap[:, 0:48],
                                                in1=c_("adab", i * 48, i * 48 + 48), op=ALU.add), [pm, cstB], [modB])
            op("dve", lambda e: e.scalar_tensor_tensor(out=G12[:, i, 0:8], in0=MODS[:, i, 8:16], scalar=1.0,
                                                       in1=c_("gmix", i * 8, i * 8 + 8), op0=ALU.add, op1=ALU.mult),
               [modB, cstB], [modB])
            op("dve", lambda e: e.scalar_tensor_tensor(out=G12[:, i, 8:16], in0=MODS[:, i, 32:40], scalar=1.0,
                                                       in1=c_("gffn", i * 8, i * 8 + 8), op0=ALU.add, op1=ALU.mult),
               [modB, cstB], [modB])

        dma("sp", CST[:], cst_d[:, :], writes=[cstB])
        op("dve", lambda e: e.tensor_copy(out=CB[:, 0, :], in_=c_("identf", 0, 128)), [cstB], [cbB])
        op("dve", lambda e: e.tensor_copy(out=CB[:, 1, :], in_=c_("ones", 0, 128)), [cstB], [cbB])
        op("dve", lambda e: e.tensor_copy(out=CB[:, 2, :], in_=c_("maskf", 0, 128)), [cstB], [cbB])
        op("dve", lambda e: e.tensor_copy(out=CB[:, 3, :], in_=c_("maskb", 0, 128)), [cstB], [cbB])
        op("act", lambda e: e.activation(out=SCV[:], in_=c_("cv", 0, 8), func=AF.Silu), [cstB], [scvB])
        with ExitStack() as ph:
            ELR = sb(ph, "ELR", [128, 8])
            elrB = Buf(ELR)
            op("act", lambda e: e.activation(out=ELR[:], in_=c_("lr", 0, 8), func=AF.Exp), [cstB], [elrB])
            for t, (cf, cb_) in enumerate([(0, 1), (2, 3), (4, 5), (6, 6)]):
                op("act", lambda e, t=t, cf=cf: e.activation(out=DEC[:, t, 0:4], in_=ELR[:, 0:4], func=AF.Exp,
                                                            scale=c_("coef", cf)), [elrB, cstB], [decB])
                op("act", lambda e, t=t, cb_=cb_: e.activation(out=DEC[:, t, 4:8], in_=ELR[:, 4:8], func=AF.Exp,
                                                              scale=c_("coef", cb_)), [elrB, cstB], [decB])
            op("dve", lambda e: e.tensor_scalar(out=DEC[:, 0, :], in0=DEC[:, 0, :], scalar1=0.0625, scalar2=None,
                                                op0=ALU.mult), [decB], [decB])
            op("dve", lambda e: e.tensor_scalar(out=DEC[:, 2, :], in0=DEC[:, 2, :], scalar1=0.0625, scalar2=None,
                                                op0=ALU.mult), [decB], [decB])
            XS = [sb(ph, "XS%d" % i, [128, D]) for i in range(3)]
            xsB = [Buf(t) for t in XS]
            for tt in range(16):
                s = tt % 3
                dma("sp", XS[s][:], x_d[tt * 128:(tt + 1) * 128, :], writes=[xsB[s]])
                for half in range(2):
                    pb = ps_next()
                    for q in range(4):
                        fc = half * 4 + q
                        op("pe", lambda e, pb=pb, q=q, fc=fc, s=s: e.transpose(
                            out=pb.ap[:, q * 128:(q + 1) * 128], in_=XS[s][:, fc * 128:(fc + 1) * 128],
                            identity=identf), [xsB[s], cstB], [pb])
                    eng = "act" if half == 0 else "dve"
                    if eng == "act":
                        op("act", lambda e, pb=pb, half=half, tt=tt: e.copy(
                            out=X[:, half * 4:half * 4 + 4, tt * 128:(tt + 1) * 128],
                            in_=pb.ap.rearrange("p (q t) -> p q t", q=4)), [pb], [Xb[tt // 4]])
                    else:
                        op("dve", lambda e, pb=pb, half=half, tt=tt: e.tensor_copy(
                            out=X[:, half * 4:half * 4 + 4, tt * 128:(tt + 1) * 128],
                            in_=pb.ap.rearrange("p (q t) -> p q t", q=4)), [pb], [Xb[tt // 4]])
            AW = [sb(ph, "AW%d" % i, [128, 8, 1536], BF16) for i in range(4)]
            awB = [Buf(t) for t in AW]
            if layers:
                pm = PS[7]
                src0 = ada_w[layers[0]].rearrange("(c p) n -> p c n", p=128)
                for bl in range(4):
                    for hf in range(2):
                        dma("pool", AW[bl][:, hf * 4:hf * 4 + 4, :],
                            src0[:, hf * 4:hf * 4 + 4, bl * 1536:(bl + 1) * 1536], writes=[awB[bl]])
                for bl in range(4):
                    for n in range(12):
                        col = bl * 12 + n
                        for kc in range(8):
                            op("pe", lambda e: e.matmul(pm.ap[:, col:col + 1], lhsT=AW[bl][:, kc, n * 128:(n + 1) * 128],
                                                        rhs=SCV[:, kc:kc + 1], start=(kc == 0), stop=(kc == 7)),
                               [awB[bl], scvB], [pm])
                ada_finish(layers[0], pm)
            k.barrier()
            k.release(xsB + awB + [elrB])

        def rms_rstd(scope_bufs, srcs, nchunk, out_rstd, out_buf, src_reads, inv_n):
            SQ, sqB = scope_bufs
            pss = ps_next()
            for c in range(nchunk):
                s = c % 2
                op("act", lambda e, c=c, s=s: e.activation(out=SQ[s][:], in_=srcs[c], func=AF.Square),
                   src_reads, [sqB[s]])
                op("pe", lambda e, c=c, s=s, pss=pss: e.matmul(pss.ap[:], lhsT=onesb, rhs=SQ[s][:],
                                                              start=(c == 0), stop=(c == nchunk - 1)),
                   [sqB[s], cbB], [pss])
            op("dve", lambda e, pss=pss: e.tensor_scalar(out=out_rstd, in0=pss.ap[:], scalar1=inv_n, scalar2=EPS,
                                                         op0=ALU.mult, op1=ALU.add), [pss], [out_buf])
            op("act", lambda e: e.activation(out=out_rstd, in_=out_rstd, func=AF.Sqrt), [out_buf], [out_buf])
            op("dve", lambda e: e.reciprocal(out=out_rstd, in_=out_rstd), [out_buf], [out_buf])

        mSQ = [sb(es, "mSQ%d" % s_, [128, 512], BF16) for s_ in range(2)]
        msqB = [Buf(t) for t in mSQ]
        mRS = [sb(es, "mRS%d" % s_, [128, 512]) for s_ in range(3)]
        mrsB = [Buf(t) for t in mRS]
        mTM = [sb(es, "mTM%d" % s_, [128, 512]) for s_ in range(2)]
        mtmB = [Buf(t) for t in mTM]
        mcnt = [0]

        def modulate(ph, i, which, H, Hb):
            goff = 0 if which == 0 else 8
            shoff = 0 if which == 0 else 24
            pss = {}

            def st1a(tb):
                r = tb % 3
                pss[tb] = ps_next()
                for c in range(8):
                    s_ = c % 2
                    op("act", lambda e: e.activation(out=mSQ[s_][:], in_=X[:, c, tb * 512:(tb + 1) * 512],
                                                     func=AF.Square), [Xb[tb]], [msqB[s_]])
                    op("pe", lambda e: e.matmul(pss[tb].ap[:], lhsT=onesb, rhs=mSQ[s_][:], start=(c == 0),
                                                stop=(c == 7)), [msqB[s_], cbB], [pss[tb]])
                op("dve", lambda e: e.tensor_scalar(out=mRS[r][:], in0=pss[tb].ap[:], scalar1=1.0 / D, scalar2=EPS,
                                                    op0=ALU.mult, op1=ALU.add), [pss[tb]], [mrsB[r]])

            def st1b(tb):
                r = tb % 3
                op("act", lambda e: e.activation(out=mRS[r][:], in_=mRS[r][:], func=AF.Sqrt), [mrsB[r]], [mrsB[r]])
                op("dve", lambda e: e.reciprocal(out=mRS[r][:], in_=mRS[r][:]), [mrsB[r]], [mrsB[r]])

            def st2(tb):
                r = tb % 3
                for fc in range(8):
                    s_ = mcnt[0] % 2
                    mcnt[0] += 1
                    op("dve", lambda e: e.scalar_tensor_tensor(
                        out=mTM[s_][:], in0=X[:, fc, tb * 512:(tb + 1) * 512], scalar=G12[:, i, goff + fc:goff + fc + 1],
                        in1=mRS[r][:], op0=ALU.mult, op1=ALU.mult), [Xb[tb], modB, mrsB[r]], [mtmB[s_]])
                    op("act", lambda e: e.activation(
                        out=H[:, fc, tb * 512:(tb + 1) * 512], in_=mTM[s_][:], func=AF.Identity,
                        bias=MODS[:, i, shoff + fc:shoff + fc + 1], scale=1.0), [mtmB[s_], modB], [Hb[tb]])

            st1a(0)
            st1a(1)
            st1b(0)
            for tb in range(TB):
                if tb + 2 < TB:
                    st1a(tb + 2)
                if tb + 1 < TB:
                    st1b(tb + 1)
                st2(tb)
            return []

        def resid_add(pb, f, tb, i, goff):
            op("dve", lambda e: e.scalar_tensor_tensor(
                out=X[:, f, tb * 512:(tb + 1) * 512], in0=pb.ap[:], scalar=MODS[:, i, goff + f:goff + f + 1],
                in1=X[:, f, tb * 512:(tb + 1) * 512], op0=ALU.mult, op1=ALU.add), [pb, modB, Xb[tb]], [Xb[tb]])

        def ffn(i, inext=None, is_last=False):
            with ExitStack() as ph:
                H = sb(ph, "fH", [128, 8, NT], BF16)
                Hb = [Buf(H[:, :, t * 512:(t + 1) * 512]) for t in range(TB)]
                tmp = []
                if inext is not None:
                    AW = [sb(ph, "fAW%d" % s, [128, 8, 384], BF16) for s in range(2)]
                    awB = [Buf(t) for t in AW]
                else:
                    AW, awB = [], []
                pm = PS[7]
                if is_last:
                    YF = [sb(ph, "oYF%d" % s, [128, 8, 128]) for s in range(2)]
                    yfB = [Buf(t) for t in YF]
                    YS = [sb(ph, "oYS%d" % s, [128, D]) for s in range(2)]
                    ysB = [Buf(t) for t in YS]
                    tmp = yfB + ysB

                def final_tb(tb):
                    r = tb % 3
                    pss = ps_next(range(7))
                    for c in range(8):
                        s_ = c % 2
                        op("act", lambda e: e.activation(out=mSQ[s_][:], in_=X[:, c, tb * 512:(tb + 1) * 512],
                                                         func=AF.Square), [Xb[tb]], [msqB[s_]])
                        op("pe", lambda e: e.matmul(pss.ap[:], lhsT=onesb, rhs=mSQ[s_][:], start=(c == 0),
                                                    stop=(c == 7)), [msqB[s_], cbB], [pss])
                    op("dve", lambda e: e.tensor_scalar(out=mRS[r][:], in0=pss.ap[:], scalar1=1.0 / D, scalar2=EPS,
                                                        op0=ALU.mult, op1=ALU.add), [pss], [mrsB[r]])
                    op("act", lambda e: e.activation(out=mRS[r][:], in_=mRS[r][:], func=AF.Sqrt), [mrsB[r]], [mrsB[r]])
                    op("dve", lambda e: e.reciprocal(out=mRS[r][:], in_=mRS[r][:]), [mrsB[r]], [mrsB[r]])
                    for q in range(4):
                        tt = tb * 4 + q
                        y_ = tt % 2
                        tok = slice(tt * 128, (tt + 1) * 128)
                        for fc in range(8):
                            op("dve", lambda e: e.scalar_tensor_tensor(
                                out=YF[y_][:, fc, :], in0=X[:, fc, tok], scalar=c_("gfin", fc),
                                in1=mRS[r][:, q * 128:(q + 1) * 128], op0=ALU.mult, op1=ALU.mult),
                               [Xb[tb], cstB, mrsB[r]], [yfB[y_]])
                        for half in range(2):
                            pb = ps_next(range(7))
                            for rr_ in range(4):
                                fc = half * 4 + rr_
                                op("pe", lambda e: e.transpose(out=pb.ap[:, rr_ * 128:(rr_ + 1) * 128],
                                                               in_=YF[y_][:, fc, :], identity=identf),
                                   [yfB[y_], cstB], [pb])
                            if half == 0:
                                op("act", lambda e: e.copy(out=YS[y_][:, 0:512], in_=pb.ap[:]), [pb], [ysB[y_]])
                            else:
                                op("dve", lambda e: e.tensor_copy(out=YS[y_][:, 512:1024], in_=pb.ap[:]),
                                   [pb], [ysB[y_]])
                        dma("sp", y_d[tok, :], YS[y_][:], reads=[ysB[y_]])
                ACTB = sb(ph, "fACT", [128, 4, NT], BF16)
                actB = [Buf(ACTB[:, :, t * 512:(t + 1) * 512]) for t in range(TB)]
                WI = [sb(ph, "fWI%d" % s, [128, 8, 2, 512], BF16) for s in range(2)]
                wiB = [Buf(t) for t in WI]
                WO = [sb(ph, "fWO%d" % s, [128, 4, D], BF16) for s in range(2)]
                woB = [Buf(t) for t in WO]
                SA = [sb(ph, "fSA%d" % s, [128, 512]) for s in range(2)]
                saB = [Buf(t) for t in SA]
                win = ffn_w_in[i].rearrange("(c p) n -> p c n", p=128)
                wout = ffn_w_out[i].rearrange("(j p) n -> p j n", p=128)
                groups = [(0, 4), (4, 4), (8, 4), (12, 4), (16, 4), (20, 2)]
                nsa = 0
                def load_w(g):
                    j0_, nj_ = groups[g]
                    s_ = g % 2
                    for ab in range(2):
                        c0 = ab * FFN_H + j0_ * 128
                        dma("pool", WI[s_][:, :, ab, 0:nj_ * 128], win[:, :, c0:c0 + nj_ * 128], writes=[wiB[s_]])
                    dma("pool", WO[s_][:, 0:nj_, :], wout[:, j0_:j0_ + nj_, :], writes=[woB[s_]])

                load_w(0)
                if inext is not None:
                    for b in range(0, 2):
                        ada_dma(inext, b, AW, awB)
                modulate(ph, i, 1, H, Hb)
                for g, (j0, nj) in enumerate(groups):
                    s = g % 2
                    if inext is not None and 0 < g < 4:
                        for b in range(4 * g, 4 * g + 2):
                            ada_dma(inext, b, AW, awB)
                    for tb in range(TB):
                        for j in range(nj):
                            pa = ps_next(range(7))
                            pbb = ps_next(range(7))
                            for ab, pp in ((0, pa), (1, pbb)):
                                for kc in range(8):
                                    op("pe", lambda e, pp=pp, kc=kc, ab=ab, j=j, tb=tb, s=s: e.matmul(
                                        pp.ap[:], lhsT=WI[s][:, kc, ab, j * 128:(j + 1) * 128],
                                        rhs=H[:, kc, tb * 512:(tb + 1) * 512], start=(kc == 0), stop=(kc == 7)),
                                       [wiB[s], Hb[tb]], [pp])
                            q = nsa % 2
                            nsa += 1
                            op("act", lambda e, pa=pa, q=q: e.activation(out=SA[q][:], in_=pa.ap[:], func=AF.Silu),
                               [pa], [saB[q]])
                            op("dve", lambda e, pbb=pbb, q=q, j=j, tb=tb: e.tensor_tensor(
                                out=ACTB[:, j, tb * 512:(tb + 1) * 512], in0=SA[q][:], in1=pbb.ap[:], op=ALU.mult),
                               [saB[q], pbb], [actB[tb]])
                    if g + 1 < len(groups):
                        load_w(g + 1)
                    if inext is not None and g < 4:
                        for b in range(4 * g, 4 * g + 2):
                            ada_pe(b, pm, AW, awB)
                        for b in range(4 * g + 2, 4 * g + 4):
                            ada_dma(inext, b, AW, awB)
                    def pass2(tb):
                        for f in range(8):
                            po = ps_next(range(7))
                            for j in range(nj):
                                op("pe", lambda e: e.matmul(
                                    po.ap[:], lhsT=WO[s][:, j, f * 128:(f + 1) * 128],
                                    rhs=ACTB[:, j, tb * 512:(tb + 1) * 512], start=(j == 0), stop=(j == nj - 1)),
                                   [woB[s], actB[tb]], [po])
                            resid_add(po, f, tb, i, 40)

                    if is_last and g == len(groups) - 1:
                        pass2(0)
                        pass2(1)
                        final_tb(0)
                        pass2(2)
                        final_tb(1)
                        pass2(3)
                        final_tb(2)
                        final_tb(3)
                    else:
                        for tb in range(TB):
                            pass2(tb)
                    if inext is not None and g < 4:
                        for b in range(4 * g + 2, 4 * g + 4):
                            ada_pe(b, pm, AW, awB)
                        if g == 3:
                            ada_finish(inext, pm)
                k.barrier()
                k.release(Hb + tmp + actB + wiB + woB + saB + awB)

        def conv_layer(i):
            with ExitStack() as ph:
                H = sb(ph, "cH", [128, 8, NT], BF16)
                Hb = [Buf(H[:, :, t * 512:(t + 1) * 512]) for t in range(TB)]
                ZB = sb(ph, "cZB", [128, 8, NT], BF16)
                zbB = [Buf(ZB[:, :, t * 512:(t + 1) * 512]) for t in range(TB)]
                rel = []
                with ExitStack() as ph2:
                    WCI = [sb(ph2, "cWI%d" % s, [128, 8, 3, 128], BF16) for s in range(2)]
                    wciB = [Buf(t) for t in WCI]
                    CS = sb(ph2, "cCS", [128, NT])
                    csB = Buf(CS)
                    Z = sb(ph2, "cZ", [128, NT + 2])
                    zB = Buf(Z)
                    TMP = sb(ph2, "cTMP", [128, NT])
                    tB = Buf(TMP)
                    MK = sb(ph2, "cMK", [128, 2, NT], BF16)
                    mkB = Buf(MK)
                    for m in range(2):
                        dma("pool", MK[:, m, :], cmask_d[m], writes=[mkB])
                    op("dve", lambda e: e.memset(Z[:, 0:1], 0.0), [], [zB])
                    op("dve", lambda e: e.memset(Z[:, NT + 1:NT + 2], 0.0), [], [zB])
                    cwin = conv_w_in[0].rearrange("(c p) n -> p c n", p=128)
                    for fc in range(8):
                        s = fc % 2
                        for g3 in range(3):
                            dma("pool", WCI[s][:, :, g3, :], cwin[:, :, g3 * D + fc * 128:g3 * D + (fc + 1) * 128],
                                writes=[wciB[s]])
                        if fc == 0:
                            modulate(ph, i, 0, H, Hb)

                        def proj(g3, banks):
                            for tb in range(TB):
                                pp = PS[banks[tb]]
                                for kc in range(8):
                                    op("pe", lambda e, pp=pp, kc=kc, tb=tb: e.matmul(
                                        pp.ap[:], lhsT=WCI[s][:, kc, g3, :], rhs=H[:, kc, tb * 512:(tb + 1) * 512],
                                        start=(kc == 0), stop=(kc == 7)), [wciB[s], Hb[tb]], [pp])
                        bA = [0, 1, 2, 3] if fc % 2 == 0 else [4, 5, 6, 7]
                        bB = [4, 5, 6, 7] if fc % 2 == 0 else [0, 1, 2, 3]
                        proj(1, bA)
                        for tb in range(TB):
                            op("act", lambda e, tb=tb: e.copy(out=CS[:, tb * 512:(tb + 1) * 512], in_=PS[bA[tb]].ap[:]),
                               [PS[bA[tb]]], [csB])
                        proj(2, bB)
                        for tb in range(TB):
                            op("dve", lambda e, tb=tb: e.tensor_tensor(
                                out=Z[:, 1 + tb * 512:1 + (tb + 1) * 512], in0=CS[:, tb * 512:(tb + 1) * 512],
                                in1=PS[bB[tb]].ap[:], op=ALU.mult), [csB, PS[bB[tb]]], [zB])
                        proj(0, bA)
                        w0 = c_("convw", 0 * 8 + fc)
                        w1 = c_("convw", 1 * 8 + fc)
                        w2 = c_("convw", 2 * 8 + fc)
                        op("dve", lambda e: e.tensor_scalar(out=CS[:], in0=Z[:, 1:NT + 1], scalar1=w1, scalar2=None,
                                                            op0=ALU.mult), [zB, cstB], [csB])
                        op("dve", lambda e: e.scalar_tensor_tensor(out=TMP[:], in0=Z[:, 0:NT], scalar=w0,
                                                                   in1=MK[:, 0, :], op0=ALU.mult, op1=ALU.mult),
                           [zB, cstB, mkB], [tB])
                        op("dve", lambda e: e.tensor_tensor(out=CS[:], in0=CS[:], in1=TMP[:], op=ALU.add),
                           [csB, tB], [csB])
                        op("dve", lambda e: e.scalar_tensor_tensor(out=TMP[:], in0=Z[:, 2:NT + 2], scalar=w2,
                                                                   in1=MK[:, 1, :], op0=ALU.mult, op1=ALU.mult),
                           [zB, cstB, mkB], [tB])
                        op("dve", lambda e: e.tensor_tensor(out=CS[:], in0=CS[:], in1=TMP[:], op=ALU.add),
                           [csB, tB], [csB])
                        for tb in range(TB):
                            op("dve", lambda e, tb=tb, fc=fc: e.tensor_tensor(
                                out=ZB[:, fc, tb * 512:(tb + 1) * 512], in0=CS[:, tb * 512:(tb + 1) * 512],
                                in1=PS[bA[tb]].ap[:], op=ALU.mult), [csB, PS[bA[tb]]], [zbB[tb]])
                    k.barrier()
                    k.release(wciB + [csB, zB, tB, mkB])
                with ExitStack() as ph2:
                    WCO = sb(ph2, "cWO", [128, 8, D], BF16)
                    wcoB = Buf(WCO)
                    dma("pool", WCO[:], conv_w_out[0].rearrange("(c p) n -> p c n", p=128), writes=[wcoB])
                    for tb in range(TB):
                        for f in range(8):
                            po = ps_next()
                            for kc in range(8):
                                op("pe", lambda e, po=po, kc=kc, f=f, tb=tb: e.matmul(
                                    po.ap[:], lhsT=WCO[:, kc, f * 128:(f + 1) * 128],
                                    rhs=ZB[:, kc, tb * 512:(tb + 1) * 512], start=(kc == 0), stop=(kc == 7)),
                                   [wcoB, zbB[tb]], [po])
                            resid_add(po, f, tb, i, 16)
                    k.barrier()
                    k.release([wcoB])
                k.release(Hb + zbB)

        def mla_layer(i, j):
            with ExitStack() as ph:
                QAN = sb(ph, "aQAN", [128, 3, NT], BF16)
                qanB = [Buf(QAN[:, :, t * 512:(t + 1) * 512]) for t in range(TB)]
                CKT = sb(ph, "aCKT", [128, 2, 2560], BF16)
                cktB = [Buf(CKT[:, :, t * 512:(t + 1) * 512]) for t in range(5)]
                KPT = sb(ph, "aKPT", [128, 2560], BF16)
                kptB = [Buf(KPT[:, t * 512:(t + 1) * 512]) for t in range(5)]
                for t in range(5):
                    op("dve", lambda e: e.memset(KPT[64:128, t * 512:(t + 1) * 512], 0.0), [], [kptB[t]])
                with ExitStack() as ph2:
                    H = sb(ph2, "aH", [128, 8, NT], BF16)
                    Hb = [Buf(H[:, :, t * 512:(t + 1) * 512]) for t in range(TB)]
                    WA = sb(ph2, "aWA", [128, 8, 768], BF16)
                    waB = Buf(WA)
                    wa_src = mla_w_a[j].rearrange("(c p) n -> p c n", p=128)
                    dma("pool", WA[:, :, 0:704], wa_src, writes=[waB])
                    for c in range(2):
                        dma("pool", CKT[:, c, 0:512], cckv_d[j, c * 128:(c + 1) * 128, :], writes=[cktB[0]])
                    dma("pool", KPT[0:64, 0:512], ckpe_d[j], writes=[kptB[0]])
                    tmp = modulate(ph2, i, 0, H, Hb)
                    for ax in range(2):
                        for hf in range(2):
                            d0 = 704 + ax * 32 + hf * 16
                            s0 = 640 + ax * 32 + (1 - hf) * 16
                            op("dve", lambda e: e.tensor_copy(out=WA[:, :, d0:d0 + 16], in_=WA[:, :, s0:s0 + 16]),
                               [waB], [waB])
                    SQ = [sb(ph2, "aSQ%d" % s, [128, 512], BF16) for s in range(2)]
                    sqB = [Buf(t) for t in SQ]
                    QG = sb(ph2, "aQG", [128, 3, 512])
                    qgB = Buf(QG)
                    RS = sb(ph2, "aRS", [128, 512])
                    rsB = Buf(RS)
                    CKF = sb(ph2, "aCKF", [128, 2, 512])
                    ckfB = Buf(CKF)
                    KPF = sb(ph2, "aKPF", [64, 512])
                    kpfB = Buf(KPF)
                    CSN = sb(ph2, "aCSN", [64, 2, 512])
                    csnB = Buf(CSN)
                    T1 = sb(ph2, "aT1", [64, 512])
                    t1B = Buf(T1)
                    T2 = sb(ph2, "aT2", [64, 512])
                    t2B = Buf(T2)
                    OST = [sb(ph2, "aOST%d" % s, [128, 320]) for s in range(2)]
                    ostB = [Buf(t) for t in OST]
                    for tb in range(TB):
                        tsl = slice(tb * 512, (tb + 1) * 512)
                        dma("sp", CSN[:, 0, :], rope_d[0, :, tsl], writes=[csnB])
                        dma("sp", CSN[:, 1, :], rope_d[1, :, tsl], writes=[csnB])
                        pq = [ps_next() for _ in range(3)]
                        for c in range(3):
                            for kc in range(8):
                                op("pe", lambda e, c=c, kc=kc, tsl=tsl: e.matmul(
                                    pq[c].ap[:], lhsT=WA[:, kc, c * 128:(c + 1) * 128], rhs=H[:, kc, tsl],
                                    start=(kc == 0), stop=(kc == 7)), [waB, Hb[tb]], [pq[c]])
                        rms_rstd((SQ, sqB), [pq[c].ap[:] for c in range(3)], 3, RS[:], rsB, pq, 1.0 / 384)
                        for c in range(3):
                            op("dve", lambda e, c=c: e.tensor_scalar(out=QG[:, c, :], in0=pq[c].ap[:],
                                                                     scalar1=c_("qng", j * 3 + c), scalar2=None,
                                                                     op0=ALU.mult), [pq[c], cstB], [qgB])
                        for c in range(3):
                            op("dve", lambda e, c=c, tsl=tsl: e.tensor_tensor(out=QAN[:, c, tsl], in0=QG[:, c, :],
                                                                              in1=RS[:], op=ALU.mult),
                               [qgB, rsB], [qanB[tb]])
                        pc = [ps_next() for _ in range(2)]
                        for c in range(2):
                            for kc in range(8):
                                op("pe", lambda e, c=c, kc=kc, tsl=tsl: e.matmul(
                                    pc[c].ap[:], lhsT=WA[:, kc, 384 + c * 128:384 + (c + 1) * 128], rhs=H[:, kc, tsl],
                                    start=(kc == 0), stop=(kc == 7)), [waB, Hb[tb]], [pc[c]])
                        rms_rstd((SQ, sqB), [pc[c].ap[:] for c in range(2)], 2, RS[:], rsB, pc, 1.0 / 256)
                        for c in range(2):
                            op("dve", lambda e, c=c: e.scalar_tensor_tensor(
                                out=CKF[:, c, :], in0=pc[c].ap[:], scalar=c_("kvg", j * 2 + c), in1=RS[:],
                                op0=ALU.mult, op1=ALU.mult), [pc[c], cstB, rsB], [ckfB])
                        op("act", lambda e, tb=tb: e.copy(out=CKT[:, :, 512 + tb * 512:512 + (tb + 1) * 512],
                                                          in_=CKF[:]), [ckfB], [cktB[1 + tb]])
                        pk = ps_next()
                        pks = ps_next()
                        for pp, c0 in ((pk, 640), (pks, 704)):
                            for kc in range(8):
                                op("pe", lambda e, pp=pp, c0=c0, kc=kc, tsl=tsl: e.matmul(
                                    pp.ap[0:64, :], lhsT=WA[:, kc, c0:c0 + 64], rhs=H[:, kc, tsl],
                                    start=(kc == 0), stop=(kc == 7)), [waB, Hb[tb]], [pp])
                        op("act", lambda e, pk=pk: e.copy(out=KPF[:], in_=pk.ap[0:64, :]), [pk], [kpfB])
                        op("dve", lambda e, pk=pk: e.tensor_tensor(out=T1[:], in0=pk.ap[0:64, :], in1=CSN[:, 0, :],
                                                                   op=ALU.mult), [pk, csnB], [t1B])
                        op("dve", lambda e, pks=pks: e.tensor_tensor(out=T2[:], in0=pks.ap[0:64, :], in1=CSN[:, 1, :],
                                                                     op=ALU.mult), [pks, csnB], [t2B])
                        op("dve", lambda e, tb=tb: e.tensor_tensor(
                            out=KPT[0:64, 512 + tb * 512:512 + (tb + 1) * 512], in0=T1[:], in1=T2[:], op=ALU.add),
                           [t1B, t2B], [kptB[1 + tb]])
                        for q in range(0 if 'noout' in _DBG else 4):
                            tt = tb * 4 + q
                            s = tt % 2
                            pb = ps_next()
                            for c in range(2):
                                op("pe", lambda e, pb=pb, c=c, q=q: e.transpose(
                                    out=pb.ap[:, c * 128:(c + 1) * 128], in_=CKF[:, c, q * 128:(q + 1) * 128],
                                    identity=identf), [ckfB, cstB], [pb])
                            op("pe", lambda e, pb=pb, q=q: e.transpose(
                                out=pb.ap[:, 256:320], in_=KPF[:, q * 128:(q + 1) * 128], identity=identf[0:64, 0:64]),
                               [kpfB, cstB], [pb])
                            op("act", lambda e, pb=pb, s=s: e.copy(out=OST[s][:], in_=pb.ap[:, 0:320]),
                               [pb], [ostB[s]])
                            dma("sp", ockv_d[j, tt * 128:(tt + 1) * 128, :], OST[s][:, 0:256], reads=[ostB[s]])
                            dma("sp", okpe_d[j, tt * 128:(tt + 1) * 128, :], OST[s][:, 256:320], reads=[ostB[s]])
                    k.barrier()
                    k.release(Hb + [waB, qgB, rsB, ckfB, kpfB, csnB, t1B, t2B] + sqB + ostB)
                with ExitStack() as ph2:
                    WQ = sb(ph2, "bWQ", [128, 3, 2048], BF16)
                    wqB = Buf(WQ)
                    wq_src = mla_w_q_b[j].rearrange("(c p) n -> p c n", p=128)
                    dma("pool", WQ[:, :, 0:1536], wq_src, writes=[wqB])
                    for kc in range(3):
                        dst = WQ[:, kc, 1536:2048].rearrange("p (h a f g) -> p h a f g", h=8, a=2, f=2)
                        srcv = WQ[:, kc, 0:1536].rearrange("p (h x) -> p h x", x=192)[:, :, 128:192].rearrange(
                            "p h (a f g) -> p h a f g", a=2, f=2)
                        for hf in range(2):
                            op("dve", lambda e: e.tensor_copy(out=dst[:, :, :, hf, :], in_=srcv[:, :, :, 1 - hf, :]),
                               [wqB], [wqB])
                    WKV = sb(ph2, "bWKV", [128, 2, 2048], BF16)
                    wkvB = Buf(WKV)
                    dma("pool", WKV[:], mla_w_kv_b[j].rearrange("(c p) n -> p c n", p=128), writes=[wkvB])
                    WO = sb(ph2, "bWO", [128, 2, D], BF16)
                    woB = Buf(WO)
                    KN = sb(ph2, "bKN", [128, 2, 2560], BF16)
                    knB = [[Buf(KN[:, hh, kb * 512:(kb + 1) * 512]) for kb in range(5)] for hh in range(2)]
                    V = sb(ph2, "bV", [128, 20, 2, 128], BF16)
                    vB = [Buf(V[:, kt]) for kt in range(20)]
                    OT = sb(ph2, "bOT", [128, 2, NT], BF16)
                    otB = [[Buf(OT[:, hh, t * 512:(t + 1) * 512]) for t in range(TB)] for hh in range(2)]
                    PT = [sb(ph2, "bPT%d" % s, [128, 512], BF16) for s in range(4)]
                    ptB = [Buf(t) for t in PT]
                    QN = [sb(ph2, "bQN%d" % s_, [128, 512], BF16) for s_ in range(2)]
                    qnB = [Buf(t) for t in QN]
                    QP = [sb(ph2, "bQP%d" % s_, [128, 512], BF16) for s_ in range(2)]
                    qpB = [Buf(t) for t in QP]
                    for s_ in range(2):
                        op("dve", lambda e: e.memset(QP[s_][64:128, :], 0.0), [], [qpB[s_]])
                    CSN = sb(ph2, "bCSN", [64, 2, 512])
                    csnB = Buf(CSN)
                    T1 = sb(ph2, "bT1", [64, 512])
                    t1B = Buf(T1)
                    T2 = sb(ph2, "bT2", [64, 512])
                    t2B = Buf(T2)
                    RI = sb(ph2, "bRI", [128, 512])
                    riB = Buf(RI)
                    RA = [sb(ph2, "bRA%d" % s_, [128, 512]) for s_ in range(2)]
                    raB = [Buf(t) for t in RA]
                    wo_src = mla_w_o[j].rearrange("(h p) n -> p h n", p=128)
                    wkv3 = WKV[:].rearrange("p c (h x) -> p c h x", x=256)
                    def wo_part():
                        for tb in range(0 if 'a3' in _DBG else TB):
                            for f in range(8):
                                pb = ps_next([6, 7])
                                for hh in range(2):
                                    op("pe", lambda e: e.matmul(
                                        pb.ap[:], lhsT=WO[:, hh, f * 128:(f + 1) * 128],
                                        rhs=OT[:, hh, tb * 512:(tb + 1) * 512], start=(hh == 0), stop=(hh == 1)),
                                       [woB, otB[hh][tb]], [pb])
                                resid_add(pb, f, tb, i, 16)

                    for hp in range(0 if 'noattn' in _DBG else 4):
                        for hh in range(0 if 'a1' in _DBG else 2):
                            h = 2 * hp + hh
                            for kb in range(5):
                                pb = ps_next([6, 7])
                                for c in range(2):
                                    op("pe", lambda e, pb=pb, c=c, h=h, kb=kb: e.matmul(
                                        pb.ap[:], lhsT=WKV[:, c, h * 256:h * 256 + 128],
                                        rhs=CKT[:, c, kb * 512:(kb + 1) * 512], start=(c == 0), stop=(c == 1)),
                                       [wkvB, cktB[kb]], [pb])
                                op("act", lambda e, pb=pb, hh=hh, kb=kb: e.copy(
                                    out=KN[:, hh, kb * 512:(kb + 1) * 512], in_=pb.ap[:]), [pb], [knB[hh][kb]])
                        for kt in range(0 if 'a4' in _DBG else 20):
                            pb = ps_next([6, 7])
                            for c in range(2):
                                op("pe", lambda e, pb=pb, c=c, kt=kt: e.matmul(
                                    pb.ap[:, 0:256].rearrange("p (h v) -> p h v", h=2),
                                    lhsT=CKT[:, c, kt * 128:(kt + 1) * 128],
                                    rhs=wkv3[:, c, 2 * hp:2 * hp + 2, 128:256], start=(c == 0), stop=(c == 1)),
                                   [wkvB, cktB[kt // 4]], [pb])
                            op("dve", lambda e, pb=pb, kt=kt: e.tensor_copy(
                                out=V[:, kt], in_=pb.ap[:, 0:256].rearrange("p (h v) -> p h v", h=2)), [pb], [vB[kt]])
                        if hp > 0:
                            wo_part()
                        dma("pool", WO[:], wo_src[:, 2 * hp:2 * hp + 2, :], writes=[woB])

                        def prep(hq):
                            hh_, qb_ = hq // TB, hq % TB
                            h_ = 2 * hp + hh_
                            sl_ = hq % 2
                            qsl_ = slice(qb_ * 512, (qb_ + 1) * 512)
                            dma("sp", CSN[:, 0, :], rope_d[0, :, qsl_], writes=[csnB])
                            dma("sp", CSN[:, 1, :], rope_d[1, :, qsl_], writes=[csnB])
                            pn = ps_next([6, 7])
                            for c in range(3):
                                op("pe", lambda e: e.matmul(
                                    pn.ap[:], lhsT=WQ[:, c, h_ * 192:h_ * 192 + 128], rhs=QAN[:, c, qsl_],
                                    start=(c == 0), stop=(c == 2)), [wqB, qanB[qb_]], [pn])
                            op("act", lambda e: e.copy(out=QN[sl_][:], in_=pn.ap[:]), [pn], [qnB[sl_]])
                            pr = ps_next([6, 7])
                            for c in range(3):
                                op("pe", lambda e: e.matmul(
                                    pr.ap[0:64, :], lhsT=WQ[:, c, h_ * 192 + 128:h_ * 192 + 192], rhs=QAN[:, c, qsl_],
                                    start=(c == 0), stop=(c == 2)), [wqB, qanB[qb_]], [pr])
                            op("dve", lambda e: e.tensor_tensor(out=T1[:], in0=pr.ap[0:64, :], in1=CSN[:, 0, :],
                                                                op=ALU.mult), [pr, csnB], [t1B])
                            prs = ps_next([6, 7])
                            for c in range(3):
                                op("pe", lambda e: e.matmul(
                                    prs.ap[0:64, :], lhsT=WQ[:, c, 1536 + h_ * 64:1536 + h_ * 64 + 64],
                                    rhs=QAN[:, c, qsl_], start=(c == 0), stop=(c == 2)), [wqB, qanB[qb_]], [prs])
                            op("dve", lambda e: e.tensor_tensor(out=T2[:], in0=prs.ap[0:64, :], in1=CSN[:, 1, :],
                                                                op=ALU.mult), [prs, csnB], [t2B])
                            op("dve", lambda e: e.tensor_tensor(out=QP[sl_][0:64, :], in0=T1[:], in1=T2[:], op=ALU.add),
                               [t1B, t2B], [qpB[sl_]])

                        if 'a2' not in _DBG:
                            prep(0)
                        for hq in range(0 if 'a2' in _DBG else 2 * TB):
                            hh, qb = hq // TB, hq % TB
                            h = 2 * hp + hh
                            sl = hq % 2
                            qsl = slice(qb * 512, (qb + 1) * 512)
                            if True:
                                po, prw = PS[4], PS[5]

                                def s_tile(kt):
                                    pb = PS[kt % 4]
                                    op("pe", lambda e: e.matmul(pb.ap[:], lhsT=KN[:, hh, kt * 128:(kt + 1) * 128],
                                                                rhs=QN[sl][:], start=True, stop=False),
                                       [knB[hh][kt // 4], qnB[sl]], [pb])
                                    op("pe", lambda e: e.matmul(pb.ap[:], lhsT=KPT[:, kt * 128:(kt + 1) * 128],
                                                                rhs=QP[sl][:], start=False, stop=True),
                                       [kptB[kt // 4], qpB[sl]], [pb])
                                    for half in range(2):
                                        op("act", lambda e, half=half: e.activation(
                                            out=PT[kt % 4][:, half * 256:(half + 1) * 256],
                                            in_=pb.ap[:, half * 256:(half + 1) * 256], func=AF.Exp,
                                            bias=c_("btab", kt * 8 + qb * 2 + half), scale=MLA_SCALE),
                                           [pb, cstB], [ptB[kt % 4]])

                                def pv_tile(kt):
                                    op("pe", lambda e: e.matmul(po.ap[:], lhsT=V[:, kt, hh, :], rhs=PT[kt % 4][:],
                                                                start=(kt == 0), stop=(kt == 19)),
                                       [vB[kt], ptB[kt % 4]], [po])
                                    op("pe", lambda e: e.matmul(prw.ap[:], lhsT=onesb, rhs=PT[kt % 4][:],
                                                                start=(kt == 0), stop=(kt == 19)),
                                       [cbB, ptB[kt % 4]], [prw])
                                s_tile(0)
                                s_tile(1)
                                for kt in range(20):
                                    if kt + 2 < 20:
                                        s_tile(kt + 2)
                                    pv_tile(kt)
                                    if kt == 9 and hq + 1 < 2 * TB:
                                        prep(hq + 1)
                                op("act", lambda e: e.copy(out=RA[0][:], in_=po.ap[:]), [po], [raB[0]])
                                op("act", lambda e: e.copy(out=RA[1][:], in_=prw.ap[:]), [prw], [raB[1]])
                                op("dve", lambda e: e.reciprocal(out=RI[:], in_=RA[1][:]), [raB[1]], [riB])
                                op("dve", lambda e, hh=hh, qsl=qsl: e.tensor_tensor(out=OT[:, hh, qsl], in0=RA[0][:],
                                                                                    in1=RI[:], op=ALU.mult),
                                   [raB[0], riB], [otB[hh][qb]])
                    if 'noattn' not in _DBG:
                        wo_part()
                    k.barrier()
                    k.release([wqB, wkvB, woB, csnB, t1B, t2B, riB] + qnB + qpB + raB + ptB + vB + sum(knB, []) + sum(otB, []))
                k.release(qanB + cktB + kptB)

        def ret_layer(i):
            with ExitStack() as ph:
                H = sb(ph, "rH", [128, 8, NT], BF16)
                Hb = [Buf(H[:, :, t * 512:(t + 1) * 512]) for t in range(TB)]
                WR = sb(ph, "rWR", [128, 8, 1536], BF16)
                wrB = Buf(WR)
                WRO = sb(ph, "rWRO", [128, 4, D], BF16)
                wroB = Buf(WRO)
                OF = sb(ph, "rOF", [128, 16, 512], BF16)
                ofB = [Buf(OF[:, t]) for t in range(16)]
                U = sb(ph, "rU", [128, 2, 512])
                uB = Buf(U)
                UBs = [sb(ph, "rUB%d" % s_, [128, 2, 512], BF16) for s_ in range(2)]
                ubB = [Buf(t) for t in UBs]
                ubC = [[Buf(t[:, c, :]) for c in range(2)] for t in UBs]
                ubi = [0]
                SO = [sb(ph, "rSO%d" % s, [128, 2, 512]) for s in range(2)]
                soB = [Buf(t) for t in SO]
                soC = [[Buf(t[:, c, :]) for c in range(2)] for t in SO]
                QT = sb(ph, "rQT", [128, 2, 512], BF16)
                qtB = Buf(QT)
                KT = sb(ph, "rKT", [128, 2, 512], BF16)
                ktB = Buf(KT)
                KTM = sb(ph, "rKTM", [128, 4, 256], BF16)
                ktmB = [Buf(KTM[:, q]) for q in range(4)]
                VTM = sb(ph, "rVTM", [128, 4, 512], BF16)
                vtmB = [Buf(VTM[:, q]) for q in range(4)]
                PTM = [sb(ph, "rPTM%d" % s, [128, 128], BF16) for s in range(3)]
                ptmB = [Buf(t) for t in PTM]
                OSM = sb(ph, "rOSM", [128, 512])
                osmB = Buf(OSM)
                SG = sb(ph, "rSG", [128, 512])
                sgB = Buf(SG)
                GN = sb(ph, "rGN", [128, 512])
                gnB = Buf(GN)
                YB = [sb(ph, "rYB%d" % s_, [128, 512], BF16) for s_ in range(3)]
                ybB = [Buf(t) for t in YB]
                YT = sb(ph, "rYT", [128, 4, 512], BF16)
                ytB = Buf(YT)
                ST = sb(ph, "rST", [128, 16])
                stB = Buf(ST)
                win = ret_w_in[0].rearrange("(c p) n -> p c n", p=128)
                wout = ret_w_out[0].rearrange("(e p) n -> p e n", p=128)
                nso = 0
                for h in range(4):
                    dma("pool", WR[:, :, 0:256], win[:, :, h * 256:(h + 1) * 256], writes=[wrB])
                    dma("pool", WR[:, :, 256:512], win[:, :, 1024 + h * 256:1024 + (h + 1) * 256], writes=[wrB])
                    dma("pool", WR[:, :, 512:1024], win[:, :, 2048 + h * 512:2048 + (h + 1) * 512], writes=[wrB])
                    dma("pool", WR[:, :, 1024:1536], win[:, :, 4096 + h * 512:4096 + (h + 1) * 512], writes=[wrB])
                    dma("pool", WRO[:], wout[:, 4 * h:4 * h + 4, :], writes=[wroB])
                    dma("sp", GN[:], gng_d[:, h * 512:(h + 1) * 512], writes=[gnB])
                    if h == 0:
                        modulate(ph, i, 0, H, Hb)
                    for d in range(2):
                        dh = d * 4 + h
                        Acol = DEC[:, 0, dh:dh + 1]
                        Bcol = DEC[:, 1, dh:dh + 1]
                        Kcol = DEC[:, 2, dh:dh + 1]
                        Ccol = DEC[:, 3, dh:dh + 1]
                        mask = maskf if d == 0 else maskb
                        dma("sp", U[:], st0_d[d, h].rearrange("(c p) e -> p c e", p=128), writes=[uB])
                        cm0 = c_("cmf", 0) if d == 0 else c_("cmb", 15)
                        op("act", lambda e: e.activation(out=U[:], in_=U[:], func=AF.Identity, scale=cm0),
                           [uB, cstB], [uB])
                        op("act", lambda e: e.copy(out=UBs[ubi[0]][:], in_=U[:]), [uB],
                           [ubB[ubi[0]], ubC[ubi[0]][0], ubC[ubi[0]][1]])
                        scs = range(4) if d == 0 else range(3, -1, -1)
                        pending = []

                        def drain(keep):
                            while sum(1 for k_, _ in pending if k_ == "tr") > keep:
                                pending.pop(0)[1]()
                        for sc in scs:
                            tsl = slice(sc * 512, (sc + 1) * 512)
                            for c in range(2):
                                pb = ps_next()
                                for kc in range(8):
                                    op("pe", lambda e: e.matmul(
                                        pb.ap[:], lhsT=WR[:, kc, 256 + c * 128:256 + (c + 1) * 128], rhs=H[:, kc, tsl],
                                        start=(kc == 0), stop=(kc == 7)), [wrB, Hb[sc]], [pb])
                                op("dve", lambda e: e.tensor_copy(out=KT[:, c, :], in_=pb.ap[:]), [pb], [ktB])
                            for c in range(2):
                                pb = ps_next()
                                for kc in range(8):
                                    op("pe", lambda e: e.matmul(
                                        pb.ap[:], lhsT=WR[:, kc, c * 128:(c + 1) * 128], rhs=H[:, kc, tsl],
                                        start=(kc == 0), stop=(kc == 7)), [wrB, Hb[sc]], [pb])
                                op("act", lambda e: e.copy(out=QT[:, c, :], in_=pb.ap[:]), [pb], [qtB])
                            qs = list(range(4)) if d == 0 else [3, 2, 1, 0]

                            def proj_tm(q):
                                tok = slice(sc * 512 + q * 128, sc * 512 + (q + 1) * 128)
                                pb = ps_next()
                                pbv = pb.ap.bitcast(BF16)
                                for c in range(2):
                                    op("pe", lambda e: e.transpose(
                                        out=pbv[:, c * 128:(c + 1) * 128], in_=KT[:, c, q * 128:(q + 1) * 128],
                                        identity=identb), [ktB, cbB], [pb])
                                op("dve", lambda e: e.tensor_scalar(
                                    out=KTM[:, q, :], in0=pbv[:, 0:256], scalar1=Kcol, scalar2=None, op0=ALU.mult),
                                   [pb, decB], [ktmB[q]])
                                pb2 = ps_next()
                                for kc in range(8):
                                    op("pe", lambda e: e.matmul(
                                        pb2.ap[:], lhsT=H[:, kc, tok], rhs=WR[:, kc, 512:1024],
                                        start=(kc == 0), stop=(kc == 7)), [wrB, Hb[sc]], [pb2])
                                op("act", lambda e: e.copy(out=VTM[:, q, :], in_=pb2.ap[:]), [pb2], [vtmB[q]])

                            def st_tile(q):
                                loc = slice(q * 128, (q + 1) * 128)
                                pm_ = (sc * 4 + q) % 3
                                pb = ps_next()
                                for c in range(2):
                                    op("pe", lambda e: e.matmul(
                                        pb.ap[:, 0:128], lhsT=KT[:, c, loc], rhs=QT[:, c, loc],
                                        start=(c == 0), stop=(c == 1)), [ktB, qtB], [pb])
                                op("dve", lambda e: e.scalar_tensor_tensor(
                                    out=PTM[pm_][:], in0=pb.ap[:, 0:128], scalar=Acol, in1=mask,
                                    op0=ALU.mult, op1=ALU.mult), [pb, decB, cbB], [ptmB[pm_]])

                            proj_tm(qs[0])
                            st_tile(qs[0])
                            proj_tm(qs[1])
                            st_tile(qs[1])
                            for idx, q in enumerate(qs):
                                tt = sc * 4 + q
                                loc = slice(q * 128, (q + 1) * 128)
                                tok = slice(tt * 128, (tt + 1) * 128)
                                pm = tt % 3
                                if idx + 2 < 4:
                                    proj_tm(qs[idx + 2])
                                    st_tile(qs[idx + 2])
                                psts = []
                                for c in range(2):
                                    pst = ps_next()
                                    psts.append(pst)
                                    op("pe", lambda e: e.matmul(
                                        pst.ap[:], lhsT=KTM[:, q, c * 128:(c + 1) * 128], rhs=VTM[:, q, :],
                                        start=True, stop=True), [ktmB[q], vtmB[q]], [pst])
                                ucur = ubi[0]
                                ubi[0] = 1 - ubi[0]
                                so = nso % 2
                                nso += 1
                                for c in range(2):
                                    op("dve", lambda e: e.scalar_tensor_tensor(
                                        out=SO[so][:, c, :], in0=U[:, c, :], scalar=Ccol, in1=psts[c].ap[:],
                                        op0=ALU.mult, op1=ALU.add), [uB, decB, psts[c]], [soB[so], soC[so][c]])
                                if (d == 0 and tt % 2 == 1) or (d == 1 and tt % 2 == 0):
                                    dma("sp", ost_d[tt // 2, d, h].rearrange("(c p) e -> p c e", p=128), SO[so][:],
                                        reads=[soB[so]])
                                nxt = tt + 1 if d == 0 else tt - 1
                                if 0 <= nxt < 16:
                                    cm = c_("cmf" if d == 0 else "cmb", nxt)
                                    for c in range(2):
                                        op("act", lambda e: e.activation(
                                            out=UBs[1 - ucur][:, c, :], in_=SO[so][:, c, :], func=AF.Identity,
                                            scale=cm), [soC[so][c], cstB], [ubC[1 - ucur][c]])
                                    op("dve", lambda e: e.tensor_scalar(
                                        out=U[:], in0=SO[so][:], scalar1=cm, scalar2=None, op0=ALU.mult),
                                       [soB[so], cstB], [uB])
                                if d == 1:
                                    pg = ps_next()
                                    for kc in range(8):
                                        op("pe", lambda e: e.matmul(
                                            pg.ap[:], lhsT=H[:, kc, tok], rhs=WR[:, kc, 1024:1536],
                                            start=(kc == 0), stop=(kc == 7)), [wrB, Hb[sc]], [pg])
                                    op("act", lambda e: e.activation(out=SG[:], in_=pg.ap[:], func=AF.Silu),
                                       [pg], [sgB])
                                po = ps_next()
                                op("pe", lambda e: e.matmul(
                                    po.ap[:], lhsT=PTM[pm][:], rhs=VTM[:, q, :], start=True, stop=False),
                                   [ptmB[pm], vtmB[q]], [po])
                                for c in range(2):
                                    op("pe", lambda e: e.matmul(
                                        po.ap[:], lhsT=QT[:, c, loc], rhs=UBs[ucur][:, c, :], start=False,
                                        stop=(c == 1)), [qtB, ubC[ucur][c]], [po])
                                if d == 0:
                                    op("act", lambda e: e.activation(
                                        out=OF[:, tt, :], in_=po.ap[:], func=AF.Identity, scale=Bcol),
                                       [po, decB], [ofB[tt]])
                                else:
                                    yb = tt % 3
                                    op("dve", lambda e: e.scalar_tensor_tensor(
                                        out=OSM[:], in0=po.ap[:], scalar=Bcol, in1=OF[:, tt, :],
                                        op0=ALU.mult, op1=ALU.add), [po, decB, ofB[tt]], [osmB])
                                    op("dve", lambda e: e.bn_stats(out=ST[:, 0:6], in_=OSM[:]), [osmB], [stB])
                                    op("dve", lambda e: e.bn_aggr(out=ST[:, 8:10], in_=ST[:, 0:6]), [stB], [stB])
                                    op("dve", lambda e: e.tensor_scalar(
                                        out=ST[:, 10:11], in0=ST[:, 9:10], scalar1=EPS, scalar2=None,
                                        op0=ALU.add), [stB], [stB])
                                    op("act", lambda e: e.activation(out=ST[:, 10:11], in_=ST[:, 10:11], func=AF.Sqrt),
                                       [stB], [stB])
                                    op("dve", lambda e: e.reciprocal(out=ST[:, 10:11], in_=ST[:, 10:11]), [stB], [stB])
                                    op("dve", lambda e: e.tensor_scalar(
                                        out=OSM[:], in0=OSM[:], scalar1=ST[:, 8:9], scalar2=ST[:, 10:11],
                                        op0=ALU.subtract, op1=ALU.mult), [osmB, stB], [osmB])
                                    op("dve", lambda e: e.tensor_tensor(out=SG[:], in0=SG[:], in1=GN[:], op=ALU.mult),
                                       [sgB, gnB], [sgB])
                                    op("dve", lambda e: e.tensor_tensor(out=YB[yb][:], in0=OSM[:], in1=SG[:],
                                                                        op=ALU.mult), [osmB, sgB], [ybB[yb]])

                                    def tr(yb=yb, loc=loc):
                                        pt = ps_next()
                                        ptv = pt.ap.bitcast(BF16)
                                        for ec in range(4):
                                            op("pe", lambda e: e.transpose(
                                                out=ptv[:, ec * 128:(ec + 1) * 128],
                                                in_=YB[yb][:, ec * 128:(ec + 1) * 128], identity=identb),
                                               [ybB[yb], cbB], [pt])
                                        op("act", lambda e: e.copy(
                                            out=YT[:, :, loc], in_=ptv[:, 0:512].rearrange("p (e t) -> p e t", e=4)),
                                           [pt], [ytB])
                                    pending.append(("tr", tr))
                                    drain(2)
                            if d == 1:
                                def wout_fn(sc=sc):
                                    for f in range(8):
                                        pb = ps_next()
                                        for ec in range(4):
                                            op("pe", lambda e: e.matmul(
                                                pb.ap[:], lhsT=WRO[:, ec, f * 128:(f + 1) * 128], rhs=YT[:, ec, :],
                                                start=(ec == 0), stop=(ec == 3)), [wroB, ytB], [pb])
                                        resid_add(pb, f, sc, i, 16)
                                pending.append(("w", wout_fn))
                        while pending:
                            pending.pop(0)[1]()
                k.barrier()
                k.release(Hb + [wrB, wroB, uB, qtB, ktB, osmB, sgB, gnB, ytB, stB] + ubB + ybB + ofB + soB + ktmB
                          + vtmB + ptmB)

        for li, i in enumerate(layers):
            kind, j = i % 3, i // 3
            if kind == 0:
                mla_layer(i, j)
            elif kind == 1:
                conv_layer(i)
            else:
                ret_layer(i)
            ffn(i, layers[li + 1] if li + 1 < len(layers) else None, is_last=(li + 1 == len(layers)))

        with ExitStack() as ph:
          if not layers:
            SQ = [sb(ph, "oSQ%d" % s, [128, 512], BF16) for s in range(2)]
            sqB = [Buf(t) for t in SQ]
            RS = sb(ph, "oRS", [128, 512])
            rsB = Buf(RS)
            YF = sb(ph, "oYF", [128, 8, 512])
            yfB = Buf(YF)
            YS = [sb(ph, "oYS%d" % s, [128, D]) for s in range(2)]
            ysB = [Buf(t) for t in YS]
            for tb in range(TB):
                rms_rstd((SQ, sqB), [X[:, fc, tb * 512:(tb + 1) * 512] for fc in range(8)], 8, RS[:], rsB,
                         [Xb[tb]], 1.0 / D)
                for fc in range(8):
                    op("dve", lambda e, fc=fc, tb=tb: e.scalar_tensor_tensor(
                        out=YF[:, fc, :], in0=X[:, fc, tb * 512:(tb + 1) * 512], scalar=c_("gfin", fc), in1=RS[:],
                        op0=ALU.mult, op1=ALU.mult), [Xb[tb], cstB, rsB], [yfB])
                for q in range(4):
                    tt = tb * 4 + q
                    s = tt % 2
                    for half in range(2):
                        pb = ps_next()
                        for r in range(4):
                            fc = half * 4 + r
                            op("pe", lambda e, pb=pb, r=r, fc=fc, q=q: e.transpose(
                                out=pb.ap[:, r * 128:(r + 1) * 128], in_=YF[:, fc, q * 128:(q + 1) * 128],
                                identity=identf), [yfB, cstB], [pb])
                        if half == 0:
                            op("act", lambda e, pb=pb, s=s: e.copy(out=YS[s][:, 0:512], in_=pb.ap[:]), [pb], [ysB[s]])
                        else:
                            op("dve", lambda e, pb=pb, s=s: e.tensor_copy(out=YS[s][:, 512:1024], in_=pb.ap[:]),
                               [pb], [ysB[s]])
                    dma("sp", y_d[tt * 128:(tt + 1) * 128, :], YS[s][:], reads=[ysB[s]])
            k.barrier()
    return nc


def _const_table(cvec, inp, is_sample):
    t = np.zeros((128, NCST), np.float32)

    def put(name, arr):
        arr = np.asarray(arr, np.float32)
        t[:, _CO[name]:_CO[name] + arr.shape[1]] = arr

    def pp(v):
        v = np.asarray(v, np.float32)
        return v.reshape(-1, 128).T

    put("cv", pp(cvec))
    put("adab", np.concatenate([pp(inp["ada_b"][i]) for i in range(4)], axis=1))
    put("gmix", np.concatenate([pp(inp["norm_mix_g"][i]) for i in range(4)], axis=1))
    put("gffn", np.concatenate([pp(inp["norm_ffn_g"][i]) for i in range(4)], axis=1))
    put("gfin", pp(inp["final_norm_g"]))
    put("qng", np.concatenate([pp(inp["mla_q_norm_g"][j]) for j in range(2)], axis=1))
    put("kvg", np.concatenate([pp(inp["mla_kv_norm_g"][j]) for j in range(2)], axis=1))
    put("convw", np.concatenate([pp(inp["conv_w"][0][kk]) for kk in range(3)], axis=1))
    put("lr", np.broadcast_to(np.asarray(inp["ret_log_rate"][0], np.float32).reshape(1, 8), (128, 8)))
    p = np.arange(128, dtype=np.float32)
    coef = np.stack([p + 1, 128 - p, -(p + 1), -(128 - p), -(127 - p), -p, np.full(128, -128.0, np.float32),
                     np.zeros(128, np.float32)], axis=1)
    put("coef", coef)
    if is_sample:
        cmf = np.ones(16, np.float32)
        cmb = np.ones(16, np.float32)
    else:
        cmf = np.array([0.0 if n % 2 == 0 else 1.0 for n in range(16)], np.float32)
        cmb = np.array([0.0 if n % 2 == 1 else 1.0 for n in range(16)], np.float32)
    put("cmf", np.broadcast_to(cmf.reshape(1, 16), (128, 16)))
    put("cmb", np.broadcast_to(cmb.reshape(1, 16), (128, 16)))
    bt = np.zeros((20, 8), np.float32)
    if not is_sample:
        bt[:] = NEG
        for kt in range(4, 20):
            sk = (kt - 4) // 2
            bt[kt, sk] = 0.0
    put("btab", np.broadcast_to(bt.reshape(1, 160), (128, 160)))
    put("identf", np.eye(128, dtype=np.float32))
    put("ones", np.ones((128, 128), np.float32))
    jj = np.arange(128)[:, None]
    ii = np.arange(128)[None, :]
    put("maskf", (jj <= ii).astype(np.float32))
    put("maskb", (jj >= ii).astype(np.float32))
    return t


def _rope_tables(is_sample):
    r = np.zeros((2, 64, NT), np.float32)
    if not is_sample:
        r[0] = 1.0
        return r
    tok = np.arange(NT)
    row = (tok // 64).astype(np.float32)
    col = (tok % 64).astype(np.float32)
    inv = (np.float32(10000.0) ** (-np.arange(16, dtype=np.float32) / np.float32(16))).astype(np.float32)
    for ax, pos in enumerate((row, col)):
        ang = (pos[None, :] * inv[:, None]).astype(np.float32)
        c, s = np.cos(ang), np.sin(ang)
        for hf in range(2):
            p0 = ax * 32 + hf * 16
            r[0, p0:p0 + 16] = c
            r[1, p0:p0 + 16] = -s if hf == 0 else s
    return r


def _conv_masks(is_sample):
    seq = NT if is_sample else 256
    t = np.arange(NT)
    mp = (t % seq != 0).astype(np.float32)
    mn = (t % seq != seq - 1).astype(np.float32)
    return np.ascontiguousarray(np.broadcast_to(np.stack([mp, mn])[:, None, :], (2, 128, NT))).astype(np.float32)


_NC_CACHE = {}
_LAYERS = (0, 1, 2, 3)


def kernel(**inp):
    inp = {k_: np.asarray(v) for k_, v in inp.items()}
    if _LAYERS not in _NC_CACHE:
        _NC_CACHE[_LAYERS] = build(_LAYERS)
    nc = _NC_CACHE[_LAYERS]
    wnames = ["ada_w", "mla_w_a", "mla_w_q_b", "mla_w_kv_b", "mla_w_o", "conv_w_in", "conv_w_out", "ret_w_in",
              "ret_w_out", "ffn_w_in", "ffn_w_out"]
    shared = {n: np.ascontiguousarray(inp[n], dtype=np.float32) for n in wnames}
    gng = np.ascontiguousarray(np.broadcast_to(inp["ret_gn_g"][0].reshape(1, 2048), (128, 2048))).astype(np.float32)
    in_maps = []
    for core in range(8):
        is_s = core < 4
        m = dict(shared)
        if is_s:
            b = core
            m["x"] = np.ascontiguousarray(inp["x_sample"][b])
            cvec = inp["c"][b]
            m["st0"] = np.ascontiguousarray(inp["state_ret"][b, 0])
            m["cckv"] = np.ascontiguousarray(inp["cache_mla_ckv"][b].transpose(0, 2, 1))
            m["ckpe"] = np.ascontiguousarray(inp["cache_mla_kpe"][b].transpose(0, 2, 1))
        else:
            p = core - 4
            m["x"] = np.ascontiguousarray(inp["x_prompt"][8 * p:8 * p + 8].reshape(NT, D))
            cvec = inp["c_ctx"]
            m["st0"] = np.ascontiguousarray(inp["state_ret"][p, 0])
            m["cckv"] = np.ascontiguousarray(inp["cache_mla_ckv"][p].transpose(0, 2, 1))
            m["ckpe"] = np.ascontiguousarray(inp["cache_mla_kpe"][p].transpose(0, 2, 1))
        m["cst"] = _const_table(cvec, inp, is_s)
        m["rope"] = _rope_tables(is_s)
        m["cmask"] = _conv_masks(is_s)
        m["gng"] = gng
        in_maps.append(m)
    res = run_bass_kernel_spmd(nc, in_maps, core_ids=list(range(8)))
    R = res.results
    y_sample = np.stack([R[b]["y"] for b in range(4)], axis=0).astype(np.float32)
    y_prompt = np.concatenate([R[4 + p]["y"].reshape(8, 256, D) for p in range(4)], axis=0).astype(np.float32)
    ckv = np.concatenate([R[4 + p]["ockv"].reshape(2, 8, 256, 256).transpose(1, 0, 2, 3) for p in range(4)], axis=0)
    kpe = np.concatenate([R[4 + p]["okpe"].reshape(2, 8, 256, 64).transpose(1, 0, 2, 3) for p in range(4)], axis=0)
    st = np.concatenate([R[4 + p]["ost"].reshape(8, 1, 2, 4, 256, 512) for p in range(4)], axis=0)
    return (y_prompt, y_sample, np.ascontiguousarray(ckv, dtype=np.float32),
            np.ascontiguousarray(kpe, dtype=np.float32), np.ascontiguousarray(st, dtype=np.float32))
```

```python
import numpy as np
from contextlib import ExitStack
import concourse.bass as bass
import concourse.mybir as mybir
from concourse.bass_utils import run_bass_kernel_spmd

F32 = mybir.dt.float32
BF16 = mybir.dt.bfloat16
AF = mybir.ActivationFunctionType
ALU = mybir.AluOpType

D = 1024
NT = 2048
TB = 4
FFN_H = 2816
EPS = 1e-6
MLA_SCALE = 192 ** -0.5
NEG = -30000.0

_CO = {}
_n = 0
for _name, _w in [("cv", 8), ("adab", 192), ("gmix", 32), ("gffn", 32), ("gfin", 8), ("qng", 6),
                  ("kvg", 4), ("convw", 24), ("lr", 8), ("coef", 8), ("cmf", 16), ("cmb", 16),
                  ("btab", 160), ("identf", 128), ("ones", 128), ("maskf", 128), ("maskb", 128)]:
    _CO[_name] = _n
    _n += _w
NCST = _n


class DS:
    def __init__(self, sem, key):
        self.sem = sem
        self.cnt = 0
        self.key = key


class Buf:
    __slots__ = ("ap", "w", "r", "ds")

    def __init__(self, ap):
        self.ap = ap
        self.w = None
        self.r = {}
        self.ds = None


class Eng:
    def __init__(self, eng, sem):
        self.eng = eng
        self.sem = sem
        self.cnt = 0
        self.seen = {}


class K:
    def __init__(self, nc, es):
        self.nc = nc
        self.E = {}
        for name, eng in [("pe", nc.tensor), ("act", nc.scalar), ("dve", nc.vector),
                          ("pool", nc.gpsimd), ("sp", nc.sync)]:
            self.E[name] = Eng(eng, es.enter_context(nc.semaphore("s_" + name)))
        self.free_ds = {"sp": [], "pool": []}
        self.all_ds = {}
        for i in range(64):
            d = DS(es.enter_context(nc.semaphore("d%d" % i)), ("d", i))
            d.q = "sp" if i < 24 else "pool"
            self.free_ds[d.q].append(d)
            self.all_ds[d.key] = d
        self.dirty = set()

    def _sem_of(self, key):
        if isinstance(key, tuple):
            d = self.all_ds[key]
            return d.sem, 16
        return self.E[key].sem, 1

    def _waits(self, en, reads, writes):
        e = self.E[en]
        need = {}
        for b in reads:
            if b.w is not None and b.w[1] > need.get(b.w[0], 0):
                need[b.w[0]] = b.w[1]
        for b in writes:
            if b.w is not None and b.w[1] > need.get(b.w[0], 0):
                need[b.w[0]] = b.w[1]
            for k, v in b.r.items():
                if k == en and en == "pe":
                    continue
                if v > need.get(k, 0):
                    need[k] = v
        for key, val in need.items():
            if key == en and en == "pe":
                continue
            if e.seen.get(key, 0) >= val:
                continue
            sem, mult = self._sem_of(key)
            e.eng.wait_ge(sem, val * mult)
            e.seen[key] = val

    def op(self, en, fn, reads=(), writes=()):
        e = self.E[en]
        self._waits(en, reads, writes)
        inst = fn(e.eng)
        e.cnt += 1
        inst.then_inc(e.sem, 1)
        for b in writes:
            b.w = (en, e.cnt)
            b.r = {}
        for b in reads:
            b.r[en] = e.cnt

    def dma(self, q, out, in_, reads=(), writes=(), **kw):
        owner = writes[0] if writes else reads[0]
        if owner.ds is None:
            owner.ds = self.free_ds[q].pop()
        ds = owner.ds
        assert ds.q == q
        e = self.E[q]
        self._waits(q, reads, writes)
        if q == "pool":
            kw.setdefault("max_dma_last_dim", 8192)
        inst = e.eng.dma_start(out=out, in_=in_, **kw)
        ds.cnt += 1
        inst.then_inc(ds.sem, 16)
        self.dirty.add(ds.key)
        for b in writes:
            b.w = (ds.key, ds.cnt)
            b.r = {}
        for b in reads:
            b.r[ds.key] = ds.cnt

    def barrier(self):
        keys = [(k, self.E[k].cnt) for k in ("pe", "act", "dve", "pool") if self.E[k].cnt > 0]
        keys += [(k, self.all_ds[k].cnt) for k in sorted(self.dirty)]
        self.dirty = set()
        for en, e in self.E.items():
            for key, val in keys:
                if key == en:
                    continue
                if e.seen.get(key, 0) >= val:
                    continue
                sem, mult = self._sem_of(key)
                e.eng.wait_ge(sem, val * mult)
                e.seen[key] = val

    def release(self, bufs):
        for b in bufs:
            if b.ds is not None:
                self.free_ds[b.ds.q].append(b.ds)
                b.ds = None


_DBG = set()
RSUM_ENG = ("dve", "pool")


def build(layers=(0, 1, 2, 3)):
    nc = bass.Bass("TRN2", target_bir_lowering=False)

    def din(name, shape):
        return nc.dram_tensor(name, list(shape), F32, kind="ExternalInput").ap()

    def dout(name, shape):
        return nc.dram_tensor(name, list(shape), F32, kind="ExternalOutput").ap()

    x_d = din("x", [NT, D])
    cst_d = din("cst", [128, NCST])
    rope_d = din("rope", [2, 64, NT])
    cmask_d = din("cmask", [2, 128, NT])
    gng_d = din("gng", [128, 2048])
    mk_d = din("mk", [2, 16, 2560])
    st0_d = din("st0", [2, 4, 256, 512])
    cckv_d = din("cckv", [2, 256, 512])
    ckpe_d = din("ckpe", [2, 64, 512])
    ada_w = din("ada_w", [4, D, 6 * D])
    mla_w_a = din("mla_w_a", [2, D, 704])
    mla_w_q_b = din("mla_w_q_b", [2, 384, 1536])
    mla_w_kv_b = din("mla_w_kv_b", [2, 256, 2048])
    mla_w_o = din("mla_w_o", [2, 1024, D])
    conv_w_in = din("conv_w_in", [1, D, 3 * D])
    conv_w_out = din("conv_w_out", [1, D, D])
    ret_w_in = din("ret_w_in", [1, D, 6144])
    ret_w_out = din("ret_w_out", [1, 2048, D])
    ffn_w_in = din("ffn_w_in", [4, D, 2 * FFN_H])
    ffn_w_out = din("ffn_w_out", [4, FFN_H, D])

    y_d = dout("y", [NT, D])
    ockv_d = dout("ockv", [2, NT, 256])
    okpe_d = dout("okpe", [2, NT, 64])
    ost_d = dout("ost", [8, 2, 4, 256, 512])

    es = ExitStack()
    with es:
        k = K(nc, es)
        op, dma = k.op, k.dma

        uid = [0]

        def sb(scope, name, shape, dt=F32):
            uid[0] += 1
            return scope.enter_context(nc.sbuf_tensor("%s_%d" % (name, uid[0]), list(shape), dt))

        X = sb(es, "X", [128, 8, NT])
        Xb = [Buf(X[:, :, t * 512:(t + 1) * 512]) for t in range(TB)]
        CST = sb(es, "CST", [128, NCST])
        cstB = Buf(CST)
        CB = sb(es, "CB", [128, 4, 128], BF16)
        cbB = Buf(CB)
        MODS = sb(es, "MODS", [128, 4, 48])
        G12 = sb(es, "G12", [128, 4, 16])
        modB = Buf(MODS)
        SCV = sb(es, "SCV", [128, 8], BF16)
        scvB = Buf(SCV)
        DEC = sb(es, "DEC", [128, 4, 8])
        decB = Buf(DEC)
        PSt = [es.enter_context(nc.psum_tensor("ps%d" % i, [128, 512], F32)) for i in range(8)]
        PS = [Buf(t) for t in PSt]
        rr = [0]

        def ps_next(banks=range(8)):
            banks = list(banks)
            b = banks[rr[0] % len(banks)]
            rr[0] += 1
            return PS[b]

        def c_(name, a=0, b=None):
            o = _CO[name]
            if b is None:
                b = a + 1
            return CST[:, o + a:o + b]

        identf = c_("identf", 0, 128)
        identb = CB[:, 0, :]
        onesb = CB[:, 1, :]
        maskf = CB[:, 2, :]
        maskb = CB[:, 3, :]

        def ada_dma(i, b, AW, awB):
            src = ada_w[i].rearrange("(c p) n -> p c n", p=128)
            dma("pool", AW[b % 2][:], src[:, :, b * 384:(b + 1) * 384], writes=[awB[b % 2]])

        def ada_pe(b, pm, AW, awB):
            s_ = b % 2
            for n in range(3):
                col = b * 3 + n
                for kc in range(8):
                    op("pe", lambda e: e.matmul(pm.ap[:, col:col + 1], lhsT=AW[s_][:, kc, n * 128:(n + 1) * 128],
                                                rhs=SCV[:, kc:kc + 1], start=(kc == 0), stop=(kc == 7)),
                       [awB[s_], scvB], [pm])

        def ada_finish(i, pm):
            op("dve", lambda e: e.tensor_tensor(out=MODS[:, i, :], in0=pm.ap[:, 0:48],
                                                in1=c_("adab", i * 48, i * 48 + 48), op=ALU.add), [pm, cstB], [modB])
            op("dve", lambda e: e.scalar_tensor_tensor(out=G12[:, i, 0:8], in0=MODS[:, i, 8:16], scalar=1.0,
                                                       in1=c_("gmix", i * 8, i * 8 + 8), op0=ALU.add, op1=ALU.mult),
               [modB, cstB], [modB])
            op("dve", lambda e: e.scalar_tensor_tensor(out=G12[:, i, 8:16], in0=MODS[:, i, 32:40], scalar=1.0,
                                                       in1=c_("gffn", i * 8, i * 8 + 8), op0=ALU.add, op1=ALU.mult),
               [modB, cstB], [modB])

        dma("sp", CST[:], cst_d[:, :], writes=[cstB])
        op("dve", lambda e: e.tensor_copy(out=CB[:, 0, :], in_=c_("identf", 0, 128)), [cstB], [cbB])
        op("dve", lambda e: e.tensor_copy(out=CB[:, 1, :], in_=c_("ones", 0, 128)), [cstB], [cbB])
        op("dve", lambda e: e.tensor_copy(out=CB[:, 2, :], in_=c_("maskf", 0, 128)), [cstB], [cbB])
        op("dve", lambda e: e.tensor_copy(out=CB[:, 3, :], in_=c_("maskb", 0, 128)), [cstB], [cbB])
        op("act", lambda e: e.activation(out=SCV[:], in_=c_("cv", 0, 8), func=AF.Silu), [cstB], [scvB])
        with ExitStack() as ph:
            ELR = sb(ph, "ELR", [128, 8])
            elrB = Buf(ELR)
            op("act", lambda e: e.activation(out=ELR[:], in_=c_("lr", 0, 8), func=AF.Exp), [cstB], [elrB])
            for t, (cf, cb_) in enumerate([(0, 1), (2, 3), (4, 5), (6, 6)]):
                op("act", lambda e, t=t, cf=cf: e.activation(out=DEC[:, t, 0:4], in_=ELR[:, 0:4], func=AF.Exp,
                                                            scale=c_("coef", cf)), [elrB, cstB], [decB])
                op("act", lambda e, t=t, cb_=cb_: e.activation(out=DEC[:, t, 4:8], in_=ELR[:, 4:8], func=AF.Exp,
                                                              scale=c_("coef", cb_)), [elrB, cstB], [decB])
            op("dve", lambda e: e.tensor_scalar(out=DEC[:, 0, :], in0=DEC[:, 0, :], scalar1=0.0625, scalar2=None,
                                                op0=ALU.mult), [decB], [decB])
            op("dve", lambda e: e.tensor_scalar(out=DEC[:, 2, :], in0=DEC[:, 2, :], scalar1=0.0625, scalar2=None,
                                                op0=ALU.mult), [decB], [decB])
            XS = [sb(ph, "XS%d" % i, [128, D]) for i in range(3)]
            xsB = [Buf(t) for t in XS]
            for tt in range(16):
                s = tt % 3
                dma("sp", XS[s][:], x_d[tt * 128:(tt + 1) * 128, :], writes=[xsB[s]])
                for half in range(2):
                    pb = ps_next()
                    for q in range(4):
                        fc = half * 4 + q
                        op("pe", lambda e, pb=pb, q=q, fc=fc, s=s: e.transpose(
                            out=pb.ap[:, q * 128:(q + 1) * 128], in_=XS[s][:, fc * 128:(fc + 1) * 128],
                            identity=identf), [xsB[s], cstB], [pb])
                    eng = "act" if half == 0 else "dve"
                    if eng == "act":
                        op("act", lambda e, pb=pb, half=half, tt=tt: e.copy(
                            out=X[:, half * 4:half * 4 + 4, tt * 128:(tt + 1) * 128],
                            in_=pb.ap.rearrange("p (q t) -> p q t", q=4)), [pb], [Xb[tt // 4]])
                    else:
                        op("dve", lambda e, pb=pb, half=half, tt=tt: e.tensor_copy(
                            out=X[:, half * 4:half * 4 + 4, tt * 128:(tt + 1) * 128],
                            in_=pb.ap.rearrange("p (q t) -> p q t", q=4)), [pb], [Xb[tt // 4]])
            AW = [sb(ph, "AW%d" % i, [128, 8, 1536], BF16) for i in range(4)]
            awB = [Buf(t) for t in AW]
            if layers:
                pm = PS[7]
                src0 = ada_w[layers[0]].rearrange("(c p) n -> p c n", p=128)
                for bl in range(4):
                    for hf in range(2):
                        dma("pool", AW[bl][:, hf * 4:hf * 4 + 4, :],
                            src0[:, hf * 4:hf * 4 + 4, bl * 1536:(bl + 1) * 1536], writes=[awB[bl]])
                for bl in range(4):
                    for n in range(12):
                        col = bl * 12 + n
                        for kc in range(8):
                            op("pe", lambda e: e.matmul(pm.ap[:, col:col + 1], lhsT=AW[bl][:, kc, n * 128:(n + 1) * 128],
                                                        rhs=SCV[:, kc:kc + 1], start=(kc == 0), stop=(kc == 7)),
                               [awB[bl], scvB], [pm])
                ada_finish(layers[0], pm)
            k.barrier()
            k.release(xsB + awB + [elrB])

        def rms_rstd(scope_bufs, srcs, nchunk, out_rstd, out_buf, src_reads, inv_n):
            SQ, sqB = scope_bufs
            pss = ps_next()
            for c in range(nchunk):
                s = c % 2
                op("act", lambda e, c=c, s=s: e.activation(out=SQ[s][:], in_=srcs[c], func=AF.Square),
                   src_reads, [sqB[s]])
                op("pe", lambda e, c=c, s=s, pss=pss: e.matmul(pss.ap[:], lhsT=onesb, rhs=SQ[s][:],
                                                              start=(c == 0), stop=(c == nchunk - 1)),
                   [sqB[s], cbB], [pss])
            op("dve", lambda e, pss=pss: e.tensor_scalar(out=out_rstd, in0=pss.ap[:], scalar1=inv_n, scalar2=EPS,
                                                         op0=ALU.mult, op1=ALU.add), [pss], [out_buf])
            op("act", lambda e: e.activation(out=out_rstd, in_=out_rstd, func=AF.Sqrt), [out_buf], [out_buf])
            op("dve", lambda e: e.reciprocal(out=out_rstd, in_=out_rstd), [out_buf], [out_buf])

        mSQ = [sb(es, "mSQ%d" % s_, [128, 512], BF16) for s_ in range(2)]
        msqB = [Buf(t) for t in mSQ]
        mRS = [sb(es, "mRS%d" % s_, [128, 512]) for s_ in range(3)]
        mrsB = [Buf(t) for t in mRS]
        mTM = [sb(es, "mTM%d" % s_, [128, 512]) for s_ in range(2)]
        mtmB = [Buf(t) for t in mTM]
        mcnt = [0]

        def modulate(ph, i, which, H, Hb):
            goff = 0 if which == 0 else 8
            shoff = 0 if which == 0 else 24
            pss = {}

            def st1a(tb):
                r = tb % 3
                pss[tb] = ps_next()
                for c in range(8):
                    s_ = c % 2
                    op("act", lambda e: e.activation(out=mSQ[s_][:], in_=X[:, c, tb * 512:(tb + 1) * 512],
                                                     func=AF.Square), [Xb[tb]], [msqB[s_]])
                    op("pe", lambda e: e.matmul(pss[tb].ap[:], lhsT=onesb, rhs=mSQ[s_][:], start=(c == 0),
                                                stop=(c == 7)), [msqB[s_], cbB], [pss[tb]])
                op("dve", lambda e: e.tensor_scalar(out=mRS[r][:], in0=pss[tb].ap[:], scalar1=1.0 / D, scalar2=EPS,
                                                    op0=ALU.mult, op1=ALU.add), [pss[tb]], [mrsB[r]])

            def st1b(tb):
                r = tb % 3
                op("act", lambda e: e.activation(out=mRS[r][:], in_=mRS[r][:], func=AF.Sqrt), [mrsB[r]], [mrsB[r]])
                op("dve", lambda e: e.reciprocal(out=mRS[r][:], in_=mRS[r][:]), [mrsB[r]], [mrsB[r]])

            def st2(tb):
                r = tb % 3
                for fc in range(8):
                    s_ = mcnt[0] % 2
                    mcnt[0] += 1
                    op("dve", lambda e: e.scalar_tensor_tensor(
                        out=mTM[s_][:], in0=X[:, fc, tb * 512:(tb + 1) * 512], scalar=G12[:, i, goff + fc:goff + fc + 1],
                        in1=mRS[r][:], op0=ALU.mult, op1=ALU.mult), [Xb[tb], modB, mrsB[r]], [mtmB[s_]])
                    op("act", lambda e: e.activation(
                        out=H[:, fc, tb * 512:(tb + 1) * 512], in_=mTM[s_][:], func=AF.Identity,
                        bias=MODS[:, i, shoff + fc:shoff + fc + 1], scale=1.0), [mtmB[s_], modB], [Hb[tb]])

            st1a(0)
            st1a(1)
            st1b(0)
            for tb in range(TB):
                if tb + 2 < TB:
                    st1a(tb + 2)
                if tb + 1 < TB:
                    st1b(tb + 1)
                st2(tb)
            return []

        def resid_add(pb, f, tb, i, goff):
            op("dve", lambda e: e.scalar_tensor_tensor(
                out=X[:, f, tb * 512:(tb + 1) * 512], in0=pb.ap[:], scalar=MODS[:, i, goff + f:goff + f + 1],
                in1=X[:, f, tb * 512:(tb + 1) * 512], op0=ALU.mult, op1=ALU.add), [pb, modB, Xb[tb]], [Xb[tb]])

        def ffn(i, inext=None, is_last=False):
            with ExitStack() as ph:
                H = sb(ph, "fH", [128, 8, NT], BF16)
                Hb = [Buf(H[:, :, t * 512:(t + 1) * 512]) for t in range(TB)]
                tmp = []
                if inext is not None:
                    AW = [sb(ph, "fAW%d" % s, [128, 8, 384], BF16) for s in range(2)]
                    awB = [Buf(t) for t in AW]
                else:
                    AW, awB = [], []
                pm = PS[7]
                if is_last:
                    YF = [sb(ph, "oYF%d" % s, [128, 8, 128]) for s in range(2)]
                    yfB = [Buf(t) for t in YF]
                    YS = [sb(ph, "oYS%d" % s, [128, D]) for s in range(2)]
                    ysB = [Buf(t) for t in YS]
                    tmp = yfB + ysB

                def final_tb(tb):
                    r = tb % 3
                    pss = ps_next(range(7))
                    for c in range(8):
                        s_ = c % 2
                        op("act", lambda e: e.activation(out=mSQ[s_][:], in_=X[:, c, tb * 512:(tb + 1) * 512],
                                                         func=AF.Square), [Xb[tb]], [msqB[s_]])
                        op("pe", lambda e: e.matmul(pss.ap[:], lhsT=onesb, rhs=mSQ[s_][:], start=(c == 0),
                                                    stop=(c == 7)), [msqB[s_], cbB], [pss])
                    op("dve", lambda e: e.tensor_scalar(out=mRS[r][:], in0=pss.ap[:], scalar1=1.0 / D, scalar2=EPS,
                                                        op0=ALU.mult, op1=ALU.add), [pss], [mrsB[r]])
                    op("act", lambda e: e.activation(out=mRS[r][:], in_=mRS[r][:], func=AF.Sqrt), [mrsB[r]], [mrsB[r]])
                    op("dve", lambda e: e.reciprocal(out=mRS[r][:], in_=mRS[r][:]), [mrsB[r]], [mrsB[r]])
                    for q in range(4):
                        tt = tb * 4 + q
                        y_ = tt % 2
                        tok = slice(tt * 128, (tt + 1) * 128)
                        for fc in range(8):
                            op("dve", lambda e: e.scalar_tensor_tensor(
                                out=YF[y_][:, fc, :], in0=X[:, fc, tok], scalar=c_("gfin", fc),
                                in1=mRS[r][:, q * 128:(q + 1) * 128], op0=ALU.mult, op1=ALU.mult),
                               [Xb[tb], cstB, mrsB[r]], [yfB[y_]])
                        for half in range(2):
                            pb = ps_next(range(7))
                            for rr_ in range(4):
                                fc = half * 4 + rr_
                                op("pe", lambda e: e.transpose(out=pb.ap[:, rr_ * 128:(rr_ + 1) * 128],
                                                               in_=YF[y_][:, fc, :], identity=identf),
                                   [yfB[y_], cstB], [pb])
                            if half == 0:
                                op("act", lambda e: e.copy(out=YS[y_][:, 0:512], in_=pb.ap[:]), [pb], [ysB[y_]])
                            else:
                                op("dve", lambda e: e.tensor_copy(out=YS[y_][:, 512:1024], in_=pb.ap[:]),
                                   [pb], [ysB[y_]])
                        dma("sp", y_d[tok, :], YS[y_][:], reads=[ysB[y_]])
                ACTB = sb(ph, "fACT", [128, 4, NT], BF16)
                actB = [Buf(ACTB[:, :, t * 512:(t + 1) * 512]) for t in range(TB)]
                WI = [sb(ph, "fWI%d" % s, [128, 8, 2, 512], BF16) for s in range(2)]
                wiB = [Buf(t) for t in WI]
                WO = [sb(ph, "fWO%d" % s, [128, 4, D], BF16) for s in range(2)]
                woB = [Buf(t) for t in WO]
                SA = [sb(ph, "fSA%d" % s, [128, 512]) for s in range(2)]
                saB = [Buf(t) for t in SA]
                win = ffn_w_in[i].rearrange("(c p) n -> p c n", p=128)
                wout = ffn_w_out[i].rearrange("(j p) n -> p j n", p=128)
                groups = [(0, 4), (4, 4), (8, 4), (12, 4), (16, 4), (20, 2)]
                nsa = 0
                def load_w(g):
                    j0_, nj_ = groups[g]
                    s_ = g % 2
                    for ab in range(2):
                        c0 = ab * FFN_H + j0_ * 128
                        dma("pool", WI[s_][:, :, ab, 0:nj_ * 128], win[:, :, c0:c0 + nj_ * 128], writes=[wiB[s_]])
                    dma("pool", WO[s_][:, 0:nj_, :], wout[:, j0_:j0_ + nj_, :], writes=[woB[s_]])

                load_w(0)
                if inext is not None:
                    for b in range(0, 2):
                        ada_dma(inext, b, AW, awB)
                modulate(ph, i, 1, H, Hb)
                for g, (j0, nj) in enumerate(groups):
                    s = g % 2
                    if inext is not None and 0 < g < 4:
                        for b in range(4 * g, 4 * g + 2):
                            ada_dma(inext, b, AW, awB)
                    for tb in range(TB):
                        for j in range(nj):
                            pa = ps_next(range(7))
                            pbb = ps_next(range(7))
                            for ab, pp in ((0, pa), (1, pbb)):
                                for kc in range(8):
                                    op("pe", lambda e, pp=pp, kc=kc, ab=ab, j=j, tb=tb, s=s: e.matmul(
                                        pp.ap[:], lhsT=WI[s][:, kc, ab, j * 128:(j + 1) * 128],
                                        rhs=H[:, kc, tb * 512:(tb + 1) * 512], start=(kc == 0), stop=(kc == 7)),
                                       [wiB[s], Hb[tb]], [pp])
                            q = nsa % 2
                            nsa += 1
                            op("act", lambda e, pa=pa, q=q: e.activation(out=SA[q][:], in_=pa.ap[:], func=AF.Silu),
                               [pa], [saB[q]])
                            op("dve", lambda e, pbb=pbb, q=q, j=j, tb=tb: e.tensor_tensor(
                                out=ACTB[:, j, tb * 512:(tb + 1) * 512], in0=SA[q][:], in1=pbb.ap[:], op=ALU.mult),
                               [saB[q], pbb], [actB[tb]])
                    if g + 1 < len(groups):
                        load_w(g + 1)
                    if inext is not None and g < 4:
                        for b in range(4 * g, 4 * g + 2):
                            ada_pe(b, pm, AW, awB)
                        for b in range(4 * g + 2, 4 * g + 4):
                            ada_dma(inext, b, AW, awB)
                    def pass2(tb):
                        for f in range(8):
                            po = ps_next(range(7))
                            for j in range(nj):
                                op("pe", lambda e: e.matmul(
                                    po.ap[:], lhsT=WO[s][:, j, f * 128:(f + 1) * 128],
                                    rhs=ACTB[:, j, tb * 512:(tb + 1) * 512], start=(j == 0), stop=(j == nj - 1)),
                                   [woB[s], actB[tb]], [po])
                            resid_add(po, f, tb, i, 40)

                    if is_last and g == len(groups) - 1:
                        pass2(0)
                        pass2(1)
                        final_tb(0)
                        pass2(2)
                        final_tb(1)
                        pass2(3)
                        final_tb(2)
                        final_tb(3)
                    else:
                        for tb in range(TB):
                            pass2(tb)
                    if inext is not None and g < 4:
                        for b in range(4 * g + 2, 4 * g + 4):
                            ada_pe(b, pm, AW, awB)
                        if g == 3:
                            ada_finish(inext, pm)
                k.barrier()
                k.release(Hb + tmp + actB + wiB + woB + saB + awB)

        def conv_layer(i):
            with ExitStack() as ph:
                H = sb(ph, "cH", [128, 8, NT], BF16)
                Hb = [Buf(H[:, :, t * 512:(t + 1) * 512]) for t in range(TB)]
                ZB = sb(ph, "cZB", [128, 8, NT], BF16)
                zbB = [Buf(ZB[:, :, t * 512:(t + 1) * 512]) for t in range(TB)]
                rel = []
                with ExitStack() as ph2:
                    WCI = [sb(ph2, "cWI%d" % s, [128, 8, 3, 128], BF16) for s in range(2)]
                    wciB = [Buf(t) for t in WCI]
                    CS = sb(ph2, "cCS", [128, NT])
                    csB = Buf(CS)
                    Z = sb(ph2, "cZ", [128, NT + 2])
                    zB = Buf(Z)
                    TMP = sb(ph2, "cTMP", [128, NT])
                    tB = Buf(TMP)
                    MK = sb(ph2, "cMK", [128, 2, NT], BF16)
                    mkB = Buf(MK)
                    for m in range(2):
                        dma("pool", MK[:, m, :], cmask_d[m], writes=[mkB])
                    op("dve", lambda e: e.memset(Z[:, 0:1], 0.0), [], [zB])
                    op("dve", lambda e: e.memset(Z[:, NT + 1:NT + 2], 0.0), [], [zB])
                    cwin = conv_w_in[0].rearrange("(c p) n -> p c n", p=128)
                    for fc in range(8):
                        s = fc % 2
                        for g3 in range(3):
                            dma("pool", WCI[s][:, :, g3, :], cwin[:, :, g3 * D + fc * 128:g3 * D + (fc + 1) * 128],
                                writes=[wciB[s]])
                        if fc == 0:
                            modulate(ph, i, 0, H, Hb)

                        def proj(g3, banks):
                            for tb in range(TB):
                                pp = PS[banks[tb]]
                                for kc in range(8):
                                    op("pe", lambda e, pp=pp, kc=kc, tb=tb: e.matmul(
                                        pp.ap[:], lhsT=WCI[s][:, kc, g3, :], rhs=H[:, kc, tb * 512:(tb + 1) * 512],
                                        start=(kc == 0), stop=(kc == 7)), [wciB[s], Hb[tb]], [pp])
                        bA = [0, 1, 2, 3] if fc % 2 == 0 else [4, 5, 6, 7]
                        bB = [4, 5, 6, 7] if fc % 2 == 0 else [0, 1, 2, 3]
                        proj(1, bA)
                        for tb in range(TB):
                            op("act", lambda e, tb=tb: e.copy(out=CS[:, tb * 512:(tb + 1) * 512], in_=PS[bA[tb]].ap[:]),
                               [PS[bA[tb]]], [csB])
                        proj(2, bB)
                        for tb in range(TB):
                            op("dve", lambda e, tb=tb: e.tensor_tensor(
                                out=Z[:, 1 + tb * 512:1 + (tb + 1) * 512], in0=CS[:, tb * 512:(tb + 1) * 512],
                                in1=PS[bB[tb]].ap[:], op=ALU.mult), [csB, PS[bB[tb]]], [zB])
                        proj(0, bA)
                        w0 = c_("convw", 0 * 8 + fc)
                        w1 = c_("convw", 1 * 8 + fc)
                        w2 = c_("convw", 2 * 8 + fc)
                        op("dve", lambda e: e.tensor_scalar(out=CS[:], in0=Z[:, 1:NT + 1], scalar1=w1, scalar2=None,
                                                            op0=ALU.mult), [zB, cstB], [csB])
                        op("dve", lambda e: e.scalar_tensor_tensor(out=TMP[:], in0=Z[:, 0:NT], scalar=w0,
                                                                   in1=MK[:, 0, :], op0=ALU.mult, op1=ALU.mult),
                           [zB, cstB, mkB], [tB])
                        op("dve", lambda e: e.tensor_tensor(out=CS[:], in0=CS[:], in1=TMP[:], op=ALU.add),
                           [csB, tB], [csB])
                        op("dve", lambda e: e.scalar_tensor_tensor(out=TMP[:], in0=Z[:, 2:NT + 2], scalar=w2,
                                                                   in1=MK[:, 1, :], op0=ALU.mult, op1=ALU.mult),
                           [zB, cstB, mkB], [tB])
                        op("dve", lambda e: e.tensor_tensor(out=CS[:], in0=CS[:], in1=TMP[:], op=ALU.add),
                           [csB, tB], [csB])
                        for tb in range(TB):
                            op("dve", lambda e, tb=tb, fc=fc: e.tensor_tensor(
                                out=ZB[:, fc, tb * 512:(tb + 1) * 512], in0=CS[:, tb * 512:(tb + 1) * 512],
                                in1=PS[bA[tb]].ap[:], op=ALU.mult), [csB, PS[bA[tb]]], [zbB[tb]])
                    k.barrier()
                    k.release(wciB + [csB, zB, tB, mkB])
                with ExitStack() as ph2:
                    WCO = sb(ph2, "cWO", [128, 8, D], BF16)
                    wcoB = Buf(WCO)
                    dma("pool", WCO[:], conv_w_out[0].rearrange("(c p) n -> p c n", p=128), writes=[wcoB])
                    for tb in range(TB):
                        for f in range(8):
                            po = ps_next()
                            for kc in range(8):
                                op("pe", lambda e, po=po, kc=kc, f=f, tb=tb: e.matmul(
                                    po.ap[:], lhsT=WCO[:, kc, f * 128:(f + 1) * 128],
                                    rhs=ZB[:, kc, tb * 512:(tb + 1) * 512], start=(kc == 0), stop=(kc == 7)),
                                   [wcoB, zbB[tb]], [po])
                            resid_add(po, f, tb, i, 16)
                    k.barrier()
                    k.release([wcoB])
                k.release(Hb + zbB)

        def mla_layer(i, j):
            with ExitStack() as ph:
                QAN = sb(ph, "aQAN", [128, 3, NT], BF16)
                qanB = [Buf(QAN[:, :, t * 512:(t + 1) * 512]) for t in range(TB)]
                CKT = sb(ph, "aCKT", [128, 2, 2560], BF16)
                cktB = [Buf(CKT[:, :, t * 512:(t + 1) * 512]) for t in range(5)]
                KPT = sb(ph, "aKPT", [128, 2560], BF16)
                kptB = [Buf(KPT[:, t * 512:(t + 1) * 512]) for t in range(5)]
                for t in range(5):
                    op("dve", lambda e: e.memset(KPT[64:128, t * 512:(t + 1) * 512], 0.0), [], [kptB[t]])
                dma("pool", KPT[64:80, :], mk_d[0], writes=kptB)
                with ExitStack() as ph2:
                    H = sb(ph2, "aH", [128, 8, NT], BF16)
                    Hb = [Buf(H[:, :, t * 512:(t + 1) * 512]) for t in range(TB)]
                    WA = sb(ph2, "aWA", [128, 8, 768], BF16)
                    waB = Buf(WA)
                    wa_src = mla_w_a[j].rearrange("(c p) n -> p c n", p=128)
                    dma("pool", WA[:, :, 0:704], wa_src, writes=[waB])
                    for c in range(2):
                        dma("pool", CKT[:, c, 0:512], cckv_d[j, c * 128:(c + 1) * 128, :], writes=[cktB[0]])
                    dma("pool", KPT[0:64, 0:512], ckpe_d[j], writes=[kptB[0]])
                    tmp = modulate(ph2, i, 0, H, Hb)
                    for ax in range(2):
                        for hf in range(2):
                            d0 = 704 + ax * 32 + hf * 16
                            s0 = 640 + ax * 32 + (1 - hf) * 16
                            op("dve", lambda e: e.tensor_copy(out=WA[:, :, d0:d0 + 16], in_=WA[:, :, s0:s0 + 16]),
                               [waB], [waB])
                    SQ = [sb(ph2, "aSQ%d" % s, [128, 512], BF16) for s in range(2)]
                    sqB = [Buf(t) for t in SQ]
                    QG = sb(ph2, "aQG", [128, 3, 512])
                    qgB = Buf(QG)
                    RS = sb(ph2, "aRS", [128, 512])
                    rsB = Buf(RS)
                    CKF = sb(ph2, "aCKF", [128, 2, 512])
                    ckfB = Buf(CKF)
                    KPF = sb(ph2, "aKPF", [64, 512])
                    kpfB = Buf(KPF)
                    CSN = sb(ph2, "aCSN", [64, 2, 512])
                    csnB = Buf(CSN)
                    T1 = sb(ph2, "aT1", [64, 512])
                    t1B = Buf(T1)
                    T2 = sb(ph2, "aT2", [64, 512])
                    t2B = Buf(T2)
                    OST = [sb(ph2, "aOST%d" % s, [128, 320]) for s in range(2)]
                    ostB = [Buf(t) for t in OST]
                    for tb in range(TB):
                        tsl = slice(tb * 512, (tb + 1) * 512)
                        dma("sp", CSN[:, 0, :], rope_d[0, :, tsl], writes=[csnB])
                        dma("sp", CSN[:, 1, :], rope_d[1, :, tsl], writes=[csnB])
                        pq = [ps_next() for _ in range(3)]
                        for c in range(3):
                            for kc in range(8):
                                op("pe", lambda e, c=c, kc=kc, tsl=tsl: e.matmul(
                                    pq[c].ap[:], lhsT=WA[:, kc, c * 128:(c + 1) * 128], rhs=H[:, kc, tsl],
                                    start=(kc == 0), stop=(kc == 7)), [waB, Hb[tb]], [pq[c]])
                        rms_rstd((SQ, sqB), [pq[c].ap[:] for c in range(3)], 3, RS[:], rsB, pq, 1.0 / 384)
                        for c in range(3):
                            op("dve", lambda e, c=c: e.tensor_scalar(out=QG[:, c, :], in0=pq[c].ap[:],
                                                                     scalar1=c_("qng", j * 3 + c), scalar2=None,
                                                                     op0=ALU.mult), [pq[c], cstB], [qgB])
                        for c in range(3):
                            op("dve", lambda e, c=c, tsl=tsl: e.tensor_tensor(out=QAN[:, c, tsl], in0=QG[:, c, :],
                                                                              in1=RS[:], op=ALU.mult),
                               [qgB, rsB], [qanB[tb]])
                        pc = [ps_next() for _ in range(2)]
                        for c in range(2):
                            for kc in range(8):
                                op("pe", lambda e, c=c, kc=kc, tsl=tsl: e.matmul(
                                    pc[c].ap[:], lhsT=WA[:, kc, 384 + c * 128:384 + (c + 1) * 128], rhs=H[:, kc, tsl],
                                    start=(kc == 0), stop=(kc == 7)), [waB, Hb[tb]], [pc[c]])
                        rms_rstd((SQ, sqB), [pc[c].ap[:] for c in range(2)], 2, RS[:], rsB, pc, 1.0 / 256)
                        for c in range(2):
                            op("dve", lambda e, c=c: e.scalar_tensor_tensor(
                                out=CKF[:, c, :], in0=pc[c].ap[:], scalar=c_("kvg", j * 2 + c), in1=RS[:],
                                op0=ALU.mult, op1=ALU.mult), [pc[c], cstB, rsB], [ckfB])
                        op("act", lambda e, tb=tb: e.copy(out=CKT[:, :, 512 + tb * 512:512 + (tb + 1) * 512],
                                                          in_=CKF[:]), [ckfB], [cktB[1 + tb]])
                        pk = ps_next()
                        pks = ps_next()
                        for pp, c0 in ((pk, 640), (pks, 704)):
                            for kc in range(8):
                                op("pe", lambda e, pp=pp, c0=c0, kc=kc, tsl=tsl: e.matmul(
                                    pp.ap[0:64, :], lhsT=WA[:, kc, c0:c0 + 64], rhs=H[:, kc, tsl],
                                    start=(kc == 0), stop=(kc == 7)), [waB, Hb[tb]], [pp])
                        op("act", lambda e, pk=pk: e.copy(out=KPF[:], in_=pk.ap[0:64, :]), [pk], [kpfB])
                        op("dve", lambda e, pk=pk: e.tensor_tensor(out=T1[:], in0=pk.ap[0:64, :], in1=CSN[:, 0, :],
                                                                   op=ALU.mult), [pk, csnB], [t1B])
                        op("dve", lambda e, pks=pks: e.tensor_tensor(out=T2[:], in0=pks.ap[0:64, :], in1=CSN[:, 1, :],
                                                                     op=ALU.mult), [pks, csnB], [t2B])
                        op("dve", lambda e, tb=tb: e.tensor_tensor(
                            out=KPT[0:64, 512 + tb * 512:512 + (tb + 1) * 512], in0=T1[:], in1=T2[:], op=ALU.add),
                           [t1B, t2B], [kptB[1 + tb]])
                        for q in range(0 if 'noout' in _DBG else 4):
                            tt = tb * 4 + q
                            s = tt % 2
                            pb = ps_next()
                            for c in range(2):
                                op("pe", lambda e, pb=pb, c=c, q=q: e.transpose(
                                    out=pb.ap[:, c * 128:(c + 1) * 128], in_=CKF[:, c, q * 128:(q + 1) * 128],
                                    identity=identf), [ckfB, cstB], [pb])
                            op("pe", lambda e, pb=pb, q=q: e.transpose(
                                out=pb.ap[:, 256:320], in_=KPF[:, q * 128:(q + 1) * 128], identity=identf[0:64, 0:64]),
                               [kpfB, cstB], [pb])
                            op("act", lambda e, pb=pb, s=s: e.copy(out=OST[s][:], in_=pb.ap[:, 0:320]),
                               [pb], [ostB[s]])
                            dma("sp", ockv_d[j, tt * 128:(tt + 1) * 128, :], OST[s][:, 0:256], reads=[ostB[s]])
                            dma("sp", okpe_d[j, tt * 128:(tt + 1) * 128, :], OST[s][:, 256:320], reads=[ostB[s]])
                    k.barrier()
                    k.release(Hb + [waB, qgB, rsB, ckfB, kpfB, csnB, t1B, t2B] + sqB + ostB)
                with ExitStack() as ph2:
                    WQ = sb(ph2, "bWQ", [128, 3, 2048], BF16)
                    wqB = Buf(WQ)
                    wq_src = mla_w_q_b[j].rearrange("(c p) n -> p c n", p=128)
                    dma("pool", WQ[:, :, 0:1536], wq_src, writes=[wqB])
                    for kc in range(3):
                        dst = WQ[:, kc, 1536:2048].rearrange("p (h a f g) -> p h a f g", h=8, a=2, f=2)
                        srcv = WQ[:, kc, 0:1536].rearrange("p (h x) -> p h x", x=192)[:, :, 128:192].rearrange(
                            "p h (a f g) -> p h a f g", a=2, f=2)
                        for hf in range(2):
                            op("dve", lambda e: e.tensor_copy(out=dst[:, :, :, hf, :], in_=srcv[:, :, :, 1 - hf, :]),
                               [wqB], [wqB])
                    WKV = sb(ph2, "bWKV", [128, 2, 2048], BF16)
                    wkvB = Buf(WKV)
                    dma("pool", WKV[:], mla_w_kv_b[j].rearrange("(c p) n -> p c n", p=128), writes=[wkvB])
                    WO = sb(ph2, "bWO", [128, 2, D], BF16)
                    woB = Buf(WO)
                    KN = sb(ph2, "bKN", [128, 2, 2560], BF16)
                    knB = [[Buf(KN[:, hh, kb * 512:(kb + 1) * 512]) for kb in range(5)] for hh in range(2)]
                    V = sb(ph2, "bV", [128, 20, 2, 128], BF16)
                    vB = [Buf(V[:, kt]) for kt in range(20)]
                    OT = sb(ph2, "bOT", [128, 2, NT], BF16)
                    otB = [[Buf(OT[:, hh, t * 512:(t + 1) * 512]) for t in range(TB)] for hh in range(2)]
                    PT = [sb(ph2, "bPT%d" % s, [128, 512], BF16) for s in range(4)]
                    ptB = [Buf(t) for t in PT]
                    QN = [sb(ph2, "bQN%d" % s_, [128, 512], BF16) for s_ in range(2)]
                    qnB = [Buf(t) for t in QN]
                    QP = [sb(ph2, "bQP%d" % s_, [128, 512], BF16) for s_ in range(2)]
                    qpB = [Buf(t) for t in QP]
                    for s_ in range(2):
                        op("dve", lambda e: e.memset(QP[s_][64:128, :], 0.0), [], [qpB[s_]])
                    CSN = sb(ph2, "bCSN", [64, 2, 512])
                    csnB = Buf(CSN)
                    T1 = sb(ph2, "bT1", [64, 512])
                    t1B = Buf(T1)
                    T2 = sb(ph2, "bT2", [64, 512])
                    t2B = Buf(T2)
                    RI = sb(ph2, "bRI", [128, 512])
                    riB = Buf(RI)
                    RA = [sb(ph2, "bRA%d" % s_, [128, 512]) for s_ in range(2)]
                    raB = [Buf(t) for t in RA]
                    wo_src = mla_w_o[j].rearrange("(h p) n -> p h n", p=128)
                    wkv3 = WKV[:].rearrange("p c (h x) -> p c h x", x=256)
                    def wo_part():
                        for tb in range(0 if 'a3' in _DBG else TB):
                            for f in range(8):
                                pb = ps_next([6, 7])
                                for hh in range(2):
                                    op("pe", lambda e: e.matmul(
                                        pb.ap[:], lhsT=WO[:, hh, f * 128:(f + 1) * 128],
                                        rhs=OT[:, hh, tb * 512:(tb + 1) * 512], start=(hh == 0), stop=(hh == 1)),
                                       [woB, otB[hh][tb]], [pb])
                                resid_add(pb, f, tb, i, 16)

                    for hp in range(0 if 'noattn' in _DBG else 4):
                        for hh in range(0 if 'a1' in _DBG else 2):
                            h = 2 * hp + hh
                            for kb in range(5):
                                pb = ps_next([6, 7])
                                for c in range(2):
                                    op("pe", lambda e, pb=pb, c=c, h=h, kb=kb: e.matmul(
                                        pb.ap[:], lhsT=WKV[:, c, h * 256:h * 256 + 128],
                                        rhs=CKT[:, c, kb * 512:(kb + 1) * 512], start=(c == 0), stop=(c == 1)),
                                       [wkvB, cktB[kb]], [pb])
                                op("act", lambda e, pb=pb, hh=hh, kb=kb: e.copy(
                                    out=KN[:, hh, kb * 512:(kb + 1) * 512], in_=pb.ap[:]), [pb], [knB[hh][kb]])
                        for kt in range(0 if 'a4' in _DBG else 20):
                            pb = ps_next([6, 7])
                            for c in range(2):
                                op("pe", lambda e, pb=pb, c=c, kt=kt: e.matmul(
                                    pb.ap[:, 0:256].rearrange("p (h v) -> p h v", h=2),
                                    lhsT=CKT[:, c, kt * 128:(kt + 1) * 128],
                                    rhs=wkv3[:, c, 2 * hp:2 * hp + 2, 128:256], start=(c == 0), stop=(c == 1)),
                                   [wkvB, cktB[kt // 4]], [pb])
                            op("dve", lambda e, pb=pb, kt=kt: e.tensor_copy(
                                out=V[:, kt], in_=pb.ap[:, 0:256].rearrange("p (h v) -> p h v", h=2)), [pb], [vB[kt]])
                        if hp > 0:
                            wo_part()
                        dma("pool", WO[:], wo_src[:, 2 * hp:2 * hp + 2, :], writes=[woB])

                        def prep(hq):
                            hh_, qb_ = hq // TB, hq % TB
                            h_ = 2 * hp + hh_
                            sl_ = hq % 2
                            qsl_ = slice(qb_ * 512, (qb_ + 1) * 512)
                            dma("sp", CSN[:, 0, :], rope_d[0, :, qsl_], writes=[csnB])
                            dma("sp", CSN[:, 1, :], rope_d[1, :, qsl_], writes=[csnB])
                            pn = ps_next([6, 7])
                            for c in range(3):
                                op("pe", lambda e: e.matmul(
                                    pn.ap[:], lhsT=WQ[:, c, h_ * 192:h_ * 192 + 128], rhs=QAN[:, c, qsl_],
                                    start=(c == 0), stop=(c == 2)), [wqB, qanB[qb_]], [pn])
                            op("act", lambda e: e.copy(out=QN[sl_][:], in_=pn.ap[:]), [pn], [qnB[sl_]])
                            pr = ps_next([6, 7])
                            for c in range(3):
                                op("pe", lambda e: e.matmul(
                                    pr.ap[0:64, :], lhsT=WQ[:, c, h_ * 192 + 128:h_ * 192 + 192], rhs=QAN[:, c, qsl_],
                                    start=(c == 0), stop=(c == 2)), [wqB, qanB[qb_]], [pr])
                            op("dve", lambda e: e.tensor_tensor(out=T1[:], in0=pr.ap[0:64, :], in1=CSN[:, 0, :],
                                                                op=ALU.mult), [pr, csnB], [t1B])
                            prs = ps_next([6, 7])
                            for c in range(3):
                                op("pe", lambda e: e.matmul(
                                    prs.ap[0:64, :], lhsT=WQ[:, c, 1536 + h_ * 64:1536 + h_ * 64 + 64],
                                    rhs=QAN[:, c, qsl_], start=(c == 0), stop=(c == 2)), [wqB, qanB[qb_]], [prs])
                            op("dve", lambda e: e.tensor_tensor(out=T2[:], in0=prs.ap[0:64, :], in1=CSN[:, 1, :],
                                                                op=ALU.mult), [prs, csnB], [t2B])
                            dma("pool", QP[sl_][64:80, :], mk_d[1, :, qsl_], writes=[qpB[sl_]])
                            op("dve", lambda e: e.tensor_tensor(out=QP[sl_][0:64, :], in0=T1[:], in1=T2[:], op=ALU.add),
                               [t1B, t2B], [qpB[sl_]])

                        if 'a2' not in _DBG:
                            prep(0)
                        for hq in range(0 if 'a2' in _DBG else 2 * TB):
                            hh, qb = hq // TB, hq % TB
                            h = 2 * hp + hh
                            sl = hq % 2
                            qsl = slice(qb * 512, (qb + 1) * 512)
                            if True:
                                po, prw = PS[4], PS[5]

                                def s_tile(kt):
                                    pb = PS[kt % 4]
                                    op("pe", lambda e: e.matmul(pb.ap[:], lhsT=KN[:, hh, kt * 128:(kt + 1) * 128],
                                                                rhs=QN[sl][:], start=True, stop=False),
                                       [knB[hh][kt // 4], qnB[sl]], [pb])
                                    op("pe", lambda e: e.matmul(pb.ap[:], lhsT=KPT[:, kt * 128:(kt + 1) * 128],
                                                                rhs=QP[sl][:], start=False, stop=True),
                                       [kptB[kt // 4], qpB[sl]], [pb])
                                    op("act", lambda e: e.activation(out=PT[kt % 4][:], in_=pb.ap[:], func=AF.Exp,
                                                                     scale=MLA_SCALE), [pb], [ptB[kt % 4]])

                                def pv_tile(kt):
                                    op("pe", lambda e: e.matmul(po.ap[:], lhsT=V[:, kt, hh, :], rhs=PT[kt % 4][:],
                                                                start=(kt == 0), stop=(kt == 19)),
                                       [vB[kt], ptB[kt % 4]], [po])
                                    op("pe", lambda e: e.matmul(prw.ap[:], lhsT=onesb, rhs=PT[kt % 4][:],
                                                                start=(kt == 0), stop=(kt == 19)),
                                       [cbB, ptB[kt % 4]], [prw])
                                s_tile(0)
                                s_tile(1)
                                for kt in range(20):
                                    if kt + 2 < 20:
                                        s_tile(kt + 2)
                                    pv_tile(kt)
                                    if kt == 9 and hq + 1 < 2 * TB:
                                        prep(hq + 1)
                                op("act", lambda e: e.copy(out=RA[0][:], in_=po.ap[:]), [po], [raB[0]])
                                op("act", lambda e: e.copy(out=RA[1][:], in_=prw.ap[:]), [prw], [raB[1]])
                                op("dve", lambda e: e.reciprocal(out=RI[:], in_=RA[1][:]), [raB[1]], [riB])
                                op("dve", lambda e, hh=hh, qsl=qsl: e.tensor_tensor(out=OT[:, hh, qsl], in0=RA[0][:],
                                                                                    in1=RI[:], op=ALU.mult),
                                   [raB[0], riB], [otB[hh][qb]])
                    if 'noattn' not in _DBG:
                        wo_part()
                    k.barrier()
                    k.release([wqB, wkvB, woB, csnB, t1B, t2B, riB] + qnB + qpB + raB + ptB + vB + sum(knB, []) + sum(otB, []))
                k.release(qanB + cktB + kptB)

        def ret_layer(i):
            with ExitStack() as ph:
                H = sb(ph, "rH", [128, 8, NT], BF16)
                Hb = [Buf(H[:, :, t * 512:(t + 1) * 512]) for t in range(TB)]
                WR = sb(ph, "rWR", [128, 8, 1536], BF16)
                wrB = Buf(WR)
                WRO = sb(ph, "rWRO", [128, 4, D], BF16)
                wroB = Buf(WRO)
                OF = sb(ph, "rOF", [128, 16, 512], BF16)
                ofB = [Buf(OF[:, t]) for t in range(16)]
                U = sb(ph, "rU", [128, 2, 512])
                uB = Buf(U)
                UBs = [sb(ph, "rUB%d" % s_, [128, 2, 512], BF16) for s_ in range(2)]
                ubB = [Buf(t) for t in UBs]
                ubC = [[Buf(t[:, c, :]) for c in range(2)] for t in UBs]
                ubi = [0]
                SO = [sb(ph, "rSO%d" % s, [128, 2, 512]) for s in range(2)]
                soB = [Buf(t) for t in SO]
                soC = [[Buf(t[:, c, :]) for c in range(2)] for t in SO]
                QT = sb(ph, "rQT", [128, 2, 512], BF16)
                qtB = Buf(QT)
                KT = sb(ph, "rKT", [128, 2, 512], BF16)
                ktB = Buf(KT)
                KTM = sb(ph, "rKTM", [128, 4, 256], BF16)
                ktmB = [Buf(KTM[:, q]) for q in range(4)]
                VTM = sb(ph, "rVTM", [128, 4, 512], BF16)
                vtmB = [Buf(VTM[:, q]) for q in range(4)]
                PTM = [sb(ph, "rPTM%d" % s, [128, 128], BF16) for s in range(3)]
                ptmB = [Buf(t) for t in PTM]
                OSM = sb(ph, "rOSM", [128, 512])
                osmB = Buf(OSM)
                SG = sb(ph, "rSG", [128, 512])
                sgB = Buf(SG)
                GN = sb(ph, "rGN", [128, 512])
                gnB = Buf(GN)
                YB = [sb(ph, "rYB%d" % s_, [128, 512], BF16) for s_ in range(3)]
                ybB = [Buf(t) for t in YB]
                YT = sb(ph, "rYT", [128, 4, 512], BF16)
                ytB = Buf(YT)
                ST = sb(ph, "rST", [128, 16])
                stB = Buf(ST)
                win = ret_w_in[0].rearrange("(c p) n -> p c n", p=128)
                wout = ret_w_out[0].rearrange("(e p) n -> p e n", p=128)
                nso = 0
                for h in range(4):
                    dma("pool", WR[:, :, 0:256], win[:, :, h * 256:(h + 1) * 256], writes=[wrB])
                    dma("pool", WR[:, :, 256:512], win[:, :, 1024 + h * 256:1024 + (h + 1) * 256], writes=[wrB])
                    dma("pool", WR[:, :, 512:1024], win[:, :, 2048 + h * 512:2048 + (h + 1) * 512], writes=[wrB])
                    dma("pool", WR[:, :, 1024:1536], win[:, :, 4096 + h * 512:4096 + (h + 1) * 512], writes=[wrB])
                    dma("pool", WRO[:], wout[:, 4 * h:4 * h + 4, :], writes=[wroB])
                    dma("sp", GN[:], gng_d[:, h * 512:(h + 1) * 512], writes=[gnB])
                    if h == 0:
                        modulate(ph, i, 0, H, Hb)
                    for d in range(2):
                        dh = d * 4 + h
                        Acol = DEC[:, 0, dh:dh + 1]
                        Bcol = DEC[:, 1, dh:dh + 1]
                        Kcol = DEC[:, 2, dh:dh + 1]
                        Ccol = DEC[:, 3, dh:dh + 1]
                        mask = maskf if d == 0 else maskb
                        dma("sp", U[:], st0_d[d, h].rearrange("(c p) e -> p c e", p=128), writes=[uB])
                        cm0 = c_("cmf", 0) if d == 0 else c_("cmb", 15)
                        op("act", lambda e: e.activation(out=U[:], in_=U[:], func=AF.Identity, scale=cm0),
                           [uB, cstB], [uB])
                        op("act", lambda e: e.copy(out=UBs[ubi[0]][:], in_=U[:]), [uB],
                           [ubB[ubi[0]], ubC[ubi[0]][0], ubC[ubi[0]][1]])
                        scs = range(4) if d == 0 else range(3, -1, -1)
                        pending = []

                        def drain(keep):
                            while sum(1 for k_, _ in pending if k_ == "tr") > keep:
                                pending.pop(0)[1]()
                        for sc in scs:
                            tsl = slice(sc * 512, (sc + 1) * 512)
                            for c in range(2):
                                pb = ps_next()
                                for kc in range(8):
                                    op("pe", lambda e: e.matmul(
                                        pb.ap[:], lhsT=WR[:, kc, 256 + c * 128:256 + (c + 1) * 128], rhs=H[:, kc, tsl],
                                        start=(kc == 0), stop=(kc == 7)), [wrB, Hb[sc]], [pb])
                                op("dve", lambda e: e.tensor_copy(out=KT[:, c, :], in_=pb.ap[:]), [pb], [ktB])
                            for c in range(2):
                                pb = ps_next()
                                for kc in range(8):
                                    op("pe", lambda e: e.matmul(
                                        pb.ap[:], lhsT=WR[:, kc, c * 128:(c + 1) * 128], rhs=H[:, kc, tsl],
                                        start=(kc == 0), stop=(kc == 7)), [wrB, Hb[sc]], [pb])
                                op("act", lambda e: e.copy(out=QT[:, c, :], in_=pb.ap[:]), [pb], [qtB])
                            qs = list(range(4)) if d == 0 else [3, 2, 1, 0]

                            def proj_tm(q):
                                tok = slice(sc * 512 + q * 128, sc * 512 + (q + 1) * 128)
                                pb = ps_next()
                                pbv = pb.ap.bitcast(BF16)
                                for c in range(2):
                                    op("pe", lambda e: e.transpose(
                                        out=pbv[:, c * 128:(c + 1) * 128], in_=KT[:, c, q * 128:(q + 1) * 128],
                                        identity=identb), [ktB, cbB], [pb])
                                op("dve", lambda e: e.tensor_scalar(
                                    out=KTM[:, q, :], in0=pbv[:, 0:256], scalar1=Kcol, scalar2=None, op0=ALU.mult),
                                   [pb, decB], [ktmB[q]])
                                pb2 = ps_next()
                                for kc in range(8):
                                    op("pe", lambda e: e.matmul(
                                        pb2.ap[:], lhsT=H[:, kc, tok], rhs=WR[:, kc, 512:1024],
                                        start=(kc == 0), stop=(kc == 7)), [wrB, Hb[sc]], [pb2])
                                op("act", lambda e: e.copy(out=VTM[:, q, :], in_=pb2.ap[:]), [pb2], [vtmB[q]])

                            def st_tile(q):
                                loc = slice(q * 128, (q + 1) * 128)
                                pm_ = (sc * 4 + q) % 3
                                pb = ps_next()
                                for c in range(2):
                                    op("pe", lambda e: e.matmul(
                                        pb.ap[:, 0:128], lhsT=KT[:, c, loc], rhs=QT[:, c, loc],
                                        start=(c == 0), stop=(c == 1)), [ktB, qtB], [pb])
                                op("dve", lambda e: e.scalar_tensor_tensor(
                                    out=PTM[pm_][:], in0=pb.ap[:, 0:128], scalar=Acol, in1=mask,
                                    op0=ALU.mult, op1=ALU.mult), [pb, decB, cbB], [ptmB[pm_]])

                            proj_tm(qs[0])
                            st_tile(qs[0])
                            proj_tm(qs[1])
                            st_tile(qs[1])
                            for idx, q in enumerate(qs):
                                tt = sc * 4 + q
                                loc = slice(q * 128, (q + 1) * 128)
                                tok = slice(tt * 128, (tt + 1) * 128)
                                pm = tt % 3
                                if idx + 2 < 4:
                                    proj_tm(qs[idx + 2])
                                    st_tile(qs[idx + 2])
                                psts = []
                                for c in range(2):
                                    pst = ps_next()
                                    psts.append(pst)
                                    op("pe", lambda e: e.matmul(
                                        pst.ap[:], lhsT=KTM[:, q, c * 128:(c + 1) * 128], rhs=VTM[:, q, :],
                                        start=True, stop=True), [ktmB[q], vtmB[q]], [pst])
                                ucur = ubi[0]
                                ubi[0] = 1 - ubi[0]
                                so = nso % 2
                                nso += 1
                                for c in range(2):
                                    op("dve", lambda e: e.scalar_tensor_tensor(
                                        out=SO[so][:, c, :], in0=U[:, c, :], scalar=Ccol, in1=psts[c].ap[:],
                                        op0=ALU.mult, op1=ALU.add), [uB, decB, psts[c]], [soB[so], soC[so][c]])
                                if (d == 0 and tt % 2 == 1) or (d == 1 and tt % 2 == 0):
                                    dma("sp", ost_d[tt // 2, d, h].rearrange("(c p) e -> p c e", p=128), SO[so][:],
                                        reads=[soB[so]])
                                nxt = tt + 1 if d == 0 else tt - 1
                                if 0 <= nxt < 16:
                                    cm = c_("cmf" if d == 0 else "cmb", nxt)
                                    for c in range(2):
                                        op("act", lambda e: e.activation(
                                            out=UBs[1 - ucur][:, c, :], in_=SO[so][:, c, :], func=AF.Identity,
                                            scale=cm), [soC[so][c], cstB], [ubC[1 - ucur][c]])
                                    op("dve", lambda e: e.tensor_scalar(
                                        out=U[:], in0=SO[so][:], scalar1=cm, scalar2=None, op0=ALU.mult),
                                       [soB[so], cstB], [uB])
                                if d == 1:
                                    pg = ps_next()
                                    for kc in range(8):
                                        op("pe", lambda e: e.matmul(
                                            pg.ap[:], lhsT=H[:, kc, tok], rhs=WR[:, kc, 1024:1536],
                                            start=(kc == 0), stop=(kc == 7)), [wrB, Hb[sc]], [pg])
                                    op("act", lambda e: e.activation(out=SG[:], in_=pg.ap[:], func=AF.Silu),
                                       [pg], [sgB])
                                po = ps_next()
                                op("pe", lambda e: e.matmul(
                                    po.ap[:], lhsT=PTM[pm][:], rhs=VTM[:, q, :], start=True, stop=False),
                                   [ptmB[pm], vtmB[q]], [po])
                                for c in range(2):
                                    op("pe", lambda e: e.matmul(
                                        po.ap[:], lhsT=QT[:, c, loc], rhs=UBs[ucur][:, c, :], start=False,
                                        stop=(c == 1)), [qtB, ubC[ucur][c]], [po])
                                if d == 0:
                                    op("act", lambda e: e.activation(
                                        out=OF[:, tt, :], in_=po.ap[:], func=AF.Identity, scale=Bcol),
                                       [po, decB], [ofB[tt]])
                                else:
                                    yb = tt % 3
                                    op("dve", lambda e: e.scalar_tensor_tensor(
                                        out=OSM[:], in0=po.ap[:], scalar=Bcol, in1=OF[:, tt, :],
                                        op0=ALU.mult, op1=ALU.add), [po, decB, ofB[tt]], [osmB])
                                    op("dve", lambda e: e.bn_stats(out=ST[:, 0:6], in_=OSM[:]), [osmB], [stB])
                                    op("dve", lambda e: e.bn_aggr(out=ST[:, 8:10], in_=ST[:, 0:6]), [stB], [stB])
                                    op("dve", lambda e: e.tensor_scalar(
                                        out=ST[:, 10:11], in0=ST[:, 9:10], scalar1=EPS, scalar2=None,
                                        op0=ALU.add), [stB], [stB])
                                    op("act", lambda e: e.activation(out=ST[:, 10:11], in_=ST[:, 10:11], func=AF.Sqrt),
                                       [stB], [stB])
                                    op("dve", lambda e: e.reciprocal(out=ST[:, 10:11], in_=ST[:, 10:11]), [stB], [stB])
                                    op("dve", lambda e: e.tensor_scalar(
                                        out=OSM[:], in0=OSM[:], scalar1=ST[:, 8:9], scalar2=ST[:, 10:11],
                                        op0=ALU.subtract, op1=ALU.mult), [osmB, stB], [osmB])
                                    op("dve", lambda e: e.tensor_tensor(out=SG[:], in0=SG[:], in1=GN[:], op=ALU.mult),
                                       [sgB, gnB], [sgB])
                                    op("dve", lambda e: e.tensor_tensor(out=YB[yb][:], in0=OSM[:], in1=SG[:],
                                                                        op=ALU.mult), [osmB, sgB], [ybB[yb]])

                                    def tr(yb=yb, loc=loc):
                                        pt = ps_next()
                                        ptv = pt.ap.bitcast(BF16)
                                        for ec in range(4):
                                            op("pe", lambda e: e.transpose(
                                                out=ptv[:, ec * 128:(ec + 1) * 128],
                                                in_=YB[yb][:, ec * 128:(ec + 1) * 128], identity=identb),
                                               [ybB[yb], cbB], [pt])
                                        op("act", lambda e: e.copy(
                                            out=YT[:, :, loc], in_=ptv[:, 0:512].rearrange("p (e t) -> p e t", e=4)),
                                           [pt], [ytB])
                                    pending.append(("tr", tr))
                                    drain(2)
                            if d == 1:
                                def wout_fn(sc=sc):
                                    for f in range(8):
                                        pb = ps_next()
                                        for ec in range(4):
                                            op("pe", lambda e: e.matmul(
                                                pb.ap[:], lhsT=WRO[:, ec, f * 128:(f + 1) * 128], rhs=YT[:, ec, :],
                                                start=(ec == 0), stop=(ec == 3)), [wroB, ytB], [pb])
                                        resid_add(pb, f, sc, i, 16)
                                pending.append(("w", wout_fn))
                        while pending:
                            pending.pop(0)[1]()
                k.barrier()
                k.release(Hb + [wrB, wroB, uB, qtB, ktB, osmB, sgB, gnB, ytB, stB] + ubB + ybB + ofB + soB + ktmB
                          + vtmB + ptmB)

        for li, i in enumerate(layers):
            kind, j = i % 3, i // 3
            if kind == 0:
                mla_layer(i, j)
            elif kind == 1:
                conv_layer(i)
            else:
                ret_layer(i)
            ffn(i, layers[li + 1] if li + 1 < len(layers) else None, is_last=(li + 1 == len(layers)))

        with ExitStack() as ph:
          if not layers:
            SQ = [sb(ph, "oSQ%d" % s, [128, 512], BF16) for s in range(2)]
            sqB = [Buf(t) for t in SQ]
            RS = sb(ph, "oRS", [128, 512])
            rsB = Buf(RS)
            YF = sb(ph, "oYF", [128, 8, 512])
            yfB = Buf(YF)
            YS = [sb(ph, "oYS%d" % s, [128, D]) for s in range(2)]
            ysB = [Buf(t) for t in YS]
            for tb in range(TB):
                rms_rstd((SQ, sqB), [X[:, fc, tb * 512:(tb + 1) * 512] for fc in range(8)], 8, RS[:], rsB,
                         [Xb[tb]], 1.0 / D)
                for fc in range(8):
                    op("dve", lambda e, fc=fc, tb=tb: e.scalar_tensor_tensor(
                        out=YF[:, fc, :], in0=X[:, fc, tb * 512:(tb + 1) * 512], scalar=c_("gfin", fc), in1=RS[:],
                        op0=ALU.mult, op1=ALU.mult), [Xb[tb], cstB, rsB], [yfB])
                for q in range(4):
                    tt = tb * 4 + q
                    s = tt % 2
                    for half in range(2):
                        pb = ps_next()
                        for r in range(4):
                            fc = half * 4 + r
                            op("pe", lambda e, pb=pb, r=r, fc=fc, q=q: e.transpose(
                                out=pb.ap[:, r * 128:(r + 1) * 128], in_=YF[:, fc, q * 128:(q + 1) * 128],
                                identity=identf), [yfB, cstB], [pb])
                        if half == 0:
                            op("act", lambda e, pb=pb, s=s: e.copy(out=YS[s][:, 0:512], in_=pb.ap[:]), [pb], [ysB[s]])
                        else:
                            op("dve", lambda e, pb=pb, s=s: e.tensor_copy(out=YS[s][:, 512:1024], in_=pb.ap[:]),
                               [pb], [ysB[s]])
                    dma("sp", y_d[tt * 128:(tt + 1) * 128, :], YS[s][:], reads=[ysB[s]])
            k.barrier()
    return nc


def _const_table(cvec, inp, is_sample):
    t = np.zeros((128, NCST), np.float32)

    def put(name, arr):
        arr = np.asarray(arr, np.float32)
        t[:, _CO[name]:_CO[name] + arr.shape[1]] = arr

    def pp(v):
        v = np.asarray(v, np.float32)
        return v.reshape(-1, 128).T

    put("cv", pp(cvec))
    put("adab", np.concatenate([pp(inp["ada_b"][i]) for i in range(4)], axis=1))
    put("gmix", np.concatenate([pp(inp["norm_mix_g"][i]) for i in range(4)], axis=1))
    put("gffn", np.concatenate([pp(inp["norm_ffn_g"][i]) for i in range(4)], axis=1))
    put("gfin", pp(inp["final_norm_g"]))
    put("qng", np.concatenate([pp(inp["mla_q_norm_g"][j]) for j in range(2)], axis=1))
    put("kvg", np.concatenate([pp(inp["mla_kv_norm_g"][j]) for j in range(2)], axis=1))
    put("convw", np.concatenate([pp(inp["conv_w"][0][kk]) for kk in range(3)], axis=1))
    put("lr", np.broadcast_to(np.asarray(inp["ret_log_rate"][0], np.float32).reshape(1, 8), (128, 8)))
    p = np.arange(128, dtype=np.float32)
    coef = np.stack([p + 1, 128 - p, -(p + 1), -(128 - p), -(127 - p), -p, np.full(128, -128.0, np.float32),
                     np.zeros(128, np.float32)], axis=1)
    put("coef", coef)
    if is_sample:
        cmf = np.ones(16, np.float32)
        cmb = np.ones(16, np.float32)
    else:
        cmf = np.array([0.0 if n % 2 == 0 else 1.0 for n in range(16)], np.float32)
        cmb = np.array([0.0 if n % 2 == 1 else 1.0 for n in range(16)], np.float32)
    put("cmf", np.broadcast_to(cmf.reshape(1, 16), (128, 16)))
    put("cmb", np.broadcast_to(cmb.reshape(1, 16), (128, 16)))
    bt = np.zeros((20, 8), np.float32)
    if not is_sample:
        bt[:] = NEG
        for kt in range(4, 20):
            sk = (kt - 4) // 2
            bt[kt, sk] = 0.0
    put("btab", np.broadcast_to(bt.reshape(1, 160), (128, 160)))
    put("identf", np.eye(128, dtype=np.float32))
    put("ones", np.ones((128, 128), np.float32))
    jj = np.arange(128)[:, None]
    ii = np.arange(128)[None, :]
    put("maskf", (jj <= ii).astype(np.float32))
    put("maskb", (jj >= ii).astype(np.float32))
    return t


def _rope_tables(is_sample):
    r = np.zeros((2, 64, NT), np.float32)
    if not is_sample:
        r[0] = 1.0
        return r
    tok = np.arange(NT)
    row = (tok // 64).astype(np.float32)
    col = (tok % 64).astype(np.float32)
    inv = (np.float32(10000.0) ** (-np.arange(16, dtype=np.float32) / np.float32(16))).astype(np.float32)
    for ax, pos in enumerate((row, col)):
        ang = (pos[None, :] * inv[:, None]).astype(np.float32)
        c, s = np.cos(ang), np.sin(ang)
        for hf in range(2):
            p0 = ax * 32 + hf * 16
            r[0, p0:p0 + 16] = c
            r[1, p0:p0 + 16] = -s if hf == 0 else s
    return r


def _mask_factors(is_sample):
    big = 29952.0
    m = np.zeros((2, 16, 2560), np.float32)
    m[0, 15, :] = 1.0
    if not is_sample:
        m[0, 0, :] = -big
        m[1, 0, :2048] = 1.0
        for sq in range(8):
            m[0, 1 + sq, 512 + sq * 256:512 + (sq + 1) * 256] = big
            m[1, 1 + sq, sq * 256:(sq + 1) * 256] = 1.0
    return m


def _conv_masks(is_sample):
    seq = NT if is_sample else 256
    t = np.arange(NT)
    mp = (t % seq != 0).astype(np.float32)
    mn = (t % seq != seq - 1).astype(np.float32)
    return np.ascontiguousarray(np.broadcast_to(np.stack([mp, mn])[:, None, :], (2, 128, NT))).astype(np.float32)


_NC_CACHE = {}
_LAYERS = (0, 1, 2, 3)


def kernel(**inp):
    inp = {k_: np.asarray(v) for k_, v in inp.items()}
    if _LAYERS not in _NC_CACHE:
        _NC_CACHE[_LAYERS] = build(_LAYERS)
    nc = _NC_CACHE[_LAYERS]
    wnames = ["ada_w", "mla_w_a", "mla_w_q_b", "mla_w_kv_b", "mla_w_o", "conv_w_in", "conv_w_out", "ret_w_in",
              "ret_w_out", "ffn_w_in", "ffn_w_out"]
    shared = {n: np.ascontiguousarray(inp[n], dtype=np.float32) for n in wnames}
    gng = np.ascontiguousarray(np.broadcast_to(inp["ret_gn_g"][0].reshape(1, 2048), (128, 2048))).astype(np.float32)
    in_maps = []
    for core in range(8):
        is_s = core < 4
        m = dict(shared)
        if is_s:
            b = core
            m["x"] = np.ascontiguousarray(inp["x_sample"][b])
            cvec = inp["c"][b]
            m["st0"] = np.ascontiguousarray(inp["state_ret"][b, 0])
            m["cckv"] = np.ascontiguousarray(inp["cache_mla_ckv"][b].transpose(0, 2, 1))
            m["ckpe"] = np.ascontiguousarray(inp["cache_mla_kpe"][b].transpose(0, 2, 1))
        else:
            p = core - 4
            m["x"] = np.ascontiguousarray(inp["x_prompt"][8 * p:8 * p + 8].reshape(NT, D))
            cvec = inp["c_ctx"]
            m["st0"] = np.ascontiguousarray(inp["state_ret"][p, 0])
            m["cckv"] = np.ascontiguousarray(inp["cache_mla_ckv"][p].transpose(0, 2, 1))
            m["ckpe"] = np.ascontiguousarray(inp["cache_mla_kpe"][p].transpose(0, 2, 1))
        m["cst"] = _const_table(cvec, inp, is_s)
        m["rope"] = _rope_tables(is_s)
        m["cmask"] = _conv_masks(is_s)
        m["mk"] = _mask_factors(is_s)
        m["gng"] = gng
        in_maps.append(m)
    res = run_bass_kernel_spmd(nc, in_maps, core_ids=list(range(8)))
    R = res.results
    y_sample = np.stack([R[b]["y"] for b in range(4)], axis=0).astype(np.float32)
    y_prompt = np.concatenate([R[4 + p]["y"].reshape(8, 256, D) for p in range(4)], axis=0).astype(np.float32)
    ckv = np.concatenate([R[4 + p]["ockv"].reshape(2, 8, 256, 256).transpose(1, 0, 2, 3) for p in range(4)], axis=0)
    kpe = np.concatenate([R[4 + p]["okpe"].reshape(2, 8, 256, 64).transpose(1, 0, 2, 3) for p in range(4)], axis=0)
    st = np.concatenate([R[4 + p]["ost"].reshape(8, 1, 2, 4, 256, 512) for p in range(4)], axis=0)
    return (y_prompt, y_sample, np.ascontiguousarray(ckv, dtype=np.float32),
            np.ascontiguousarray(kpe, dtype=np.float32), np.ascontiguousarray(st, dtype=np.float32))
```

```python
import numpy as np
from contextlib import ExitStack
import concourse.bass as bass
import concourse.mybir as mybir
from concourse.bass_utils import run_bass_kernel_spmd

F32 = mybir.dt.float32
BF16 = mybir.dt.bfloat16
AF = mybir.ActivationFunctionType
ALU = mybir.AluOpType

D = 1024
NT = 2048
TB = 4
FFN_H = 2816
EPS = 1e-6
MLA_SCALE = 192 ** -0.5
NEG = -30000.0

_CO = {}
_n = 0
for _name, _w in [("cv", 8), ("adab", 192), ("gmix", 32), ("gffn", 32), ("gfin", 8), ("qng", 6),
                  ("kvg", 4), ("convw", 24), ("lr", 8), ("coef", 8), ("cmf", 16), ("cmb", 16),
                  ("btab", 160), ("identf", 128), ("ones", 128), ("maskf", 128), ("maskb", 128)]:
    _CO[_name] = _n
    _n += _w
NCST = _n


class DS:
    def __init__(self, sem, key):
        self.sem = sem
        self.cnt = 0
        self.key = key


class Buf:
    __slots__ = ("ap", "w", "r", "ds")

    def __init__(self, ap):
        self.ap = ap
        self.w = None
        self.r = {}
        self.ds = None


class Eng:
    def __init__(self, eng, sem):
        self.eng = eng
        self.sem = sem
        self.cnt = 0
        self.seen = {}


class K:
    def __init__(self, nc, es):
        self.nc = nc
        self.E = {}
        for name, eng in [("pe", nc.tensor), ("act", nc.scalar), ("dve", nc.vector),
                          ("pool", nc.gpsimd), ("sp", nc.sync)]:
            self.E[name] = Eng(eng, es.enter_context(nc.semaphore("s_" + name)))
        self.free_ds = {"sp": [], "pool": []}
        self.all_ds = {}
        for i in range(64):
            d = DS(es.enter_context(nc.semaphore("d%d" % i)), ("d", i))
            d.q = "sp" if i < 24 else "pool"
            self.free_ds[d.q].append(d)
            self.all_ds[d.key] = d
        self.dirty = set()

    def _sem_of(self, key):
        if isinstance(key, tuple):
            d = self.all_ds[key]
            return d.sem, 16
        return self.E[key].sem, 1

    def _waits(self, en, reads, writes):
        e = self.E[en]
        need = {}
        for b in reads:
            if b.w is not None and b.w[1] > need.get(b.w[0], 0):
                need[b.w[0]] = b.w[1]
        for b in writes:
            if b.w is not None and b.w[1] > need.get(b.w[0], 0):
                need[b.w[0]] = b.w[1]
            for k, v in b.r.items():
                if k == en and en == "pe":
                    continue
                if v > need.get(k, 0):
                    need[k] = v
        for key, val in need.items():
            if key == en and en == "pe":
                continue
            if e.seen.get(key, 0) >= val:
                continue
            sem, mult = self._sem_of(key)
            e.eng.wait_ge(sem, val * mult)
            e.seen[key] = val

    def op(self, en, fn, reads=(), writes=()):
        e = self.E[en]
        self._waits(en, reads, writes)
        inst = fn(e.eng)
        e.cnt += 1
        inst.then_inc(e.sem, 1)
        for b in writes:
            b.w = (en, e.cnt)
            b.r = {}
        for b in reads:
            b.r[en] = e.cnt

    def dma(self, q, out, in_, reads=(), writes=(), **kw):
        owner = writes[0] if writes else reads[0]
        if owner.ds is None:
            owner.ds = self.free_ds[q].pop()
        ds = owner.ds
        assert ds.q == q
        e = self.E[q]
        self._waits(q, reads, writes)
        if q == "pool":
            kw.setdefault("max_dma_last_dim", 8192)
        inst = e.eng.dma_start(out=out, in_=in_, **kw)
        ds.cnt += 1
        inst.then_inc(ds.sem, 16)
        self.dirty.add(ds.key)
        for b in writes:
            b.w = (ds.key, ds.cnt)
            b.r = {}
        for b in reads:
            b.r[ds.key] = ds.cnt

    def barrier(self):
        keys = [(k, self.E[k].cnt) for k in ("pe", "act", "dve", "pool") if self.E[k].cnt > 0]
        keys += [(k, self.all_ds[k].cnt) for k in sorted(self.dirty)]
        self.dirty = set()
        for en, e in self.E.items():
            for key, val in keys:
                if key == en:
                    continue
                if e.seen.get(key, 0) >= val:
                    continue
                sem, mult = self._sem_of(key)
                e.eng.wait_ge(sem, val * mult)
                e.seen[key] = val

    def release(self, bufs):
        for b in bufs:
            if b.ds is not None:
                self.free_ds[b.ds.q].append(b.ds)
                b.ds = None


_DBG = set()
RSUM_ENG = ("dve", "pool")


def build(layers=(0, 1, 2, 3)):
    nc = bass.Bass("TRN2", target_bir_lowering=False)

    def din(name, shape):
        return nc.dram_tensor(name, list(shape), F32, kind="ExternalInput").ap()

    def dout(name, shape):
        return nc.dram_tensor(name, list(shape), F32, kind="ExternalOutput").ap()

    x_d = din("x", [NT, D])
    cst_d = din("cst", [128, NCST])
    rope_d = din("rope", [2, 64, NT])
    cmask_d = din("cmask", [2, 128, NT])
    gng_d = din("gng", [128, 2048])
    mk_d = din("mk", [2, 16, 2560])
    st0_d = din("st0", [2, 4, 256, 512])
    cckv_d = din("cckv", [2, 256, 512])
    ckpe_d = din("ckpe", [2, 64, 512])
    ada_w = din("ada_w", [4, D, 6 * D])
    mla_w_a = din("mla_w_a", [2, D, 704])
    mla_w_q_b = din("mla_w_q_b", [2, 384, 1536])
    mla_w_kv_b = din("mla_w_kv_b", [2, 256, 2048])
    mla_w_o = din("mla_w_o", [2, 1024, D])
    conv_w_in = din("conv_w_in", [1, D, 3 * D])
    conv_w_out = din("conv_w_out", [1, D, D])
    ret_w_in = din("ret_w_in", [1, D, 6144])
    ret_w_out = din("ret_w_out", [1, 2048, D])
    ffn_w_in = din("ffn_w_in", [4, D, 2 * FFN_H])
    ffn_w_out = din("ffn_w_out", [4, FFN_H, D])

    y_d = dout("y", [NT, D])
    ockv_d = dout("ockv", [2, NT, 256])
    okpe_d = dout("okpe", [2, NT, 64])
    ost_d = dout("ost", [8, 2, 4, 256, 512])

    es = ExitStack()
    with es:
        k = K(nc, es)
        op, dma = k.op, k.dma

        uid = [0]

        def sb(scope, name, shape, dt=F32):
            uid[0] += 1
            return scope.enter_context(nc.sbuf_tensor("%s_%d" % (name, uid[0]), list(shape), dt))

        X = sb(es, "X", [128, 8, NT])
        Xb = [Buf(X[:, :, t * 512:(t + 1) * 512]) for t in range(TB)]
        CST = sb(es, "CST", [128, NCST])
        cstB = Buf(CST)
        CB = sb(es, "CB", [128, 4, 128], BF16)
        cbB = Buf(CB)
        MODS = sb(es, "MODS", [128, 4, 48])
        G12 = sb(es, "G12", [128, 4, 16])
        modB = Buf(MODS)
        SCV = sb(es, "SCV", [128, 8], BF16)
        scvB = Buf(SCV)
        DEC = sb(es, "DEC", [128, 4, 8])
        decB = Buf(DEC)
        PSt = [es.enter_context(nc.psum_tensor("ps%d" % i, [128, 512], F32)) for i in range(8)]
        PS = [Buf(t) for t in PSt]
        rr = [0]

        def ps_next(banks=range(8)):
            banks = list(banks)
            b = banks[rr[0] % len(banks)]
            rr[0] += 1
            return PS[b]

        def c_(name, a=0, b=None):
            o = _CO[name]
            if b is None:
                b = a + 1
            return CST[:, o + a:o + b]

        identf = c_("identf", 0, 128)
        identb = CB[:, 0, :]
        onesb = CB[:, 1, :]
        maskf = CB[:, 2, :]
        maskb = CB[:, 3, :]

        def ada_dma(i, b, AW, awB):
            src = ada_w[i].rearrange("(c p) n -> p c n", p=128)
            dma("pool", AW[b % 2][:], src[:, :, b * 384:(b + 1) * 384], writes=[awB[b % 2]])

        def ada_pe(b, pm, AW, awB):
            s_ = b % 2
            for n in range(3):
                col = b * 3 + n
                for kc in range(8):
                    op("pe", lambda e: e.matmul(pm.ap[:, col:col + 1], lhsT=AW[s_][:, kc, n * 128:(n + 1) * 128],
                                                rhs=SCV[:, kc:kc + 1], start=(kc == 0), stop=(kc == 7)),
                       [awB[s_], scvB], [pm])

        def ada_finish(i, pm):
            op("dve", lambda e: e.tensor_tensor(out=MODS[:, i, :], in0=pm.ap[:, 0:48],
                                                in1=c_("adab", i * 48, i * 48 + 48), op=ALU.add), [pm, cstB], [modB])
            op("dve", lambda e: e.scalar_tensor_tensor(out=G12[:, i, 0:8], in0=MODS[:, i, 8:16], scalar=1.0,
                                                       in1=c_("gmix", i * 8, i * 8 + 8), op0=ALU.add, op1=ALU.mult),
               [modB, cstB], [modB])
            op("dve", lambda e: e.scalar_tensor_tensor(out=G12[:, i, 8:16], in0=MODS[:, i, 32:40], scalar=1.0,
                                                       in1=c_("gffn", i * 8, i * 8 + 8), op0=ALU.add, op1=ALU.mult),
               [modB, cstB], [modB])

        dma("sp", CST[:], cst_d[:, :], writes=[cstB])
        op("dve", lambda e: e.tensor_copy(out=CB[:, 0, :], in_=c_("identf", 0, 128)), [cstB], [cbB])
        op("dve", lambda e: e.tensor_copy(out=CB[:, 1, :], in_=c_("ones", 0, 128)), [cstB], [cbB])
        op("dve", lambda e: e.tensor_copy(out=CB[:, 2, :], in_=c_("maskf", 0, 128)), [cstB], [cbB])
        op("dve", lambda e: e.tensor_copy(out=CB[:, 3, :], in_=c_("maskb", 0, 128)), [cstB], [cbB])
        op("act", lambda e: e.activation(out=SCV[:], in_=c_("cv", 0, 8), func=AF.Silu), [cstB], [scvB])
        with ExitStack() as ph:
            ELR = sb(ph, "ELR", [128, 8])
            elrB = Buf(ELR)
            op("act", lambda e: e.activation(out=ELR[:], in_=c_("lr", 0, 8), func=AF.Exp), [cstB], [elrB])
            for t, (cf, cb_) in enumerate([(0, 1), (2, 3), (4, 5), (6, 6)]):
                op("act", lambda e, t=t, cf=cf: e.activation(out=DEC[:, t, 0:4], in_=ELR[:, 0:4], func=AF.Exp,
                                                            scale=c_("coef", cf)), [elrB, cstB], [decB])
                op("act", lambda e, t=t, cb_=cb_: e.activation(out=DEC[:, t, 4:8], in_=ELR[:, 4:8], func=AF.Exp,
                                                              scale=c_("coef", cb_)), [elrB, cstB], [decB])
            op("dve", lambda e: e.tensor_scalar(out=DEC[:, 0, :], in0=DEC[:, 0, :], scalar1=0.0625, scalar2=None,
                                                op0=ALU.mult), [decB], [decB])
            op("dve", lambda e: e.tensor_scalar(out=DEC[:, 2, :], in0=DEC[:, 2, :], scalar1=0.0625, scalar2=None,
                                                op0=ALU.mult), [decB], [decB])
            XS = [sb(ph, "XS%d" % i, [128, D]) for i in range(3)]
            xsB = [Buf(t) for t in XS]
            for tt in range(16):
                s = tt % 3
                dma("sp", XS[s][:], x_d[tt * 128:(tt + 1) * 128, :], writes=[xsB[s]])
                for half in range(2):
                    pb = ps_next()
                    for q in range(4):
                        fc = half * 4 + q
                        op("pe", lambda e, pb=pb, q=q, fc=fc, s=s: e.transpose(
                            out=pb.ap[:, q * 128:(q + 1) * 128], in_=XS[s][:, fc * 128:(fc + 1) * 128],
                            identity=identf), [xsB[s], cstB], [pb])
                    eng = "act" if half == 0 else "dve"
                    if eng == "act":
                        op("act", lambda e, pb=pb, half=half, tt=tt: e.copy(
                            out=X[:, half * 4:half * 4 + 4, tt * 128:(tt + 1) * 128],
                            in_=pb.ap.rearrange("p (q t) -> p q t", q=4)), [pb], [Xb[tt // 4]])
                    else:
                        op("dve", lambda e, pb=pb, half=half, tt=tt: e.tensor_copy(
                            out=X[:, half * 4:half * 4 + 4, tt * 128:(tt + 1) * 128],
                            in_=pb.ap.rearrange("p (q t) -> p q t", q=4)), [pb], [Xb[tt // 4]])
            AW = [sb(ph, "AW%d" % i, [128, 8, 1536], BF16) for i in range(4)]
            awB = [Buf(t) for t in AW]
            if layers:
                pm = PS[7]
                src0 = ada_w[layers[0]].rearrange("(c p) n -> p c n", p=128)
                for bl in range(4):
                    for hf in range(2):
                        dma("pool", AW[bl][:, hf * 4:hf * 4 + 4, :],
                            src0[:, hf * 4:hf * 4 + 4, bl * 1536:(bl + 1) * 1536], writes=[awB[bl]])
                for bl in range(4):
                    for n in range(12):
                        col = bl * 12 + n
                        for kc in range(8):
                            op("pe", lambda e: e.matmul(pm.ap[:, col:col + 1], lhsT=AW[bl][:, kc, n * 128:(n + 1) * 128],
                                                        rhs=SCV[:, kc:kc + 1], start=(kc == 0), stop=(kc == 7)),
                               [awB[bl], scvB], [pm])
                ada_finish(layers[0], pm)
            k.barrier()
            k.release(xsB + awB + [elrB])

        def rms_rstd(scope_bufs, srcs, nchunk, out_rstd, out_buf, src_reads, inv_n):
            SQ, sqB = scope_bufs
            pss = ps_next()
            for c in range(nchunk):
                s = c % 2
                op("act", lambda e, c=c, s=s: e.activation(out=SQ[s][:], in_=srcs[c], func=AF.Square),
                   src_reads, [sqB[s]])
                op("pe", lambda e, c=c, s=s, pss=pss: e.matmul(pss.ap[:], lhsT=onesb, rhs=SQ[s][:],
                                                              start=(c == 0), stop=(c == nchunk - 1)),
                   [sqB[s], cbB], [pss])
            op("dve", lambda e, pss=pss: e.tensor_scalar(out=out_rstd, in0=pss.ap[:], scalar1=inv_n, scalar2=EPS,
                                                         op0=ALU.mult, op1=ALU.add), [pss], [out_buf])
            op("act", lambda e: e.activation(out=out_rstd, in_=out_rstd, func=AF.Sqrt), [out_buf], [out_buf])
            op("dve", lambda e: e.reciprocal(out=out_rstd, in_=out_rstd), [out_buf], [out_buf])

        mSQ = [sb(es, "mSQ%d" % s_, [128, 512], BF16) for s_ in range(2)]
        msqB = [Buf(t) for t in mSQ]
        mRS = [sb(es, "mRS%d" % s_, [128, 512]) for s_ in range(3)]
        mrsB = [Buf(t) for t in mRS]
        mTM = [sb(es, "mTM%d" % s_, [128, 512]) for s_ in range(2)]
        mtmB = [Buf(t) for t in mTM]
        mcnt = [0]

        def modulate(ph, i, which, H, Hb):
            goff = 0 if which == 0 else 8
            shoff = 0 if which == 0 else 24
            pss = {}

            def st1a(tb):
                r = tb % 3
                pss[tb] = ps_next()
                for c in range(8):
                    s_ = c % 2
                    op("act", lambda e: e.activation(out=mSQ[s_][:], in_=X[:, c, tb * 512:(tb + 1) * 512],
                                                     func=AF.Square), [Xb[tb]], [msqB[s_]])
                    op("pe", lambda e: e.matmul(pss[tb].ap[:], lhsT=onesb, rhs=mSQ[s_][:], start=(c == 0),
                                                stop=(c == 7)), [msqB[s_], cbB], [pss[tb]])
                op("dve", lambda e: e.tensor_scalar(out=mRS[r][:], in0=pss[tb].ap[:], scalar1=1.0 / D, scalar2=EPS,
                                                    op0=ALU.mult, op1=ALU.add), [pss[tb]], [mrsB[r]])

            def st1b(tb):
                r = tb % 3
                op("act", lambda e: e.activation(out=mRS[r][:], in_=mRS[r][:], func=AF.Sqrt), [mrsB[r]], [mrsB[r]])
                op("dve", lambda e: e.reciprocal(out=mRS[r][:], in_=mRS[r][:]), [mrsB[r]], [mrsB[r]])

            def st2(tb):
                r = tb % 3
                for fc in range(8):
                    s_ = mcnt[0] % 2
                    mcnt[0] += 1
                    op("dve", lambda e: e.scalar_tensor_tensor(
                        out=mTM[s_][:], in0=X[:, fc, tb * 512:(tb + 1) * 512], scalar=G12[:, i, goff + fc:goff + fc + 1],
                        in1=mRS[r][:], op0=ALU.mult, op1=ALU.mult), [Xb[tb], modB, mrsB[r]], [mtmB[s_]])
                    op("act", lambda e: e.activation(
                        out=H[:, fc, tb * 512:(tb + 1) * 512], in_=mTM[s_][:], func=AF.Identity,
                        bias=MODS[:, i, shoff + fc:shoff + fc + 1], scale=1.0), [mtmB[s_], modB], [Hb[tb]])

            st1a(0)
            st1a(1)
            st1b(0)
            for tb in range(TB):
                if tb + 2 < TB:
                    st1a(tb + 2)
                if tb + 1 < TB:
                    st1b(tb + 1)
                st2(tb)
            return []

        def resid_add(pb, f, tb, i, goff):
            op("dve", lambda e: e.scalar_tensor_tensor(
                out=X[:, f, tb * 512:(tb + 1) * 512], in0=pb.ap[:], scalar=MODS[:, i, goff + f:goff + f + 1],
                in1=X[:, f, tb * 512:(tb + 1) * 512], op0=ALU.mult, op1=ALU.add), [pb, modB, Xb[tb]], [Xb[tb]])

        def ffn(i, inext=None, is_last=False):
            with ExitStack() as ph:
                H = sb(ph, "fH", [128, 8, NT], BF16)
                Hb = [Buf(H[:, :, t * 512:(t + 1) * 512]) for t in range(TB)]
                tmp = []
                if inext is not None:
                    AW = [sb(ph, "fAW%d" % s, [128, 8, 384], BF16) for s in range(2)]
                    awB = [Buf(t) for t in AW]
                else:
                    AW, awB = [], []
                pm = PS[7]
                if is_last:
                    YF = [sb(ph, "oYF%d" % s, [128, 8, 128]) for s in range(2)]
                    yfB = [Buf(t) for t in YF]
                    YS = [sb(ph, "oYS%d" % s, [128, D]) for s in range(2)]
                    ysB = [Buf(t) for t in YS]
                    tmp = yfB + ysB

                def final_tb(tb):
                    r = tb % 3
                    pss = ps_next(range(7))
                    for c in range(8):
                        s_ = c % 2
                        op("act", lambda e: e.activation(out=mSQ[s_][:], in_=X[:, c, tb * 512:(tb + 1) * 512],
                                                         func=AF.Square), [Xb[tb]], [msqB[s_]])
                        op("pe", lambda e: e.matmul(pss.ap[:], lhsT=onesb, rhs=mSQ[s_][:], start=(c == 0),
                                                    stop=(c == 7)), [msqB[s_], cbB], [pss])
                    op("dve", lambda e: e.tensor_scalar(out=mRS[r][:], in0=pss.ap[:], scalar1=1.0 / D, scalar2=EPS,
                                                        op0=ALU.mult, op1=ALU.add), [pss], [mrsB[r]])
                    op("act", lambda e: e.activation(out=mRS[r][:], in_=mRS[r][:], func=AF.Sqrt), [mrsB[r]], [mrsB[r]])
                    op("dve", lambda e: e.reciprocal(out=mRS[r][:], in_=mRS[r][:]), [mrsB[r]], [mrsB[r]])
                    for q in range(4):
                        tt = tb * 4 + q
                        y_ = tt % 2
                        tok = slice(tt * 128, (tt + 1) * 128)
                        for fc in range(8):
                            op("dve", lambda e: e.scalar_tensor_tensor(
                                out=YF[y_][:, fc, :], in0=X[:, fc, tok], scalar=c_("gfin", fc),
                                in1=mRS[r][:, q * 128:(q + 1) * 128], op0=ALU.mult, op1=ALU.mult),
                               [Xb[tb], cstB, mrsB[r]], [yfB[y_]])
                        for half in range(2):
                            pb = ps_next(range(7))
                            for rr_ in range(4):
                                fc = half * 4 + rr_
                                op("pe", lambda e: e.transpose(out=pb.ap[:, rr_ * 128:(rr_ + 1) * 128],
                                                               in_=YF[y_][:, fc, :], identity=identf),
                                   [yfB[y_], cstB], [pb])
                            if half == 0:
                                op("act", lambda e: e.copy(out=YS[y_][:, 0:512], in_=pb.ap[:]), [pb], [ysB[y_]])
                            else:
                                op("dve", lambda e: e.tensor_copy(out=YS[y_][:, 512:1024], in_=pb.ap[:]),
                                   [pb], [ysB[y_]])
                        dma("sp", y_d[tok, :], YS[y_][:], reads=[ysB[y_]])
                ACTB = sb(ph, "fACT", [128, 4, NT], BF16)
                actB = [Buf(ACTB[:, :, t * 512:(t + 1) * 512]) for t in range(TB)]
                WI = [sb(ph, "fWI%d" % s, [128, 8, 2, 512], BF16) for s in range(2)]
                wiB = [Buf(t) for t in WI]
                WO = [sb(ph, "fWO%d" % s, [128, 4, D], BF16) for s in range(2)]
                woB = [Buf(t) for t in WO]
                SA = [sb(ph, "fSA%d" % s, [128, 512]) for s in range(2)]
                saB = [Buf(t) for t in SA]
                win = ffn_w_in[i].rearrange("(c p) n -> p c n", p=128)
                wout = ffn_w_out[i].rearrange("(j p) n -> p j n", p=128)
                groups = [(0, 4), (4, 4), (8, 4), (12, 4), (16, 4), (20, 2)]
                nsa = 0
                def load_w(g):
                    j0_, nj_ = groups[g]
                    s_ = g % 2
                    for ab in range(2):
                        c0 = ab * FFN_H + j0_ * 128
                        dma("pool", WI[s_][:, :, ab, 0:nj_ * 128], win[:, :, c0:c0 + nj_ * 128], writes=[wiB[s_]])
                    dma("pool", WO[s_][:, 0:nj_, :], wout[:, j0_:j0_ + nj_, :], writes=[woB[s_]])

                load_w(0)
                if inext is not None:
                    for b in range(0, 2):
                        ada_dma(inext, b, AW, awB)
                modulate(ph, i, 1, H, Hb)
                for g, (j0, nj) in enumerate(groups):
                    s = g % 2
                    if inext is not None and 0 < g < 4:
                        for b in range(4 * g, 4 * g + 2):
                            ada_dma(inext, b, AW, awB)
                    for tb in range(TB):
                        for j in range(nj):
                            pa = ps_next(range(7))
                            pbb = ps_next(range(7))
                            for ab, pp in ((0, pa), (1, pbb)):
                                for kc in range(8):
                                    op("pe", lambda e, pp=pp, kc=kc, ab=ab, j=j, tb=tb, s=s: e.matmul(
                                        pp.ap[:], lhsT=WI[s][:, kc, ab, j * 128:(j + 1) * 128],
                                        rhs=H[:, kc, tb * 512:(tb + 1) * 512], start=(kc == 0), stop=(kc == 7)),
                                       [wiB[s], Hb[tb]], [pp])
                            q = nsa % 2
                            nsa += 1
                            op("act", lambda e, pa=pa, q=q: e.activation(out=SA[q][:], in_=pa.ap[:], func=AF.Silu),
                               [pa], [saB[q]])
                            op("dve", lambda e, pbb=pbb, q=q, j=j, tb=tb: e.tensor_tensor(
                                out=ACTB[:, j, tb * 512:(tb + 1) * 512], in0=SA[q][:], in1=pbb.ap[:], op=ALU.mult),
                               [saB[q], pbb], [actB[tb]])
                    if g + 1 < len(groups):
                        load_w(g + 1)
                    if inext is not None and g < 4:
                        for b in range(4 * g, 4 * g + 2):
                            ada_pe(b, pm, AW, awB)
                        for b in range(4 * g + 2, 4 * g + 4):
                            ada_dma(inext, b, AW, awB)
                    def pass2(tb):
                        for f in range(8):
                            po = ps_next(range(7))
                            for j in range(nj):
                                op("pe", lambda e: e.matmul(
                                    po.ap[:], lhsT=WO[s][:, j, f * 128:(f + 1) * 128],
                                    rhs=ACTB[:, j, tb * 512:(tb + 1) * 512], start=(j == 0), stop=(j == nj - 1)),
                                   [woB[s], actB[tb]], [po])
                            resid_add(po, f, tb, i, 40)

                    if is_last and g == len(groups) - 1:
                        pass2(0)
                        pass2(1)
                        final_tb(0)
                        pass2(2)
                        final_tb(1)
                        pass2(3)
                        final_tb(2)
                        final_tb(3)
                    else:
                        for tb in range(TB):
                            pass2(tb)
                    if inext is not None and g < 4:
                        for b in range(4 * g + 2, 4 * g + 4):
                            ada_pe(b, pm, AW, awB)
                        if g == 3:
                            ada_finish(inext, pm)
                k.barrier()
                k.release(Hb + tmp + actB + wiB + woB + saB + awB)

        def conv_layer(i):
            with ExitStack() as ph:
                H = sb(ph, "cH", [128, 8, NT], BF16)
                Hb = [Buf(H[:, :, t * 512:(t + 1) * 512]) for t in range(TB)]
                ZB = sb(ph, "cZB", [128, 8, NT], BF16)
                zbB = [Buf(ZB[:, :, t * 512:(t + 1) * 512]) for t in range(TB)]
                rel = []
                with ExitStack() as ph2:
                    WCI = [sb(ph2, "cWI%d" % s, [128, 8, 3, 128], BF16) for s in range(2)]
                    wciB = [Buf(t) for t in WCI]
                    CS = sb(ph2, "cCS", [128, NT])
                    csB = Buf(CS)
                    Z = sb(ph2, "cZ", [128, NT + 2])
                    zB = Buf(Z)
                    TMP = sb(ph2, "cTMP", [128, NT])
                    tB = Buf(TMP)
                    MK = sb(ph2, "cMK", [128, 2, NT], BF16)
                    mkB = Buf(MK)
                    for m in range(2):
                        dma("pool", MK[:, m, :], cmask_d[m], writes=[mkB])
                    op("dve", lambda e: e.memset(Z[:, 0:1], 0.0), [], [zB])
                    op("dve", lambda e: e.memset(Z[:, NT + 1:NT + 2], 0.0), [], [zB])
                    cwin = conv_w_in[0].rearrange("(c p) n -> p c n", p=128)
                    for fc in range(8):
                        s = fc % 2
                        for g3 in range(3):
                            dma("pool", WCI[s][:, :, g3, :], cwin[:, :, g3 * D + fc * 128:g3 * D + (fc + 1) * 128],
                                writes=[wciB[s]])
                        if fc == 0:
                            modulate(ph, i, 0, H, Hb)

                        def proj(g3, banks):
                            for tb in range(TB):
                                pp = PS[banks[tb]]
                                for kc in range(8):
                                    op("pe", lambda e, pp=pp, kc=kc, tb=tb: e.matmul(
                                        pp.ap[:], lhsT=WCI[s][:, kc, g3, :], rhs=H[:, kc, tb * 512:(tb + 1) * 512],
                                        start=(kc == 0), stop=(kc == 7)), [wciB[s], Hb[tb]], [pp])
                        bA = [0, 1, 2, 3] if fc % 2 == 0 else [4, 5, 6, 7]
                        bB = [4, 5, 6, 7] if fc % 2 == 0 else [0, 1, 2, 3]
                        proj(1, bA)
                        for tb in range(TB):
                            op("act", lambda e, tb=tb: e.copy(out=CS[:, tb * 512:(tb + 1) * 512], in_=PS[bA[tb]].ap[:]),
                               [PS[bA[tb]]], [csB])
                        proj(2, bB)
                        for tb in range(TB):
                            op("dve", lambda e, tb=tb: e.tensor_tensor(
                                out=Z[:, 1 + tb * 512:1 + (tb + 1) * 512], in0=CS[:, tb * 512:(tb + 1) * 512],
                                in1=PS[bB[tb]].ap[:], op=ALU.mult), [csB, PS[bB[tb]]], [zB])
                        proj(0, bA)
                        w0 = c_("convw", 0 * 8 + fc)
                        w1 = c_("convw", 1 * 8 + fc)
                        w2 = c_("convw", 2 * 8 + fc)
                        op("dve", lambda e: e.tensor_scalar(out=CS[:], in0=Z[:, 1:NT + 1], scalar1=w1, scalar2=None,
                                                            op0=ALU.mult), [zB, cstB], [csB])
                        op("dve", lambda e: e.scalar_tensor_tensor(out=TMP[:], in0=Z[:, 0:NT], scalar=w0,
                                                                   in1=MK[:, 0, :], op0=ALU.mult, op1=ALU.mult),
                           [zB, cstB, mkB], [tB])
                        op("dve", lambda e: e.tensor_tensor(out=CS[:], in0=CS[:], in1=TMP[:], op=ALU.add),
                           [csB, tB], [csB])
                        op("dve", lambda e: e.scalar_tensor_tensor(out=TMP[:], in0=Z[:, 2:NT + 2], scalar=w2,
                                                                   in1=MK[:, 1, :], op0=ALU.mult, op1=ALU.mult),
                           [zB, cstB, mkB], [tB])
                        op("dve", lambda e: e.tensor_tensor(out=CS[:], in0=CS[:], in1=TMP[:], op=ALU.add),
                           [csB, tB], [csB])
                        for tb in range(TB):
                            op("dve", lambda e, tb=tb, fc=fc: e.tensor_tensor(
                                out=ZB[:, fc, tb * 512:(tb + 1) * 512], in0=CS[:, tb * 512:(tb + 1) * 512],
                                in1=PS[bA[tb]].ap[:], op=ALU.mult), [csB, PS[bA[tb]]], [zbB[tb]])
                    k.barrier()
                    k.release(wciB + [csB, zB, tB, mkB])
                with ExitStack() as ph2:
                    WCO = sb(ph2, "cWO", [128, 8, D], BF16)
                    wcoB = Buf(WCO)
                    dma("pool", WCO[:], conv_w_out[0].rearrange("(c p) n -> p c n", p=128), writes=[wcoB])
                    for tb in range(TB):
                        for f in range(8):
                            po = ps_next()
                            for kc in range(8):
                                op("pe", lambda e, po=po, kc=kc, f=f, tb=tb: e.matmul(
                                    po.ap[:], lhsT=WCO[:, kc, f * 128:(f + 1) * 128],
                                    rhs=ZB[:, kc, tb * 512:(tb + 1) * 512], start=(kc == 0), stop=(kc == 7)),
                                   [wcoB, zbB[tb]], [po])
                            resid_add(po, f, tb, i, 16)
                    k.barrier()
                    k.release([wcoB])
                k.release(Hb + zbB)

        def mla_layer(i, j):
            with ExitStack() as ph:
                QAN = sb(ph, "aQAN", [128, 3, NT], BF16)
                qanB = [Buf(QAN[:, :, t * 512:(t + 1) * 512]) for t in range(TB)]
                CKT = sb(ph, "aCKT", [128, 2, 2560], BF16)
                cktB = [Buf(CKT[:, :, t * 512:(t + 1) * 512]) for t in range(5)]
                KPT = sb(ph, "aKPT", [128, 2560], BF16)
                kptB = [Buf(KPT[:, t * 512:(t + 1) * 512]) for t in range(5)]
                for t in range(5):
                    op("dve", lambda e: e.memset(KPT[64:128, t * 512:(t + 1) * 512], 0.0), [], [kptB[t]])
                dma("pool", KPT[64:80, :], mk_d[0], writes=kptB)
                with ExitStack() as ph2:
                    H = sb(ph2, "aH", [128, 8, NT], BF16)
                    Hb = [Buf(H[:, :, t * 512:(t + 1) * 512]) for t in range(TB)]
                    WA = sb(ph2, "aWA", [128, 8, 768], BF16)
                    waB = Buf(WA)
                    wa_src = mla_w_a[j].rearrange("(c p) n -> p c n", p=128)
                    dma("pool", WA[:, :, 0:704], wa_src, writes=[waB])
                    for c in range(2):
                        dma("pool", CKT[:, c, 0:512], cckv_d[j, c * 128:(c + 1) * 128, :], writes=[cktB[0]])
                    dma("pool", KPT[0:64, 0:512], ckpe_d[j], writes=[kptB[0]])
                    tmp = modulate(ph2, i, 0, H, Hb)
                    for ax in range(2):
                        for hf in range(2):
                            d0 = 704 + ax * 32 + hf * 16
                            s0 = 640 + ax * 32 + (1 - hf) * 16
                            op("dve", lambda e: e.tensor_copy(out=WA[:, :, d0:d0 + 16], in_=WA[:, :, s0:s0 + 16]),
                               [waB], [waB])
                    SQ = [sb(ph2, "aSQ%d" % s, [128, 512], BF16) for s in range(2)]
                    sqB = [Buf(t) for t in SQ]
                    QG = sb(ph2, "aQG", [128, 3, 512])
                    qgB = Buf(QG)
                    RS = sb(ph2, "aRS", [128, 512])
                    rsB = Buf(RS)
                    CKF = sb(ph2, "aCKF", [128, 2, 512])
                    ckfB = Buf(CKF)
                    KPF = sb(ph2, "aKPF", [64, 512])
                    kpfB = Buf(KPF)
                    CSN = sb(ph2, "aCSN", [64, 2, 512])
                    csnB = Buf(CSN)
                    T1 = sb(ph2, "aT1", [64, 512])
                    t1B = Buf(T1)
                    T2 = sb(ph2, "aT2", [64, 512])
                    t2B = Buf(T2)
                    OST = [sb(ph2, "aOST%d" % s, [128, 320]) for s in range(2)]
                    ostB = [Buf(t) for t in OST]
                    for tb in range(TB):
                        tsl = slice(tb * 512, (tb + 1) * 512)
                        dma("sp", CSN[:, 0, :], rope_d[0, :, tsl], writes=[csnB])
                        dma("sp", CSN[:, 1, :], rope_d[1, :, tsl], writes=[csnB])
                        pq = [ps_next() for _ in range(3)]
                        for c in range(3):
                            for kc in range(8):
                                op("pe", lambda e, c=c, kc=kc, tsl=tsl: e.matmul(
                                    pq[c].ap[:], lhsT=WA[:, kc, c * 128:(c + 1) * 128], rhs=H[:, kc, tsl],
                                    start=(kc == 0), stop=(kc == 7)), [waB, Hb[tb]], [pq[c]])
                        rms_rstd((SQ, sqB), [pq[c].ap[:] for c in range(3)], 3, RS[:], rsB, pq, 1.0 / 384)
                        for c in range(3):
                            op("dve", lambda e, c=c: e.tensor_scalar(out=QG[:, c, :], in0=pq[c].ap[:],
                                                                     scalar1=c_("qng", j * 3 + c), scalar2=None,
                                                                     op0=ALU.mult), [pq[c], cstB], [qgB])
                        for c in range(3):
                            op("dve", lambda e, c=c, tsl=tsl: e.tensor_tensor(out=QAN[:, c, tsl], in0=QG[:, c, :],
                                                                              in1=RS[:], op=ALU.mult),
                               [qgB, rsB], [qanB[tb]])
                        pc = [ps_next() for _ in range(2)]
                        for c in range(2):
                            for kc in range(8):
                                op("pe", lambda e, c=c, kc=kc, tsl=tsl: e.matmul(
                                    pc[c].ap[:], lhsT=WA[:, kc, 384 + c * 128:384 + (c + 1) * 128], rhs=H[:, kc, tsl],
                                    start=(kc == 0), stop=(kc == 7)), [waB, Hb[tb]], [pc[c]])
                        rms_rstd((SQ, sqB), [pc[c].ap[:] for c in range(2)], 2, RS[:], rsB, pc, 1.0 / 256)
                        for c in range(2):
                            op("dve", lambda e, c=c: e.scalar_tensor_tensor(
                                out=CKF[:, c, :], in0=pc[c].ap[:], scalar=c_("kvg", j * 2 + c), in1=RS[:],
                                op0=ALU.mult, op1=ALU.mult), [pc[c], cstB, rsB], [ckfB])
                        op("act", lambda e, tb=tb: e.copy(out=CKT[:, :, 512 + tb * 512:512 + (tb + 1) * 512],
                                                          in_=CKF[:]), [ckfB], [cktB[1 + tb]])
                        pk = ps_next()
                        pks = ps_next()
                        for pp, c0 in ((pk, 640), (pks, 704)):
                            for kc in range(8):
                                op("pe", lambda e, pp=pp, c0=c0, kc=kc, tsl=tsl: e.matmul(
                                    pp.ap[0:64, :], lhsT=WA[:, kc, c0:c0 + 64], rhs=H[:, kc, tsl],
                                    start=(kc == 0), stop=(kc == 7)), [waB, Hb[tb]], [pp])
                        op("act", lambda e, pk=pk: e.copy(out=KPF[:], in_=pk.ap[0:64, :]), [pk], [kpfB])
                        op("dve", lambda e, pk=pk: e.tensor_tensor(out=T1[:], in0=pk.ap[0:64, :], in1=CSN[:, 0, :],
                                                                   op=ALU.mult), [pk, csnB], [t1B])
                        op("dve", lambda e, pks=pks: e.tensor_tensor(out=T2[:], in0=pks.ap[0:64, :], in1=CSN[:, 1, :],
                                                                     op=ALU.mult), [pks, csnB], [t2B])
                        op("dve", lambda e, tb=tb: e.tensor_tensor(
                            out=KPT[0:64, 512 + tb * 512:512 + (tb + 1) * 512], in0=T1[:], in1=T2[:], op=ALU.add),
                           [t1B, t2B], [kptB[1 + tb]])
                        for q in range(0 if 'noout' in _DBG else 4):
                            tt = tb * 4 + q
                            s = tt % 2
                            pb = ps_next()
                            for c in range(2):
                                op("pe", lambda e, pb=pb, c=c, q=q: e.transpose(
                                    out=pb.ap[:, c * 128:(c + 1) * 128], in_=CKF[:, c, q * 128:(q + 1) * 128],
                                    identity=identf), [ckfB, cstB], [pb])
                            op("pe", lambda e, pb=pb, q=q: e.transpose(
                                out=pb.ap[:, 256:320], in_=KPF[:, q * 128:(q + 1) * 128], identity=identf[0:64, 0:64]),
                               [kpfB, cstB], [pb])
                            op("act", lambda e, pb=pb, s=s: e.copy(out=OST[s][:], in_=pb.ap[:, 0:320]),
                               [pb], [ostB[s]])
                            dma("sp", ockv_d[j, tt * 128:(tt + 1) * 128, :], OST[s][:, 0:256], reads=[ostB[s]])
                            dma("sp", okpe_d[j, tt * 128:(tt + 1) * 128, :], OST[s][:, 256:320], reads=[ostB[s]])
                    k.barrier()
                    k.release(Hb + [waB, qgB, rsB, ckfB, kpfB, csnB, t1B, t2B] + sqB + ostB)
                with ExitStack() as ph2:
                    WQ = sb(ph2, "bWQ", [128, 3, 2048], BF16)
                    wqB = Buf(WQ)
                    wq_src = mla_w_q_b[j].rearrange("(c p) n -> p c n", p=128)
                    dma("pool", WQ[:, :, 0:1536], wq_src, writes=[wqB])
                    for kc in range(3):
                        dst = WQ[:, kc, 1536:2048].rearrange("p (h a f g) -> p h a f g", h=8, a=2, f=2)
                        srcv = WQ[:, kc, 0:1536].rearrange("p (h x) -> p h x", x=192)[:, :, 128:192].rearrange(
                            "p h (a f g) -> p h a f g", a=2, f=2)
                        for hf in range(2):
                            op("dve", lambda e: e.tensor_copy(out=dst[:, :, :, hf, :], in_=srcv[:, :, :, 1 - hf, :]),
                               [wqB], [wqB])
                    WKV = sb(ph2, "bWKV", [128, 2, 2048], BF16)
                    wkvB = Buf(WKV)
                    dma("pool", WKV[:], mla_w_kv_b[j].rearrange("(c p) n -> p c n", p=128), writes=[wkvB])
                    WO = sb(ph2, "bWO", [128, 2, D], BF16)
                    woB = Buf(WO)
                    KN = sb(ph2, "bKN", [128, 2, 2560], BF16)
                    knB = [[Buf(KN[:, hh, kb * 512:(kb + 1) * 512]) for kb in range(5)] for hh in range(2)]
                    V = sb(ph2, "bV", [128, 20, 2, 128], BF16)
                    vB = [Buf(V[:, kt]) for kt in range(20)]
                    OT = sb(ph2, "bOT", [128, 2, NT], BF16)
                    otB = [[Buf(OT[:, hh, t * 512:(t + 1) * 512]) for t in range(TB)] for hh in range(2)]
                    PT = [sb(ph2, "bPT%d" % s, [128, 512], BF16) for s in range(6)]
                    PSB = [sb(ph2, "bPSB%d" % s, [128, 512], BF16) for s in range(2)]
                    psbB = [Buf(t) for t in PSB]
                    ptB = [Buf(t) for t in PT]
                    QN = [sb(ph2, "bQN%d" % s_, [128, 512], BF16) for s_ in range(2)]
                    qnB = [Buf(t) for t in QN]
                    QP = [sb(ph2, "bQP%d" % s_, [128, 512], BF16) for s_ in range(2)]
                    qpB = [Buf(t) for t in QP]
                    for s_ in range(2):
                        op("dve", lambda e: e.memset(QP[s_][64:128, :], 0.0), [], [qpB[s_]])
                    CSN = sb(ph2, "bCSN", [64, 2, 512])
                    csnB = Buf(CSN)
                    T1 = sb(ph2, "bT1", [64, 512])
                    t1B = Buf(T1)
                    T2 = sb(ph2, "bT2", [64, 512])
                    t2B = Buf(T2)
                    RI = sb(ph2, "bRI", [128, 512])
                    riB = Buf(RI)
                    RA = [sb(ph2, "bRA%d" % s_, [128, 512]) for s_ in range(2)]
                    raB = [Buf(t) for t in RA]
                    wo_src = mla_w_o[j].rearrange("(h p) n -> p h n", p=128)
                    wkv3 = WKV[:].rearrange("p c (h x) -> p c h x", x=256)
                    def wo_part():
                        for tb in range(0 if 'a3' in _DBG else TB):
                            for f in range(8):
                                pb = ps_next([6, 7])
                                for hh in range(2):
                                    op("pe", lambda e: e.matmul(
                                        pb.ap[:], lhsT=WO[:, hh, f * 128:(f + 1) * 128],
                                        rhs=OT[:, hh, tb * 512:(tb + 1) * 512], start=(hh == 0), stop=(hh == 1)),
                                       [woB, otB[hh][tb]], [pb])
                                resid_add(pb, f, tb, i, 16)

                    for hp in range(0 if 'noattn' in _DBG else 4):
                        for hh in range(0 if 'a1' in _DBG else 2):
                            h = 2 * hp + hh
                            for kb in range(5):
                                pb = ps_next([6, 7])
                                for c in range(2):
                                    op("pe", lambda e, pb=pb, c=c, h=h, kb=kb: e.matmul(
                                        pb.ap[:], lhsT=WKV[:, c, h * 256:h * 256 + 128],
                                        rhs=CKT[:, c, kb * 512:(kb + 1) * 512], start=(c == 0), stop=(c == 1)),
                                       [wkvB, cktB[kb]], [pb])
                                op("act", lambda e, pb=pb, hh=hh, kb=kb: e.copy(
                                    out=KN[:, hh, kb * 512:(kb + 1) * 512], in_=pb.ap[:]), [pb], [knB[hh][kb]])
                        for kt in range(0 if 'a4' in _DBG else 20):
                            pb = ps_next([6, 7])
                            for c in range(2):
                                op("pe", lambda e, pb=pb, c=c, kt=kt: e.matmul(
                                    pb.ap[:, 0:256].rearrange("p (h v) -> p h v", h=2),
                                    lhsT=CKT[:, c, kt * 128:(kt + 1) * 128],
                                    rhs=wkv3[:, c, 2 * hp:2 * hp + 2, 128:256], start=(c == 0), stop=(c == 1)),
                                   [wkvB, cktB[kt // 4]], [pb])
                            op("dve", lambda e, pb=pb, kt=kt: e.tensor_copy(
                                out=V[:, kt], in_=pb.ap[:, 0:256].rearrange("p (h v) -> p h v", h=2)), [pb], [vB[kt]])
                        if hp > 0:
                            wo_part()
                        dma("pool", WO[:], wo_src[:, 2 * hp:2 * hp + 2, :], writes=[woB])

                        def prep(hq):
                            hh_, qb_ = hq // TB, hq % TB
                            h_ = 2 * hp + hh_
                            sl_ = hq % 2
                            qsl_ = slice(qb_ * 512, (qb_ + 1) * 512)
                            dma("sp", CSN[:, 0, :], rope_d[0, :, qsl_], writes=[csnB])
                            dma("sp", CSN[:, 1, :], rope_d[1, :, qsl_], writes=[csnB])
                            pn = ps_next([6, 7])
                            for c in range(3):
                                op("pe", lambda e: e.matmul(
                                    pn.ap[:], lhsT=WQ[:, c, h_ * 192:h_ * 192 + 128], rhs=QAN[:, c, qsl_],
                                    start=(c == 0), stop=(c == 2)), [wqB, qanB[qb_]], [pn])
                            op("act", lambda e: e.copy(out=QN[sl_][:], in_=pn.ap[:]), [pn], [qnB[sl_]])
                            pr = ps_next([6, 7])
                            for c in range(3):
                                op("pe", lambda e: e.matmul(
                                    pr.ap[0:64, :], lhsT=WQ[:, c, h_ * 192 + 128:h_ * 192 + 192], rhs=QAN[:, c, qsl_],
                                    start=(c == 0), stop=(c == 2)), [wqB, qanB[qb_]], [pr])
                            op("dve", lambda e: e.tensor_tensor(out=T1[:], in0=pr.ap[0:64, :], in1=CSN[:, 0, :],
                                                                op=ALU.mult), [pr, csnB], [t1B])
                            prs = ps_next([6, 7])
                            for c in range(3):
                                op("pe", lambda e: e.matmul(
                                    prs.ap[0:64, :], lhsT=WQ[:, c, 1536 + h_ * 64:1536 + h_ * 64 + 64],
                                    rhs=QAN[:, c, qsl_], start=(c == 0), stop=(c == 2)), [wqB, qanB[qb_]], [prs])
                            op("dve", lambda e: e.tensor_tensor(out=T2[:], in0=prs.ap[0:64, :], in1=CSN[:, 1, :],
                                                                op=ALU.mult), [prs, csnB], [t2B])
                            dma("pool", QP[sl_][64:80, :], mk_d[1, :, qsl_], writes=[qpB[sl_]])
                            op("dve", lambda e: e.tensor_tensor(out=QP[sl_][0:64, :], in0=T1[:], in1=T2[:], op=ALU.add),
                               [t1B, t2B], [qpB[sl_]])

                        if 'a2' not in _DBG:
                            prep(0)
                        for hq in range(0 if 'a2' in _DBG else 2 * TB):
                            hh, qb = hq // TB, hq % TB
                            h = 2 * hp + hh
                            sl = hq % 2
                            qsl = slice(qb * 512, (qb + 1) * 512)
                            if True:
                                po, prw = PS[4], PS[5]

                                def s_tile(kt):
                                    pb = PS[kt % 4]
                                    op("pe", lambda e: e.matmul(pb.ap[:], lhsT=KN[:, hh, kt * 128:(kt + 1) * 128],
                                                                rhs=QN[sl][:], start=True, stop=False),
                                       [knB[hh][kt // 4], qnB[sl]], [pb])
                                    op("pe", lambda e: e.matmul(pb.ap[:], lhsT=KPT[:, kt * 128:(kt + 1) * 128],
                                                                rhs=QP[sl][:], start=False, stop=True),
                                       [kptB[kt // 4], qpB[sl]], [pb])
                                    op("act", lambda e: e.activation(out=PT[kt % 6][:], in_=pb.ap[:], func=AF.Exp,
                                                                     scale=MLA_SCALE), [pb], [ptB[kt % 6]])

                                def pv_tile(kt):
                                    op("pe", lambda e: e.matmul(po.ap[:], lhsT=V[:, kt, hh, :], rhs=PT[kt % 6][:],
                                                                start=(kt == 0), stop=(kt == 19)),
                                       [vB[kt], ptB[kt % 6]], [po])
                                    if kt % 2 == 1:
                                        pj = (kt // 2) % 2
                                        op("dve", lambda e: e.tensor_tensor(out=PSB[pj][:], in0=PT[(kt - 1) % 6][:],
                                                                            in1=PT[kt % 6][:], op=ALU.add),
                                           [ptB[(kt - 1) % 6], ptB[kt % 6]], [psbB[pj]])
                                        op("pe", lambda e: e.matmul(prw.ap[:], lhsT=onesb, rhs=PSB[pj][:],
                                                                    start=(kt == 1), stop=(kt == 19)),
                                           [cbB, psbB[pj]], [prw])
                                s_tile(0)
                                s_tile(1)
                                for kt in range(20):
                                    if kt + 2 < 20:
                                        s_tile(kt + 2)
                                    pv_tile(kt)
                                    if kt == 9 and hq + 1 < 2 * TB:
                                        prep(hq + 1)
                                op("act", lambda e: e.copy(out=RA[0][:], in_=po.ap[:]), [po], [raB[0]])
                                op("act", lambda e: e.copy(out=RA[1][:], in_=prw.ap[:]), [prw], [raB[1]])
                                op("dve", lambda e: e.reciprocal(out=RI[:], in_=RA[1][:]), [raB[1]], [riB])
                                op("dve", lambda e, hh=hh, qsl=qsl: e.tensor_tensor(out=OT[:, hh, qsl], in0=RA[0][:],
                                                                                    in1=RI[:], op=ALU.mult),
                                   [raB[0], riB], [otB[hh][qb]])
                    if 'noattn' not in _DBG:
                        wo_part()
                    k.barrier()
                    k.release([wqB, wkvB, woB, csnB, t1B, t2B, riB] + qnB + qpB + raB + ptB + psbB + vB + sum(knB, []) + sum(otB, []))
                k.release(qanB + cktB + kptB)

        def ret_layer(i):
            with ExitStack() as ph:
                H = sb(ph, "rH", [128, 8, NT], BF16)
                Hb = [Buf(H[:, :, t * 512:(t + 1) * 512]) for t in range(TB)]
                WR = sb(ph, "rWR", [128, 8, 1536], BF16)
                wrB = Buf(WR)
                WRO = sb(ph, "rWRO", [128, 4, D], BF16)
                wroB = Buf(WRO)
                OF = sb(ph, "rOF", [128, 16, 512], BF16)
                ofB = [Buf(OF[:, t]) for t in range(16)]
                U = sb(ph, "rU", [128, 2, 512])
                uB = Buf(U)
                UBs = [sb(ph, "rUB%d" % s_, [128, 2, 512], BF16) for s_ in range(2)]
                ubB = [Buf(t) for t in UBs]
                ubC = [[Buf(t[:, c, :]) for c in range(2)] for t in UBs]
                ubi = [0]
                SO = [sb(ph, "rSO%d" % s, [128, 2, 512]) for s in range(2)]
                soB = [Buf(t) for t in SO]
                soC = [[Buf(t[:, c, :]) for c in range(2)] for t in SO]
                QT = sb(ph, "rQT", [128, 2, 512], BF16)
                qtB = Buf(QT)
                KT = sb(ph, "rKT", [128, 2, 512], BF16)
                ktB = Buf(KT)
                KTM = sb(ph, "rKTM", [128, 4, 256], BF16)
                ktmB = [Buf(KTM[:, q]) for q in range(4)]
                VTM = sb(ph, "rVTM", [128, 4, 512], BF16)
                vtmB = [Buf(VTM[:, q]) for q in range(4)]
                PTM = [sb(ph, "rPTM%d" % s, [128, 128], BF16) for s in range(3)]
                ptmB = [Buf(t) for t in PTM]
                OSM = sb(ph, "rOSM", [128, 512])
                osmB = Buf(OSM)
                SG = sb(ph, "rSG", [128, 512])
                sgB = Buf(SG)
                GN = sb(ph, "rGN", [128, 512])
                gnB = Buf(GN)
                YB = [sb(ph, "rYB%d" % s_, [128, 512], BF16) for s_ in range(3)]
                ybB = [Buf(t) for t in YB]
                YT = sb(ph, "rYT", [128, 4, 512], BF16)
                ytB = Buf(YT)
                ST = sb(ph, "rST", [128, 16])
                stB = Buf(ST)
                win = ret_w_in[0].rearrange("(c p) n -> p c n", p=128)
                wout = ret_w_out[0].rearrange("(e p) n -> p e n", p=128)
                nso = 0
                for h in range(4):
                    dma("pool", WR[:, :, 0:256], win[:, :, h * 256:(h + 1) * 256], writes=[wrB])
                    dma("pool", WR[:, :, 256:512], win[:, :, 1024 + h * 256:1024 + (h + 1) * 256], writes=[wrB])
                    dma("pool", WR[:, :, 512:1024], win[:, :, 2048 + h * 512:2048 + (h + 1) * 512], writes=[wrB])
                    dma("pool", WR[:, :, 1024:1536], win[:, :, 4096 + h * 512:4096 + (h + 1) * 512], writes=[wrB])
                    dma("pool", WRO[:], wout[:, 4 * h:4 * h + 4, :], writes=[wroB])
                    dma("sp", GN[:], gng_d[:, h * 512:(h + 1) * 512], writes=[gnB])
                    if h == 0:
                        modulate(ph, i, 0, H, Hb)
                    for d in range(2):
                        dh = d * 4 + h
                        Acol = DEC[:, 0, dh:dh + 1]
                        Bcol = DEC[:, 1, dh:dh + 1]
                        Kcol = DEC[:, 2, dh:dh + 1]
                        Ccol = DEC[:, 3, dh:dh + 1]
                        mask = maskf if d == 0 else maskb
                        dma("sp", U[:], st0_d[d, h].rearrange("(c p) e -> p c e", p=128), writes=[uB])
                        cm0 = c_("cmf", 0) if d == 0 else c_("cmb", 15)
                        op("act", lambda e: e.activation(out=U[:], in_=U[:], func=AF.Identity, scale=cm0),
                           [uB, cstB], [uB])
                        op("act", lambda e: e.copy(out=UBs[ubi[0]][:], in_=U[:]), [uB],
                           [ubB[ubi[0]], ubC[ubi[0]][0], ubC[ubi[0]][1]])
                        scs = range(4) if d == 0 else range(3, -1, -1)
                        pending = []

                        def drain(keep):
                            while sum(1 for k_, _ in pending if k_ == "tr") > keep:
                                pending.pop(0)[1]()
                        for sc in scs:
                            tsl = slice(sc * 512, (sc + 1) * 512)
                            for c in range(2):
                                pb = ps_next()
                                for kc in range(8):
                                    op("pe", lambda e: e.matmul(
                                        pb.ap[:], lhsT=WR[:, kc, 256 + c * 128:256 + (c + 1) * 128], rhs=H[:, kc, tsl],
                                        start=(kc == 0), stop=(kc == 7)), [wrB, Hb[sc]], [pb])
                                op("dve", lambda e: e.tensor_copy(out=KT[:, c, :], in_=pb.ap[:]), [pb], [ktB])
                            for c in range(2):
                                pb = ps_next()
                                for kc in range(8):
                                    op("pe", lambda e: e.matmul(
                                        pb.ap[:], lhsT=WR[:, kc, c * 128:(c + 1) * 128], rhs=H[:, kc, tsl],
                                        start=(kc == 0), stop=(kc == 7)), [wrB, Hb[sc]], [pb])
                                op("act", lambda e: e.copy(out=QT[:, c, :], in_=pb.ap[:]), [pb], [qtB])
                            qs = list(range(4)) if d == 0 else [3, 2, 1, 0]

                            def proj_tm(q):
                                tok = slice(sc * 512 + q * 128, sc * 512 + (q + 1) * 128)
                                pb = ps_next()
                                pbv = pb.ap.bitcast(BF16)
                                for c in range(2):
                                    op("pe", lambda e: e.transpose(
                                        out=pbv[:, c * 128:(c + 1) * 128], in_=KT[:, c, q * 128:(q + 1) * 128],
                                        identity=identb), [ktB, cbB], [pb])
                                op("dve", lambda e: e.tensor_scalar(
                                    out=KTM[:, q, :], in0=pbv[:, 0:256], scalar1=Kcol, scalar2=None, op0=ALU.mult),
                                   [pb, decB], [ktmB[q]])
                                pb2 = ps_next()
                                for kc in range(8):
                                    op("pe", lambda e: e.matmul(
                                        pb2.ap[:], lhsT=H[:, kc, tok], rhs=WR[:, kc, 512:1024],
                                        start=(kc == 0), stop=(kc == 7)), [wrB, Hb[sc]], [pb2])
                                op("act", lambda e: e.copy(out=VTM[:, q, :], in_=pb2.ap[:]), [pb2], [vtmB[q]])

                            def st_tile(q):
                                loc = slice(q * 128, (q + 1) * 128)
                                pm_ = (sc * 4 + q) % 3
                                pb = ps_next()
                                for c in range(2):
                                    op("pe", lambda e: e.matmul(
                                        pb.ap[:, 0:128], lhsT=KT[:, c, loc], rhs=QT[:, c, loc],
                                        start=(c == 0), stop=(c == 1)), [ktB, qtB], [pb])
                                op("dve", lambda e: e.scalar_tensor_tensor(
                                    out=PTM[pm_][:], in0=pb.ap[:, 0:128], scalar=Acol, in1=mask,
                                    op0=ALU.mult, op1=ALU.mult), [pb, decB, cbB], [ptmB[pm_]])

                            proj_tm(qs[0])
                            st_tile(qs[0])
                            proj_tm(qs[1])
                            st_tile(qs[1])
                            for idx, q in enumerate(qs):
                                tt = sc * 4 + q
                                loc = slice(q * 128, (q + 1) * 128)
                                tok = slice(tt * 128, (tt + 1) * 128)
                                pm = tt % 3
                                if idx + 2 < 4:
                                    proj_tm(qs[idx + 2])
                                    st_tile(qs[idx + 2])
                                psts = []
                                for c in range(2):
                                    pst = ps_next()
                                    psts.append(pst)
                                    op("pe", lambda e: e.matmul(
                                        pst.ap[:], lhsT=KTM[:, q, c * 128:(c + 1) * 128], rhs=VTM[:, q, :],
                                        start=True, stop=True), [ktmB[q], vtmB[q]], [pst])
                                ucur = ubi[0]
                                ubi[0] = 1 - ubi[0]
                                so = nso % 2
                                nso += 1
                                for c in range(2):
                                    op("dve", lambda e: e.scalar_tensor_tensor(
                                        out=SO[so][:, c, :], in0=U[:, c, :], scalar=Ccol, in1=psts[c].ap[:],
                                        op0=ALU.mult, op1=ALU.add), [uB, decB, psts[c]], [soB[so], soC[so][c]])
                                if (d == 0 and tt % 2 == 1) or (d == 1 and tt % 2 == 0):
                                    dma("sp", ost_d[tt // 2, d, h].rearrange("(c p) e -> p c e", p=128), SO[so][:],
                                        reads=[soB[so]])
                                nxt = tt + 1 if d == 0 else tt - 1
                                if 0 <= nxt < 16:
                                    cm = c_("cmf" if d == 0 else "cmb", nxt)
                                    for c in range(2):
                                        op("act", lambda e: e.activation(
                                            out=UBs[1 - ucur][:, c, :], in_=SO[so][:, c, :], func=AF.Identity,
                                            scale=cm), [soC[so][c], cstB], [ubC[1 - ucur][c]])
                                    op("dve", lambda e: e.tensor_scalar(
                                        out=U[:], in0=SO[so][:], scalar1=cm, scalar2=None, op0=ALU.mult),
                                       [soB[so], cstB], [uB])
                                if d == 1:
                                    pg = ps_next()
                                    for kc in range(8):
                                        op("pe", lambda e: e.matmul(
                                            pg.ap[:], lhsT=H[:, kc, tok], rhs=WR[:, kc, 1024:1536],
                                            start=(kc == 0), stop=(kc == 7)), [wrB, Hb[sc]], [pg])
                                    op("act", lambda e: e.activation(out=SG[:], in_=pg.ap[:], func=AF.Silu),
                                       [pg], [sgB])
                                po = ps_next()
                                op("pe", lambda e: e.matmul(
                                    po.ap[:], lhsT=PTM[pm][:], rhs=VTM[:, q, :], start=True, stop=False),
                                   [ptmB[pm], vtmB[q]], [po])
                                for c in range(2):
                                    op("pe", lambda e: e.matmul(
                                        po.ap[:], lhsT=QT[:, c, loc], rhs=UBs[ucur][:, c, :], start=False,
                                        stop=(c == 1)), [qtB, ubC[ucur][c]], [po])
                                if d == 0:
                                    op("act", lambda e: e.activation(
                                        out=OF[:, tt, :], in_=po.ap[:], func=AF.Identity, scale=Bcol),
                                       [po, decB], [ofB[tt]])
                                else:
                                    yb = tt % 3
                                    op("dve", lambda e: e.scalar_tensor_tensor(
                                        out=OSM[:], in0=po.ap[:], scalar=Bcol, in1=OF[:, tt, :],
                                        op0=ALU.mult, op1=ALU.add), [po, decB, ofB[tt]], [osmB])
                                    op("dve", lambda e: e.bn_stats(out=ST[:, 0:6], in_=OSM[:]), [osmB], [stB])
                                    op("dve", lambda e: e.bn_aggr(out=ST[:, 8:10], in_=ST[:, 0:6]), [stB], [stB])
                                    op("dve", lambda e: e.tensor_scalar(
                                        out=ST[:, 10:11], in0=ST[:, 9:10], scalar1=EPS, scalar2=None,
                                        op0=ALU.add), [stB], [stB])
                                    op("act", lambda e: e.activation(out=ST[:, 10:11], in_=ST[:, 10:11], func=AF.Sqrt),
                                       [stB], [stB])
                                    op("dve", lambda e: e.reciprocal(out=ST[:, 10:11], in_=ST[:, 10:11]), [stB], [stB])
                                    op("dve", lambda e: e.tensor_scalar(
                                        out=OSM[:], in0=OSM[:], scalar1=ST[:, 8:9], scalar2=ST[:, 10:11],
                                        op0=ALU.subtract, op1=ALU.mult), [osmB, stB], [osmB])
                                    op("dve", lambda e: e.tensor_tensor(out=SG[:], in0=SG[:], in1=GN[:], op=ALU.mult),
                                       [sgB, gnB], [sgB])
                                    op("dve", lambda e: e.tensor_tensor(out=YB[yb][:], in0=OSM[:], in1=SG[:],
                                                                        op=ALU.mult), [osmB, sgB], [ybB[yb]])

                                    def tr(yb=yb, loc=loc):
                                        pt = ps_next()
                                        ptv = pt.ap.bitcast(BF16)
                                        for ec in range(4):
                                            op("pe", lambda e: e.transpose(
                                                out=ptv[:, ec * 128:(ec + 1) * 128],
                                                in_=YB[yb][:, ec * 128:(ec + 1) * 128], identity=identb),
                                               [ybB[yb], cbB], [pt])
                                        op("act", lambda e: e.copy(
                                            out=YT[:, :, loc], in_=ptv[:, 0:512].rearrange("p (e t) -> p e t", e=4)),
                                           [pt], [ytB])
                                    pending.append(("tr", tr))
                                    drain(2)
                            if d == 1:
                                def wout_fn(sc=sc):
                                    for f in range(8):
                                        pb = ps_next()
                                        for ec in range(4):
                                            op("pe", lambda e: e.matmul(
                                                pb.ap[:], lhsT=WRO[:, ec, f * 128:(f + 1) * 128], rhs=YT[:, ec, :],
                                                start=(ec == 0), stop=(ec == 3)), [wroB, ytB], [pb])
                                        resid_add(pb, f, sc, i, 16)
                                pending.append(("w", wout_fn))
                        while pending:
                            pending.pop(0)[1]()
                k.barrier()
                k.release(Hb + [wrB, wroB, uB, qtB, ktB, osmB, sgB, gnB, ytB, stB] + ubB + ybB + ofB + soB + ktmB
                          + vtmB + ptmB)

        for li, i in enumerate(layers):
            kind, j = i % 3, i // 3
            if kind == 0:
                mla_layer(i, j)
            elif kind == 1:
                conv_layer(i)
            else:
                ret_layer(i)
            ffn(i, layers[li + 1] if li + 1 < len(layers) else None, is_last=(li + 1 == len(layers)))

        with ExitStack() as ph:
          if not layers:
            SQ = [sb(ph, "oSQ%d" % s, [128, 512], BF16) for s in range(2)]
            sqB = [Buf(t) for t in SQ]
            RS = sb(ph, "oRS", [128, 512])
            rsB = Buf(RS)
            YF = sb(ph, "oYF", [128, 8, 512])
            yfB = Buf(YF)
            YS = [sb(ph, "oYS%d" % s, [128, D]) for s in range(2)]
            ysB = [Buf(t) for t in YS]
            for tb in range(TB):
                rms_rstd((SQ, sqB), [X[:, fc, tb * 512:(tb + 1) * 512] for fc in range(8)], 8, RS[:], rsB,
                         [Xb[tb]], 1.0 / D)
                for fc in range(8):
                    op("dve", lambda e, fc=fc, tb=tb: e.scalar_tensor_tensor(
                        out=YF[:, fc, :], in0=X[:, fc, tb * 512:(tb + 1) * 512], scalar=c_("gfin", fc), in1=RS[:],
                        op0=ALU.mult, op1=ALU.mult), [Xb[tb], cstB, rsB], [yfB])
                for q in range(4):
                    tt = tb * 4 + q
                    s = tt % 2
                    for half in range(2):
                        pb = ps_next()
                        for r in range(4):
                            fc = half * 4 + r
                            op("pe", lambda e, pb=pb, r=r, fc=fc, q=q: e.transpose(
                                out=pb.ap[:, r * 128:(r + 1) * 128], in_=YF[:, fc, q * 128:(q + 1) * 128],
                                identity=identf), [yfB, cstB], [pb])
                        if half == 0:
                            op("act", lambda e, pb=pb, s=s: e.copy(out=YS[s][:, 0:512], in_=pb.ap[:]), [pb], [ysB[s]])
                        else:
                            op("dve", lambda e, pb=pb, s=s: e.tensor_copy(out=YS[s][:, 512:1024], in_=pb.ap[:]),
                               [pb], [ysB[s]])
                    dma("sp", y_d[tt * 128:(tt + 1) * 128, :], YS[s][:], reads=[ysB[s]])
            k.barrier()
    return nc


def _const_table(cvec, inp, is_sample):
    t = np.zeros((128, NCST), np.float32)

    def put(name, arr):
        arr = np.asarray(arr, np.float32)
        t[:, _CO[name]:_CO[name] + arr.shape[1]] = arr

    def pp(v):
        v = np.asarray(v, np.float32)
        return v.reshape(-1, 128).T

    put("cv", pp(cvec))
    put("adab", np.concatenate([pp(inp["ada_b"][i]) for i in range(4)], axis=1))
    put("gmix", np.concatenate([pp(inp["norm_mix_g"][i]) for i in range(4)], axis=1))
    put("gffn", np.concatenate([pp(inp["norm_ffn_g"][i]) for i in range(4)], axis=1))
    put("gfin", pp(inp["final_norm_g"]))
    put("qng", np.concatenate([pp(inp["mla_q_norm_g"][j]) for j in range(2)], axis=1))
    put("kvg", np.concatenate([pp(inp["mla_kv_norm_g"][j]) for j in range(2)], axis=1))
    put("convw", np.concatenate([pp(inp["conv_w"][0][kk]) for kk in range(3)], axis=1))
    put("lr", np.broadcast_to(np.asarray(inp["ret_log_rate"][0], np.float32).reshape(1, 8), (128, 8)))
    p = np.arange(128, dtype=np.float32)
    coef = np.stack([p + 1, 128 - p, -(p + 1), -(128 - p), -(127 - p), -p, np.full(128, -128.0, np.float32),
                     np.zeros(128, np.float32)], axis=1)
    put("coef", coef)
    if is_sample:
        cmf = np.ones(16, np.float32)
        cmb = np.ones(16, np.float32)
    else:
        cmf = np.array([0.0 if n % 2 == 0 else 1.0 for n in range(16)], np.float32)
        cmb = np.array([0.0 if n % 2 == 1 else 1.0 for n in range(16)], np.float32)
    put("cmf", np.broadcast_to(cmf.reshape(1, 16), (128, 16)))
    put("cmb", np.broadcast_to(cmb.reshape(1, 16), (128, 16)))
    bt = np.zeros((20, 8), np.float32)
    if not is_sample:
        bt[:] = NEG
        for kt in range(4, 20):
            sk = (kt - 4) // 2
            bt[kt, sk] = 0.0
    put("btab", np.broadcast_to(bt.reshape(1, 160), (128, 160)))
    put("identf", np.eye(128, dtype=np.float32))
    put("ones", np.ones((128, 128), np.float32))
    jj = np.arange(128)[:, None]
    ii = np.arange(128)[None, :]
    put("maskf", (jj <= ii).astype(np.float32))
    put("maskb", (jj >= ii).astype(np.float32))
    return t


def _rope_tables(is_sample):
    r = np.zeros((2, 64, NT), np.float32)
    if not is_sample:
        r[0] = 1.0
        return r
    tok = np.arange(NT)
    row = (tok // 64).astype(np.float32)
    col = (tok % 64).astype(np.float32)
    inv = (np.float32(10000.0) ** (-np.arange(16, dtype=np.float32) / np.float32(16))).astype(np.float32)
    for ax, pos in enumerate((row, col)):
        ang = (pos[None, :] * inv[:, None]).astype(np.float32)
        c, s = np.cos(ang), np.sin(ang)
        for hf in range(2):
            p0 = ax * 32 + hf * 16
            r[0, p0:p0 + 16] = c
            r[1, p0:p0 + 16] = -s if hf == 0 else s
    return r


def _mask_factors(is_sample):
    big = 29952.0
    m = np.zeros((2, 16, 2560), np.float32)
    m[0, 15, :] = 1.0
    if not is_sample:
        m[0, 0, :] = -big
        m[1, 0, :2048] = 1.0
        for sq in range(8):
            m[0, 1 + sq, 512 + sq * 256:512 + (sq + 1) * 256] = big
            m[1, 1 + sq, sq * 256:(sq + 1) * 256] = 1.0
    return m


def _conv_masks(is_sample):
    seq = NT if is_sample else 256
    t = np.arange(NT)
    mp = (t % seq != 0).astype(np.float32)
    mn = (t % seq != seq - 1).astype(np.float32)
    return np.ascontiguousarray(np.broadcast_to(np.stack([mp, mn])[:, None, :], (2, 128, NT))).astype(np.float32)


_NC_CACHE = {}
_LAYERS = (0, 1, 2, 3)


def kernel(**inp):
    inp = {k_: np.asarray(v) for k_, v in inp.items()}
    if _LAYERS not in _NC_CACHE:
        _NC_CACHE[_LAYERS] = build(_LAYERS)
    nc = _NC_CACHE[_LAYERS]
    wnames = ["ada_w", "mla_w_a", "mla_w_q_b", "mla_w_kv_b", "mla_w_o", "conv_w_in", "conv_w_out", "ret_w_in",
              "ret_w_out", "ffn_w_in", "ffn_w_out"]
    shared = {n: np.ascontiguousarray(inp[n], dtype=np.float32) for n in wnames}
    gng = np.ascontiguousarray(np.broadcast_to(inp["ret_gn_g"][0].reshape(1, 2048), (128, 2048))).astype(np.float32)
    in_maps = []
    for core in range(8):
        is_s = core < 4
        m = dict(shared)
        if is_s:
            b = core
            m["x"] = np.ascontiguousarray(inp["x_sample"][b])
            cvec = inp["c"][b]
            m["st0"] = np.ascontiguousarray(inp["state_ret"][b, 0])
            m["cckv"] = np.ascontiguousarray(inp["cache_mla_ckv"][b].transpose(0, 2, 1))
            m["ckpe"] = np.ascontiguousarray(inp["cache_mla_kpe"][b].transpose(0, 2, 1))
        else:
            p = core - 4
            m["x"] = np.ascontiguousarray(inp["x_prompt"][8 * p:8 * p + 8].reshape(NT, D))
            cvec = inp["c_ctx"]
            m["st0"] = np.ascontiguousarray(inp["state_ret"][p, 0])
            m["cckv"] = np.ascontiguousarray(inp["cache_mla_ckv"][p].transpose(0, 2, 1))
            m["ckpe"] = np.ascontiguousarray(inp["cache_mla_kpe"][p].transpose(0, 2, 1))
        m["cst"] = _const_table(cvec, inp, is_s)
        m["rope"] = _rope_tables(is_s)
        m["cmask"] = _conv_masks(is_s)
        m["mk"] = _mask_factors(is_s)
        m["gng"] = gng
        in_maps.append(m)
    res = run_bass_kernel_spmd(nc, in_maps, core_ids=list(range(8)))
    R = res.results
    y_sample = np.stack([R[b]["y"] for b in range(4)], axis=0).astype(np.float32)
    y_prompt = np.concatenate([R[4 + p]["y"].reshape(8, 256, D) for p in range(4)], axis=0).astype(np.float32)
    ckv = np.concatenate([R[4 + p]["ockv"].reshape(2, 8, 256, 256).transpose(1, 0, 2, 3) for p in range(4)], axis=0)
    kpe = np.concatenate([R[4 + p]["okpe"].reshape(2, 8, 256, 64).transpose(1, 0, 2, 3) for p in range(4)], axis=0)
    st = np.concatenate([R[4 + p]["ost"].reshape(8, 1, 2, 4, 256, 512) for p in range(4)], axis=0)
    return (y_prompt, y_sample, np.ascontiguousarray(ckv, dtype=np.float32),
            np.ascontiguousarray(kpe, dtype=np.float32), np.ascontiguousarray(st, dtype=np.float32))
```

```python
import numpy as np
from contextlib import ExitStack
import concourse.bass as bass
import concourse.mybir as mybir
from concourse.bass_utils import run_bass_kernel_spmd

F32 = mybir.dt.float32
BF16 = mybir.dt.bfloat16
AF = mybir.ActivationFunctionType
ALU = mybir.AluOpType

D = 1024
NT = 2048
TB = 4
FFN_H = 2816
EPS = 1e-6
MLA_SCALE = 192 ** -0.5
NEG = -30000.0

_CO = {}
_n = 0
for _name, _w in [("cv", 8), ("adab", 192), ("gmix", 32), ("gffn", 32), ("gfin", 8), ("qng", 6),
                  ("kvg", 4), ("convw", 24), ("lr", 8), ("coef", 8), ("cmf", 16), ("cmb", 16),
                  ("btab", 160), ("identf", 128), ("ones", 128), ("maskf", 128), ("maskb", 128)]:
    _CO[_name] = _n
    _n += _w
NCST = _n


class DS:
    def __init__(self, sem, key):
        self.sem = sem
        self.cnt = 0
        self.key = key


class Buf:
    __slots__ = ("ap", "w", "r", "ds")

    def __init__(self, ap):
        self.ap = ap
        self.w = None
        self.r = {}
        self.ds = None


class Eng:
    def __init__(self, eng, sem):
        self.eng = eng
        self.sem = sem
        self.cnt = 0
        self.seen = {}


class K:
    def __init__(self, nc, es):
        self.nc = nc
        self.E = {}
        for name, eng in [("pe", nc.tensor), ("act", nc.scalar), ("dve", nc.vector),
                          ("pool", nc.gpsimd), ("sp", nc.sync)]:
            self.E[name] = Eng(eng, es.enter_context(nc.semaphore("s_" + name)))
        self.free_ds = {"sp": [], "pool": []}
        self.all_ds = {}
        for i in range(64):
            d = DS(es.enter_context(nc.semaphore("d%d" % i)), ("d", i))
            d.q = "sp" if i < 24 else "pool"
            self.free_ds[d.q].append(d)
            self.all_ds[d.key] = d
        self.dirty = set()

    def _sem_of(self, key):
        if isinstance(key, tuple):
            d = self.all_ds[key]
            return d.sem, 16
        return self.E[key].sem, 1

    def _waits(self, en, reads, writes):
        e = self.E[en]
        need = {}
        for b in reads:
            if b.w is not None and b.w[1] > need.get(b.w[0], 0):
                need[b.w[0]] = b.w[1]
        for b in writes:
            if b.w is not None and b.w[1] > need.get(b.w[0], 0):
                need[b.w[0]] = b.w[1]
            for k, v in b.r.items():
                if k == en and en == "pe":
                    continue
                if v > need.get(k, 0):
                    need[k] = v
        for key, val in need.items():
            if key == en and en == "pe":
                continue
            if e.seen.get(key, 0) >= val:
                continue
            sem, mult = self._sem_of(key)
            e.eng.wait_ge(sem, val * mult)
            e.seen[key] = val

    def op(self, en, fn, reads=(), writes=()):
        e = self.E[en]
        self._waits(en, reads, writes)
        inst = fn(e.eng)
        e.cnt += 1
        inst.then_inc(e.sem, 1)
        for b in writes:
            b.w = (en, e.cnt)
            b.r = {}
        for b in reads:
            b.r[en] = e.cnt

    def dma(self, q, out, in_, reads=(), writes=(), **kw):
        owner = writes[0] if writes else reads[0]
        if owner.ds is None:
            owner.ds = self.free_ds[q].pop()
        ds = owner.ds
        assert ds.q == q
        e = self.E[q]
        self._waits(q, reads, writes)
        if q == "pool":
            kw.setdefault("max_dma_last_dim", 8192)
        inst = e.eng.dma_start(out=out, in_=in_, **kw)
        ds.cnt += 1
        inst.then_inc(ds.sem, 16)
        self.dirty.add(ds.key)
        for b in writes:
            b.w = (ds.key, ds.cnt)
            b.r = {}
        for b in reads:
            b.r[ds.key] = ds.cnt

    def barrier(self):
        keys = [(k, self.E[k].cnt) for k in ("pe", "act", "dve", "pool") if self.E[k].cnt > 0]
        keys += [(k, self.all_ds[k].cnt) for k in sorted(self.dirty)]
        self.dirty = set()
        for en, e in self.E.items():
            for key, val in keys:
                if key == en:
                    continue
                if e.seen.get(key, 0) >= val:
                    continue
                sem, mult = self._sem_of(key)
                e.eng.wait_ge(sem, val * mult)
                e.seen[key] = val

    def release(self, bufs):
        for b in bufs:
            if b.ds is not None:
                self.free_ds[b.ds.q].append(b.ds)
                b.ds = None


_DBG = set()
RSUM_ENG = ("dve", "pool")


def build(layers=(0, 1, 2, 3)):
    nc = bass.Bass("TRN2", target_bir_lowering=False)

    def din(name, shape):
        return nc.dram_tensor(name, list(shape), F32, kind="ExternalInput").ap()

    def dout(name, shape):
        return nc.dram_tensor(name, list(shape), F32, kind="ExternalOutput").ap()

    x_d = din("x", [NT, D])
    cst_d = din("cst", [128, NCST])
    rope_d = din("rope", [2, 64, NT])
    cmask_d = din("cmask", [2, 128, NT])
    gng_d = din("gng", [128, 2048])
    mk_d = din("mk", [2, 16, 2560])
    st0_d = din("st0", [2, 4, 256, 512])
    cckv_d = din("cckv", [2, 256, 512])
    ckpe_d = din("ckpe", [2, 64, 512])
    ada_w = din("ada_w", [4, D, 6 * D])
    mla_w_a = din("mla_w_a", [2, D, 704])
    mla_w_q_b = din("mla_w_q_b", [2, 384, 1536])
    mla_w_kv_b = din("mla_w_kv_b", [2, 256, 2048])
    mla_w_o = din("mla_w_o", [2, 1024, D])
    conv_w_in = din("conv_w_in", [1, D, 3 * D])
    conv_w_out = din("conv_w_out", [1, D, D])
    ret_w_in = din("ret_w_in", [1, D, 6144])
    ret_w_out = din("ret_w_out", [1, 2048, D])
    ffn_w_in = din("ffn_w_in", [4, D, 2 * FFN_H])
    ffn_w_out = din("ffn_w_out", [4, FFN_H, D])

    y_d = dout("y", [NT, D])
    ockv_d = dout("ockv", [2, NT, 256])
    okpe_d = dout("okpe", [2, NT, 64])
    ost_d = dout("ost", [8, 2, 4, 256, 512])

    es = ExitStack()
    with es:
        k = K(nc, es)
        op, dma = k.op, k.dma

        uid = [0]

        def sb(scope, name, shape, dt=F32):
            uid[0] += 1
            return scope.enter_context(nc.sbuf_tensor("%s_%d" % (name, uid[0]), list(shape), dt))

        X = sb(es, "X", [128, 8, NT])
        Xb = [Buf(X[:, :, t * 512:(t + 1) * 512]) for t in range(TB)]
        CST = sb(es, "CST", [128, NCST])
        cstB = Buf(CST)
        CB = sb(es, "CB", [128, 4, 128], BF16)
        cbB = Buf(CB)
        MODS = sb(es, "MODS", [128, 4, 48])
        G12 = sb(es, "G12", [128, 4, 16])
        modB = Buf(MODS)
        SCV = sb(es, "SCV", [128, 8], BF16)
        scvB = Buf(SCV)
        DEC = sb(es, "DEC", [128, 4, 8])
        decB = Buf(DEC)
        PSt = [es.enter_context(nc.psum_tensor("ps%d" % i, [128, 512], F32)) for i in range(8)]
        PS = [Buf(t) for t in PSt]
        rr = [0]

        def ps_next(banks=range(8)):
            banks = list(banks)
            b = banks[rr[0] % len(banks)]
            rr[0] += 1
            return PS[b]

        def c_(name, a=0, b=None):
            o = _CO[name]
            if b is None:
                b = a + 1
            return CST[:, o + a:o + b]

        identf = c_("identf", 0, 128)
        identb = CB[:, 0, :]
        onesb = CB[:, 1, :]
        maskf = CB[:, 2, :]
        maskb = CB[:, 3, :]

        def ada_dma(i, b, AW, awB):
            src = ada_w[i].rearrange("(c p) n -> p c n", p=128)
            dma("pool", AW[b % 2][:], src[:, :, b * 384:(b + 1) * 384], writes=[awB[b % 2]])

        def ada_pe(b, pm, AW, awB):
            s_ = b % 2
            for n in range(3):
                col = b * 3 + n
                for kc in range(8):
                    op("pe", lambda e: e.matmul(pm.ap[:, col:col + 1], lhsT=AW[s_][:, kc, n * 128:(n + 1) * 128],
                                                rhs=SCV[:, kc:kc + 1], start=(kc == 0), stop=(kc == 7)),
                       [awB[s_], scvB], [pm])

        def ada_finish(i, pm):
            op("dve", lambda e: e.tensor_tensor(out=MODS[:, i, :], in0=pm.ap[:, 0:48],
                                                in1=c_("adab", i * 48, i * 48 + 48), op=ALU.add), [pm, cstB], [modB])
            op("dve", lambda e: e.scalar_tensor_tensor(out=G12[:, i, 0:8], in0=MODS[:, i, 8:16], scalar=1.0,
                                                       in1=c_("gmix", i * 8, i * 8 + 8), op0=ALU.add, op1=ALU.mult),
               [modB, cstB], [modB])
            op("dve", lambda e: e.scalar_tensor_tensor(out=G12[:, i, 8:16], in0=MODS[:, i, 32:40], scalar=1.0,
                                                       in1=c_("gffn", i * 8, i * 8 + 8), op0=ALU.add, op1=ALU.mult),
               [modB, cstB], [modB])

        dma("sp", CST[:], cst_d[:, :], writes=[cstB])
        op("dve", lambda e: e.tensor_copy(out=CB[:, 0, :], in_=c_("identf", 0, 128)), [cstB], [cbB])
        op("dve", lambda e: e.tensor_copy(out=CB[:, 1, :], in_=c_("ones", 0, 128)), [cstB], [cbB])
        op("dve", lambda e: e.tensor_copy(out=CB[:, 2, :], in_=c_("maskf", 0, 128)), [cstB], [cbB])
        op("dve", lambda e: e.tensor_copy(out=CB[:, 3, :], in_=c_("maskb", 0, 128)), [cstB], [cbB])
        op("act", lambda e: e.activation(out=SCV[:], in_=c_("cv", 0, 8), func=AF.Silu), [cstB], [scvB])
        with ExitStack() as ph:
            ELR = sb(ph, "ELR", [128, 8])
            elrB = Buf(ELR)
            op("act", lambda e: e.activation(out=ELR[:], in_=c_("lr", 0, 8), func=AF.Exp), [cstB], [elrB])
            for t, (cf, cb_) in enumerate([(0, 1), (2, 3), (4, 5), (6, 6)]):
                op("act", lambda e, t=t, cf=cf: e.activation(out=DEC[:, t, 0:4], in_=ELR[:, 0:4], func=AF.Exp,
                                                            scale=c_("coef", cf)), [elrB, cstB], [decB])
                op("act", lambda e, t=t, cb_=cb_: e.activation(out=DEC[:, t, 4:8], in_=ELR[:, 4:8], func=AF.Exp,
                                                              scale=c_("coef", cb_)), [elrB, cstB], [decB])
            op("dve", lambda e: e.tensor_scalar(out=DEC[:, 0, :], in0=DEC[:, 0, :], scalar1=0.0625, scalar2=None,
                                                op0=ALU.mult), [decB], [decB])
            op("dve", lambda e: e.tensor_scalar(out=DEC[:, 2, :], in0=DEC[:, 2, :], scalar1=0.0625, scalar2=None,
                                                op0=ALU.mult), [decB], [decB])
            XS = [sb(ph, "XS%d" % i, [128, D]) for i in range(3)]
            xsB = [Buf(t) for t in XS]
            for tt in range(16):
                s = tt % 3
                dma("sp", XS[s][:], x_d[tt * 128:(tt + 1) * 128, :], writes=[xsB[s]])
                for half in range(2):
                    pb = ps_next()
                    for q in range(4):
                        fc = half * 4 + q
                        op("pe", lambda e, pb=pb, q=q, fc=fc, s=s: e.transpose(
                            out=pb.ap[:, q * 128:(q + 1) * 128], in_=XS[s][:, fc * 128:(fc + 1) * 128],
                            identity=identf), [xsB[s], cstB], [pb])
                    eng = "act" if half == 0 else "dve"
                    if eng == "act":
                        op("act", lambda e, pb=pb, half=half, tt=tt: e.copy(
                            out=X[:, half * 4:half * 4 + 4, tt * 128:(tt + 1) * 128],
                            in_=pb.ap.rearrange("p (q t) -> p q t", q=4)), [pb], [Xb[tt // 4]])
                    else:
                        op("dve", lambda e, pb=pb, half=half, tt=tt: e.tensor_copy(
                            out=X[:, half * 4:half * 4 + 4, tt * 128:(tt + 1) * 128],
                            in_=pb.ap.rearrange("p (q t) -> p q t", q=4)), [pb], [Xb[tt // 4]])
            AW = [sb(ph, "AW%d" % i, [128, 8, 1536], BF16) for i in range(4)]
            awB = [Buf(t) for t in AW]
            if layers:
                pm = PS[7]
                src0 = ada_w[layers[0]].rearrange("(c p) n -> p c n", p=128)
                for bl in range(4):
                    for hf in range(2):
                        dma("pool", AW[bl][:, hf * 4:hf * 4 + 4, :],
                            src0[:, hf * 4:hf * 4 + 4, bl * 1536:(bl + 1) * 1536], writes=[awB[bl]])
                for bl in range(4):
                    for n in range(12):
                        col = bl * 12 + n
                        for kc in range(8):
                            op("pe", lambda e: e.matmul(pm.ap[:, col:col + 1], lhsT=AW[bl][:, kc, n * 128:(n + 1) * 128],
                                                        rhs=SCV[:, kc:kc + 1], start=(kc == 0), stop=(kc == 7)),
                               [awB[bl], scvB], [pm])
                ada_finish(layers[0], pm)
            k.barrier()
            k.release(xsB + awB + [elrB])

        def rms_rstd(scope_bufs, srcs, nchunk, out_rstd, out_buf, src_reads, inv_n):
            SQ, sqB = scope_bufs
            pss = ps_next()
            for c in range(nchunk):
                s = c % 2
                op("act", lambda e, c=c, s=s: e.activation(out=SQ[s][:], in_=srcs[c], func=AF.Square),
                   src_reads, [sqB[s]])
                op("pe", lambda e, c=c, s=s, pss=pss: e.matmul(pss.ap[:], lhsT=onesb, rhs=SQ[s][:],
                                                              start=(c == 0), stop=(c == nchunk - 1)),
                   [sqB[s], cbB], [pss])
            op("dve", lambda e, pss=pss: e.tensor_scalar(out=out_rstd, in0=pss.ap[:], scalar1=inv_n, scalar2=EPS,
                                                         op0=ALU.mult, op1=ALU.add), [pss], [out_buf])
            op("act", lambda e: e.activation(out=out_rstd, in_=out_rstd, func=AF.Sqrt), [out_buf], [out_buf])
            op("dve", lambda e: e.reciprocal(out=out_rstd, in_=out_rstd), [out_buf], [out_buf])

        mSQ = [sb(es, "mSQ%d" % s_, [128, 512], BF16) for s_ in range(2)]
        msqB = [Buf(t) for t in mSQ]
        mRS = [sb(es, "mRS%d" % s_, [128, 512]) for s_ in range(3)]
        mrsB = [Buf(t) for t in mRS]
        mTM = [sb(es, "mTM%d" % s_, [128, 512]) for s_ in range(2)]
        mtmB = [Buf(t) for t in mTM]
        mcnt = [0]

        def modulate(ph, i, which, H, Hb):
            goff = 0 if which == 0 else 8
            shoff = 0 if which == 0 else 24
            pss = {}

            def st1a(tb):
                r = tb % 3
                pss[tb] = ps_next()
                for c in range(8):
                    s_ = c % 2
                    op("act", lambda e: e.activation(out=mSQ[s_][:], in_=X[:, c, tb * 512:(tb + 1) * 512],
                                                     func=AF.Square), [Xb[tb]], [msqB[s_]])
                    op("pe", lambda e: e.matmul(pss[tb].ap[:], lhsT=onesb, rhs=mSQ[s_][:], start=(c == 0),
                                                stop=(c == 7)), [msqB[s_], cbB], [pss[tb]])
                op("dve", lambda e: e.tensor_scalar(out=mRS[r][:], in0=pss[tb].ap[:], scalar1=1.0 / D, scalar2=EPS,
                                                    op0=ALU.mult, op1=ALU.add), [pss[tb]], [mrsB[r]])

            def st1b(tb):
                r = tb % 3
                op("act", lambda e: e.activation(out=mRS[r][:], in_=mRS[r][:], func=AF.Sqrt), [mrsB[r]], [mrsB[r]])
                op("dve", lambda e: e.reciprocal(out=mRS[r][:], in_=mRS[r][:]), [mrsB[r]], [mrsB[r]])

            def st2(tb):
                r = tb % 3
                for fc in range(8):
                    s_ = mcnt[0] % 2
                    mcnt[0] += 1
                    op("dve", lambda e: e.scalar_tensor_tensor(
                        out=mTM[s_][:], in0=X[:, fc, tb * 512:(tb + 1) * 512], scalar=G12[:, i, goff + fc:goff + fc + 1],
                        in1=mRS[r][:], op0=ALU.mult, op1=ALU.mult), [Xb[tb], modB, mrsB[r]], [mtmB[s_]])
                    op("act", lambda e: e.activation(
                        out=H[:, fc, tb * 512:(tb + 1) * 512], in_=mTM[s_][:], func=AF.Identity,
                        bias=MODS[:, i, shoff + fc:shoff + fc + 1], scale=1.0), [mtmB[s_], modB], [Hb[tb]])

            st1a(0)
            st1a(1)
            st1b(0)
            for tb in range(TB):
                if tb + 2 < TB:
                    st1a(tb + 2)
                if tb + 1 < TB:
                    st1b(tb + 1)
                st2(tb)
            return []

        def resid_add(pb, f, tb, i, goff):
            op("dve", lambda e: e.scalar_tensor_tensor(
                out=X[:, f, tb * 512:(tb + 1) * 512], in0=pb.ap[:], scalar=MODS[:, i, goff + f:goff + f + 1],
                in1=X[:, f, tb * 512:(tb + 1) * 512], op0=ALU.mult, op1=ALU.add), [pb, modB, Xb[tb]], [Xb[tb]])

        def ffn(i, inext=None, is_last=False):
            with ExitStack() as ph:
                H = sb(ph, "fH", [128, 8, NT], BF16)
                Hb = [Buf(H[:, :, t * 512:(t + 1) * 512]) for t in range(TB)]
                tmp = []
                if inext is not None:
                    AW = [sb(ph, "fAW%d" % s, [128, 8, 384], BF16) for s in range(2)]
                    awB = [Buf(t) for t in AW]
                else:
                    AW, awB = [], []
                pm = PS[7]
                if is_last:
                    YF = [sb(ph, "oYF%d" % s, [128, 8, 128]) for s in range(2)]
                    yfB = [Buf(t) for t in YF]
                    YS = [sb(ph, "oYS%d" % s, [128, D]) for s in range(2)]
                    ysB = [Buf(t) for t in YS]
                    tmp = yfB + ysB

                def final_tb(tb):
                    r = tb % 3
                    pss = ps_next(range(7))
                    for c in range(8):
                        s_ = c % 2
                        op("act", lambda e: e.activation(out=mSQ[s_][:], in_=X[:, c, tb * 512:(tb + 1) * 512],
                                                         func=AF.Square), [Xb[tb]], [msqB[s_]])
                        op("pe", lambda e: e.matmul(pss.ap[:], lhsT=onesb, rhs=mSQ[s_][:], start=(c == 0),
                                                    stop=(c == 7)), [msqB[s_], cbB], [pss])
                    op("dve", lambda e: e.tensor_scalar(out=mRS[r][:], in0=pss.ap[:], scalar1=1.0 / D, scalar2=EPS,
                                                        op0=ALU.mult, op1=ALU.add), [pss], [mrsB[r]])
                    op("act", lambda e: e.activation(out=mRS[r][:], in_=mRS[r][:], func=AF.Sqrt), [mrsB[r]], [mrsB[r]])
                    op("dve", lambda e: e.reciprocal(out=mRS[r][:], in_=mRS[r][:]), [mrsB[r]], [mrsB[r]])
                    for q in range(4):
                        tt = tb * 4 + q
                        y_ = tt % 2
                        tok = slice(tt * 128, (tt + 1) * 128)
                        for fc in range(8):
                            op("dve", lambda e: e.scalar_tensor_tensor(
                                out=YF[y_][:, fc, :], in0=X[:, fc, tok], scalar=c_("gfin", fc),
                                in1=mRS[r][:, q * 128:(q + 1) * 128], op0=ALU.mult, op1=ALU.mult),
                               [Xb[tb], cstB, mrsB[r]], [yfB[y_]])
                        for half in range(2):
                            pb = ps_next(range(7))
                            for rr_ in range(4):
                                fc = half * 4 + rr_
                                op("pe", lambda e: e.transpose(out=pb.ap[:, rr_ * 128:(rr_ + 1) * 128],
                                                               in_=YF[y_][:, fc, :], identity=identf),
                                   [yfB[y_], cstB], [pb])
                            if half == 0:
                                op("act", lambda e: e.copy(out=YS[y_][:, 0:512], in_=pb.ap[:]), [pb], [ysB[y_]])
                            else:
                                op("dve", lambda e: e.tensor_copy(out=YS[y_][:, 512:1024], in_=pb.ap[:]),
                                   [pb], [ysB[y_]])
                        dma("sp", y_d[tok, :], YS[y_][:], reads=[ysB[y_]])
                ACTB = sb(ph, "fACT", [128, 4, NT], BF16)
                actB = [Buf(ACTB[:, :, t * 512:(t + 1) * 512]) for t in range(TB)]
                WI = [sb(ph, "fWI%d" % s, [128, 8, 2, 512], BF16) for s in range(2)]
                wiB = [Buf(t) for t in WI]
                WO = [sb(ph, "fWO%d" % s, [128, 4, D], BF16) for s in range(2)]
                woB = [Buf(t) for t in WO]
                SA = [sb(ph, "fSA%d" % s, [128, 512]) for s in range(2)]
                saB = [Buf(t) for t in SA]
                win = ffn_w_in[i].rearrange("(c p) n -> p c n", p=128)
                wout = ffn_w_out[i].rearrange("(j p) n -> p j n", p=128)
                groups = [(0, 4), (4, 4), (8, 4), (12, 4), (16, 4), (20, 2)]
                nsa = 0
                def load_w(g):
                    j0_, nj_ = groups[g]
                    s_ = g % 2
                    for ab in range(2):
                        c0 = ab * FFN_H + j0_ * 128
                        dma("pool", WI[s_][:, :, ab, 0:nj_ * 128], win[:, :, c0:c0 + nj_ * 128], writes=[wiB[s_]])
                    dma("pool", WO[s_][:, 0:nj_, :], wout[:, j0_:j0_ + nj_, :], writes=[woB[s_]])

                load_w(0)
                if inext is not None:
                    for b in range(0, 2):
                        ada_dma(inext, b, AW, awB)
                modulate(ph, i, 1, H, Hb)
                for g, (j0, nj) in enumerate(groups):
                    s = g % 2
                    if inext is not None and 0 < g < 4:
                        for b in range(4 * g, 4 * g + 2):
                            ada_dma(inext, b, AW, awB)
                    for tb in range(TB):
                        for j in range(nj):
                            pa = ps_next(range(7))
                            pbb = ps_next(range(7))
                            for ab, pp in ((0, pa), (1, pbb)):
                                for kc in range(8):
                                    op("pe", lambda e, pp=pp, kc=kc, ab=ab, j=j, tb=tb, s=s: e.matmul(
                                        pp.ap[:], lhsT=WI[s][:, kc, ab, j * 128:(j + 1) * 128],
                                        rhs=H[:, kc, tb * 512:(tb + 1) * 512], start=(kc == 0), stop=(kc == 7)),
                                       [wiB[s], Hb[tb]], [pp])
                            q = nsa % 2
                            nsa += 1
                            op("act", lambda e, pa=pa, q=q: e.activation(out=SA[q][:], in_=pa.ap[:], func=AF.Silu),
                               [pa], [saB[q]])
                            op("dve", lambda e, pbb=pbb, q=q, j=j, tb=tb: e.tensor_tensor(
                                out=ACTB[:, j, tb * 512:(tb + 1) * 512], in0=SA[q][:], in1=pbb.ap[:], op=ALU.mult),
                               [saB[q], pbb], [actB[tb]])
                    if g + 1 < len(groups):
                        load_w(g + 1)
                    if inext is not None and g < 4:
                        for b in range(4 * g, 4 * g + 2):
                            ada_pe(b, pm, AW, awB)
                        for b in range(4 * g + 2, 4 * g + 4):
                            ada_dma(inext, b, AW, awB)
                    def pass2(tb):
                        for f in range(8):
                            po = ps_next(range(7))
                            for j in range(nj):
                                op("pe", lambda e: e.matmul(
                                    po.ap[:], lhsT=WO[s][:, j, f * 128:(f + 1) * 128],
                                    rhs=ACTB[:, j, tb * 512:(tb + 1) * 512], start=(j == 0), stop=(j == nj - 1)),
                                   [woB[s], actB[tb]], [po])
                            resid_add(po, f, tb, i, 40)

                    if is_last and g == len(groups) - 1:
                        pass2(0)
                        pass2(1)
                        final_tb(0)
                        pass2(2)
                        final_tb(1)
                        pass2(3)
                        final_tb(2)
                        final_tb(3)
                    else:
                        for tb in range(TB):
                            pass2(tb)
                    if inext is not None and g < 4:
                        for b in range(4 * g + 2, 4 * g + 4):
                            ada_pe(b, pm, AW, awB)
                        if g == 3:
                            ada_finish(inext, pm)
                k.barrier()
                k.release(Hb + tmp + actB + wiB + woB + saB + awB)

        def conv_layer(i):
            with ExitStack() as ph:
                H = sb(ph, "cH", [128, 8, NT], BF16)
                Hb = [Buf(H[:, :, t * 512:(t + 1) * 512]) for t in range(TB)]
                ZB = sb(ph, "cZB", [128, 8, NT], BF16)
                zbB = [Buf(ZB[:, :, t * 512:(t + 1) * 512]) for t in range(TB)]
                rel = []
                with ExitStack() as ph2:
                    WCI = [sb(ph2, "cWI%d" % s, [128, 8, 3, 128], BF16) for s in range(2)]
                    wciB = [Buf(t) for t in WCI]
                    CS = sb(ph2, "cCS", [128, NT])
                    csB = Buf(CS)
                    Z = sb(ph2, "cZ", [128, NT + 2])
                    zB = Buf(Z)
                    TMP = sb(ph2, "cTMP", [128, NT])
                    tB = Buf(TMP)
                    MK = sb(ph2, "cMK", [128, 2, NT], BF16)
                    mkB = Buf(MK)
                    for m in range(2):
                        dma("pool", MK[:, m, :], cmask_d[m], writes=[mkB])
                    op("dve", lambda e: e.memset(Z[:, 0:1], 0.0), [], [zB])
                    op("dve", lambda e: e.memset(Z[:, NT + 1:NT + 2], 0.0), [], [zB])
                    cwin = conv_w_in[0].rearrange("(c p) n -> p c n", p=128)
                    for fc in range(8):
                        s = fc % 2
                        for g3 in range(3):
                            dma("pool", WCI[s][:, :, g3, :], cwin[:, :, g3 * D + fc * 128:g3 * D + (fc + 1) * 128],
                                writes=[wciB[s]])
                        if fc == 0:
                            modulate(ph, i, 0, H, Hb)

                        def proj(g3, banks):
                            for tb in range(TB):
                                pp = PS[banks[tb]]
                                for kc in range(8):
                                    op("pe", lambda e, pp=pp, kc=kc, tb=tb: e.matmul(
                                        pp.ap[:], lhsT=WCI[s][:, kc, g3, :], rhs=H[:, kc, tb * 512:(tb + 1) * 512],
                                        start=(kc == 0), stop=(kc == 7)), [wciB[s], Hb[tb]], [pp])
                        bA = [0, 1, 2, 3] if fc % 2 == 0 else [4, 5, 6, 7]
                        bB = [4, 5, 6, 7] if fc % 2 == 0 else [0, 1, 2, 3]
                        proj(1, bA)
                        for tb in range(TB):
                            op("act", lambda e, tb=tb: e.copy(out=CS[:, tb * 512:(tb + 1) * 512], in_=PS[bA[tb]].ap[:]),
                               [PS[bA[tb]]], [csB])
                        proj(2, bB)
                        for tb in range(TB):
                            op("dve", lambda e, tb=tb: e.tensor_tensor(
                                out=Z[:, 1 + tb * 512:1 + (tb + 1) * 512], in0=CS[:, tb * 512:(tb + 1) * 512],
                                in1=PS[bB[tb]].ap[:], op=ALU.mult), [csB, PS[bB[tb]]], [zB])
                        proj(0, bA)
                        w0 = c_("convw", 0 * 8 + fc)
                        w1 = c_("convw", 1 * 8 + fc)
                        w2 = c_("convw", 2 * 8 + fc)
                        op("dve", lambda e: e.tensor_scalar(out=CS[:], in0=Z[:, 1:NT + 1], scalar1=w1, scalar2=None,
                                                            op0=ALU.mult), [zB, cstB], [csB])
                        op("dve", lambda e: e.scalar_tensor_tensor(out=TMP[:], in0=Z[:, 0:NT], scalar=w0,
                                                                   in1=MK[:, 0, :], op0=ALU.mult, op1=ALU.mult),
                           [zB, cstB, mkB], [tB])
                        op("dve", lambda e: e.tensor_tensor(out=CS[:], in0=CS[:], in1=TMP[:], op=ALU.add),
                           [csB, tB], [csB])
                        op("dve", lambda e: e.scalar_tensor_tensor(out=TMP[:], in0=Z[:, 2:NT + 2], scalar=w2,
                                                                   in1=MK[:, 1, :], op0=ALU.mult, op1=ALU.mult),
                           [zB, cstB, mkB], [tB])
                        op("dve", lambda e: e.tensor_tensor(out=CS[:], in0=CS[:], in1=TMP[:], op=ALU.add),
                           [csB, tB], [csB])
                        for tb in range(TB):
                            op("dve", lambda e, tb=tb, fc=fc: e.tensor_tensor(
                                out=ZB[:, fc, tb * 512:(tb + 1) * 512], in0=CS[:, tb * 512:(tb + 1) * 512],
                                in1=PS[bA[tb]].ap[:], op=ALU.mult), [csB, PS[bA[tb]]], [zbB[tb]])
                    k.barrier()
                    k.release(wciB + [csB, zB, tB, mkB])
                with ExitStack() as ph2:
                    WCO = sb(ph2, "cWO", [128, 8, D], BF16)
                    wcoB = Buf(WCO)
                    dma("pool", WCO[:], conv_w_out[0].rearrange("(c p) n -> p c n", p=128), writes=[wcoB])
                    for tb in range(TB):
                        for f in range(8):
                            po = ps_next()
                            for kc in range(8):
                                op("pe", lambda e, po=po, kc=kc, f=f, tb=tb: e.matmul(
                                    po.ap[:], lhsT=WCO[:, kc, f * 128:(f + 1) * 128],
                                    rhs=ZB[:, kc, tb * 512:(tb + 1) * 512], start=(kc == 0), stop=(kc == 7)),
                                   [wcoB, zbB[tb]], [po])
                            resid_add(po, f, tb, i, 16)
                    k.barrier()
                    k.release([wcoB])
                k.release(Hb + zbB)

        def mla_layer(i, j):
            with ExitStack() as ph:
                QAN = sb(ph, "aQAN", [128, 3, NT], BF16)
                qanB = [Buf(QAN[:, :, t * 512:(t + 1) * 512]) for t in range(TB)]
                CKT = sb(ph, "aCKT", [128, 2, 2560], BF16)
                cktB = [Buf(CKT[:, :, t * 512:(t + 1) * 512]) for t in range(5)]
                KPT = sb(ph, "aKPT", [128, 2560], BF16)
                kptB = [Buf(KPT[:, t * 512:(t + 1) * 512]) for t in range(5)]
                for t in range(5):
                    op("dve", lambda e: e.memset(KPT[64:128, t * 512:(t + 1) * 512], 0.0), [], [kptB[t]])
                dma("pool", KPT[64:80, :], mk_d[0], writes=kptB)
                with ExitStack() as ph2:
                    H = sb(ph2, "aH", [128, 8, NT], BF16)
                    Hb = [Buf(H[:, :, t * 512:(t + 1) * 512]) for t in range(TB)]
                    WA = sb(ph2, "aWA", [128, 8, 768], BF16)
                    waB = Buf(WA)
                    wa_src = mla_w_a[j].rearrange("(c p) n -> p c n", p=128)
                    dma("pool", WA[:, :, 0:704], wa_src, writes=[waB])
                    for c in range(2):
                        dma("pool", CKT[:, c, 0:512], cckv_d[j, c * 128:(c + 1) * 128, :], writes=[cktB[0]])
                    dma("pool", KPT[0:64, 0:512], ckpe_d[j], writes=[kptB[0]])
                    tmp = modulate(ph2, i, 0, H, Hb)
                    for ax in range(2):
                        for hf in range(2):
                            d0 = 704 + ax * 32 + hf * 16
                            s0 = 640 + ax * 32 + (1 - hf) * 16
                            op("dve", lambda e: e.tensor_copy(out=WA[:, :, d0:d0 + 16], in_=WA[:, :, s0:s0 + 16]),
                               [waB], [waB])
                    SQ = [sb(ph2, "aSQ%d" % s, [128, 512], BF16) for s in range(2)]
                    sqB = [Buf(t) for t in SQ]
                    QG = sb(ph2, "aQG", [128, 3, 512])
                    qgB = Buf(QG)
                    RS = sb(ph2, "aRS", [128, 512])
                    rsB = Buf(RS)
                    CKF = sb(ph2, "aCKF", [128, 2, 512])
                    ckfB = Buf(CKF)
                    KPF = sb(ph2, "aKPF", [64, 512])
                    kpfB = Buf(KPF)
                    CSN = sb(ph2, "aCSN", [64, 2, 512])
                    csnB = Buf(CSN)
                    T1 = sb(ph2, "aT1", [64, 512])
                    t1B = Buf(T1)
                    T2 = sb(ph2, "aT2", [64, 512])
                    t2B = Buf(T2)
                    OST = [sb(ph2, "aOST%d" % s, [128, 320]) for s in range(2)]
                    ostB = [Buf(t) for t in OST]
                    for tb in range(TB):
                        tsl = slice(tb * 512, (tb + 1) * 512)
                        dma("sp", CSN[:, 0, :], rope_d[0, :, tsl], writes=[csnB])
                        dma("sp", CSN[:, 1, :], rope_d[1, :, tsl], writes=[csnB])
                        pq = [ps_next() for _ in range(3)]
                        for c in range(3):
                            for kc in range(8):
                                op("pe", lambda e, c=c, kc=kc, tsl=tsl: e.matmul(
                                    pq[c].ap[:], lhsT=WA[:, kc, c * 128:(c + 1) * 128], rhs=H[:, kc, tsl],
                                    start=(kc == 0), stop=(kc == 7)), [waB, Hb[tb]], [pq[c]])
                        rms_rstd((SQ, sqB), [pq[c].ap[:] for c in range(3)], 3, RS[:], rsB, pq, 1.0 / 384)
                        for c in range(3):
                            op("dve", lambda e, c=c: e.tensor_scalar(out=QG[:, c, :], in0=pq[c].ap[:],
                                                                     scalar1=c_("qng", j * 3 + c), scalar2=None,
                                                                     op0=ALU.mult), [pq[c], cstB], [qgB])
                        for c in range(3):
                            op("dve", lambda e, c=c, tsl=tsl: e.tensor_tensor(out=QAN[:, c, tsl], in0=QG[:, c, :],
                                                                              in1=RS[:], op=ALU.mult),
                               [qgB, rsB], [qanB[tb]])
                        pc = [ps_next() for _ in range(2)]
                        for c in range(2):
                            for kc in range(8):
                                op("pe", lambda e, c=c, kc=kc, tsl=tsl: e.matmul(
                                    pc[c].ap[:], lhsT=WA[:, kc, 384 + c * 128:384 + (c + 1) * 128], rhs=H[:, kc, tsl],
                                    start=(kc == 0), stop=(kc == 7)), [waB, Hb[tb]], [pc[c]])
                        rms_rstd((SQ, sqB), [pc[c].ap[:] for c in range(2)], 2, RS[:], rsB, pc, 1.0 / 256)
                        for c in range(2):
                            op("dve", lambda e, c=c: e.scalar_tensor_tensor(
                                out=CKF[:, c, :], in0=pc[c].ap[:], scalar=c_("kvg", j * 2 + c), in1=RS[:],
                                op0=ALU.mult, op1=ALU.mult), [pc[c], cstB, rsB], [ckfB])
                        op("act", lambda e, tb=tb: e.copy(out=CKT[:, :, 512 + tb * 512:512 + (tb + 1) * 512],
                                                          in_=CKF[:]), [ckfB], [cktB[1 + tb]])
                        pk = ps_next()
                        pks = ps_next()
                        for pp, c0 in ((pk, 640), (pks, 704)):
                            for kc in range(8):
                                op("pe", lambda e, pp=pp, c0=c0, kc=kc, tsl=tsl: e.matmul(
                                    pp.ap[0:64, :], lhsT=WA[:, kc, c0:c0 + 64], rhs=H[:, kc, tsl],
                                    start=(kc == 0), stop=(kc == 7)), [waB, Hb[tb]], [pp])
                        op("act", lambda e, pk=pk: e.copy(out=KPF[:], in_=pk.ap[0:64, :]), [pk], [kpfB])
                        op("dve", lambda e, pk=pk: e.tensor_tensor(out=T1[:], in0=pk.ap[0:64, :], in1=CSN[:, 0, :],
                                                                   op=ALU.mult), [pk, csnB], [t1B])
                        op("dve", lambda e, pks=pks: e.tensor_tensor(out=T2[:], in0=pks.ap[0:64, :], in1=CSN[:, 1, :],
                                                                     op=ALU.mult), [pks, csnB], [t2B])
                        op("dve", lambda e, tb=tb: e.tensor_tensor(
                            out=KPT[0:64, 512 + tb * 512:512 + (tb + 1) * 512], in0=T1[:], in1=T2[:], op=ALU.add),
                           [t1B, t2B], [kptB[1 + tb]])
                        for q in range(0 if 'noout' in _DBG else 4):
                            tt = tb * 4 + q
                            s = tt % 2
                            pb = ps_next()
                            for c in range(2):
                                op("pe", lambda e, pb=pb, c=c, q=q: e.transpose(
                                    out=pb.ap[:, c * 128:(c + 1) * 128], in_=CKF[:, c, q * 128:(q + 1) * 128],
                                    identity=identf), [ckfB, cstB], [pb])
                            op("pe", lambda e, pb=pb, q=q: e.transpose(
                                out=pb.ap[:, 256:320], in_=KPF[:, q * 128:(q + 1) * 128], identity=identf[0:64, 0:64]),
                               [kpfB, cstB], [pb])
                            op("act", lambda e, pb=pb, s=s: e.copy(out=OST[s][:], in_=pb.ap[:, 0:320]),
                               [pb], [ostB[s]])
                            dma("sp", ockv_d[j, tt * 128:(tt + 1) * 128, :], OST[s][:, 0:256], reads=[ostB[s]])
                            dma("sp", okpe_d[j, tt * 128:(tt + 1) * 128, :], OST[s][:, 256:320], reads=[ostB[s]])
                    k.barrier()
                    k.release(Hb + [waB, qgB, rsB, ckfB, kpfB, csnB, t1B, t2B] + sqB + ostB)
                with ExitStack() as ph2:
                    WQ = sb(ph2, "bWQ", [128, 3, 2048], BF16)
                    wqB = Buf(WQ)
                    wq_src = mla_w_q_b[j].rearrange("(c p) n -> p c n", p=128)
                    dma("pool", WQ[:, :, 0:1536], wq_src, writes=[wqB])
                    for kc in range(3):
                        dst = WQ[:, kc, 1536:2048].rearrange("p (h a f g) -> p h a f g", h=8, a=2, f=2)
                        srcv = WQ[:, kc, 0:1536].rearrange("p (h x) -> p h x", x=192)[:, :, 128:192].rearrange(
                            "p h (a f g) -> p h a f g", a=2, f=2)
                        for hf in range(2):
                            op("dve", lambda e: e.tensor_copy(out=dst[:, :, :, hf, :], in_=srcv[:, :, :, 1 - hf, :]),
                               [wqB], [wqB])
                    WKV = sb(ph2, "bWKV", [128, 2, 2048], BF16)
                    wkvB = Buf(WKV)
                    dma("pool", WKV[:], mla_w_kv_b[j].rearrange("(c p) n -> p c n", p=128), writes=[wkvB])
                    WO = sb(ph2, "bWO", [128, 2, D], BF16)
                    woB = Buf(WO)
                    KN = sb(ph2, "bKN", [128, 2, 2560], BF16)
                    knB = [[Buf(KN[:, hh, kb * 512:(kb + 1) * 512]) for kb in range(5)] for hh in range(2)]
                    V = sb(ph2, "bV", [128, 20, 2, 128], BF16)
                    vB = [Buf(V[:, kt]) for kt in range(20)]
                    OT = sb(ph2, "bOT", [128, 2, NT], BF16)
                    otB = [[Buf(OT[:, hh, t * 512:(t + 1) * 512]) for t in range(TB)] for hh in range(2)]
                    PT = [sb(ph2, "bPT%d" % s, [128, 512], BF16) for s in range(5)]
                    ptB = [Buf(t) for t in PT]
                    QN = [sb(ph2, "bQN%d" % s_, [128, 512], BF16) for s_ in range(2)]
                    qnB = [Buf(t) for t in QN]
                    QP = [sb(ph2, "bQP%d" % s_, [128, 512], BF16) for s_ in range(2)]
                    qpB = [Buf(t) for t in QP]
                    for s_ in range(2):
                        op("dve", lambda e: e.memset(QP[s_][64:128, :], 0.0), [], [qpB[s_]])
                    CSN = sb(ph2, "bCSN", [64, 2, 512])
                    csnB = Buf(CSN)
                    T1 = sb(ph2, "bT1", [64, 512])
                    t1B = Buf(T1)
                    T2 = sb(ph2, "bT2", [64, 512])
                    t2B = Buf(T2)
                    RI = sb(ph2, "bRI", [128, 512])
                    riB = Buf(RI)
                    RA = [sb(ph2, "bRA%d" % s_, [128, 512]) for s_ in range(2)]
                    raB = [Buf(t) for t in RA]
                    wo_src = mla_w_o[j].rearrange("(h p) n -> p h n", p=128)
                    wkv3 = WKV[:].rearrange("p c (h x) -> p c h x", x=256)
                    def wo_part():
                        for tb in range(0 if 'a3' in _DBG else TB):
                            for f in range(8):
                                pb = ps_next([6, 7])
                                for hh in range(2):
                                    op("pe", lambda e: e.matmul(
                                        pb.ap[:], lhsT=WO[:, hh, f * 128:(f + 1) * 128],
                                        rhs=OT[:, hh, tb * 512:(tb + 1) * 512], start=(hh == 0), stop=(hh == 1)),
                                       [woB, otB[hh][tb]], [pb])
                                resid_add(pb, f, tb, i, 16)

                    for hp in range(0 if 'noattn' in _DBG else 4):
                        for hh in range(0 if 'a1' in _DBG else 2):
                            h = 2 * hp + hh
                            for kb in range(5):
                                pb = ps_next([6, 7])
                                for c in range(2):
                                    op("pe", lambda e, pb=pb, c=c, h=h, kb=kb: e.matmul(
                                        pb.ap[:], lhsT=WKV[:, c, h * 256:h * 256 + 128],
                                        rhs=CKT[:, c, kb * 512:(kb + 1) * 512], start=(c == 0), stop=(c == 1)),
                                       [wkvB, cktB[kb]], [pb])
                                op("act", lambda e, pb=pb, hh=hh, kb=kb: e.copy(
                                    out=KN[:, hh, kb * 512:(kb + 1) * 512], in_=pb.ap[:]), [pb], [knB[hh][kb]])
                        for kt in range(0 if 'a4' in _DBG else 20):
                            pb = ps_next([6, 7])
                            for c in range(2):
                                op("pe", lambda e, pb=pb, c=c, kt=kt: e.matmul(
                                    pb.ap[:, 0:256].rearrange("p (h v) -> p h v", h=2),
                                    lhsT=CKT[:, c, kt * 128:(kt + 1) * 128],
                                    rhs=wkv3[:, c, 2 * hp:2 * hp + 2, 128:256], start=(c == 0), stop=(c == 1)),
                                   [wkvB, cktB[kt // 4]], [pb])
                            op("dve", lambda e, pb=pb, kt=kt: e.tensor_copy(
                                out=V[:, kt], in_=pb.ap[:, 0:256].rearrange("p (h v) -> p h v", h=2)), [pb], [vB[kt]])
                        if hp > 0:
                            wo_part()
                        dma("pool", WO[:], wo_src[:, 2 * hp:2 * hp + 2, :], writes=[woB])

                        def prep(hq):
                            hh_, qb_ = hq // TB, hq % TB
                            h_ = 2 * hp + hh_
                            sl_ = hq % 2
                            qsl_ = slice(qb_ * 512, (qb_ + 1) * 512)
                            dma("sp", CSN[:, 0, :], rope_d[0, :, qsl_], writes=[csnB])
                            dma("sp", CSN[:, 1, :], rope_d[1, :, qsl_], writes=[csnB])
                            pn = ps_next([6, 7])
                            for c in range(3):
                                op("pe", lambda e: e.matmul(
                                    pn.ap[:], lhsT=WQ[:, c, h_ * 192:h_ * 192 + 128], rhs=QAN[:, c, qsl_],
                                    start=(c == 0), stop=(c == 2)), [wqB, qanB[qb_]], [pn])
                            op("act", lambda e: e.copy(out=QN[sl_][:], in_=pn.ap[:]), [pn], [qnB[sl_]])
                            pr = ps_next([6, 7])
                            for c in range(3):
                                op("pe", lambda e: e.matmul(
                                    pr.ap[0:64, :], lhsT=WQ[:, c, h_ * 192 + 128:h_ * 192 + 192], rhs=QAN[:, c, qsl_],
                                    start=(c == 0), stop=(c == 2)), [wqB, qanB[qb_]], [pr])
                            op("dve", lambda e: e.tensor_tensor(out=T1[:], in0=pr.ap[0:64, :], in1=CSN[:, 0, :],
                                                                op=ALU.mult), [pr, csnB], [t1B])
                            prs = ps_next([6, 7])
                            for c in range(3):
                                op("pe", lambda e: e.matmul(
                                    prs.ap[0:64, :], lhsT=WQ[:, c, 1536 + h_ * 64:1536 + h_ * 64 + 64],
                                    rhs=QAN[:, c, qsl_], start=(c == 0), stop=(c == 2)), [wqB, qanB[qb_]], [prs])
                            op("dve", lambda e: e.tensor_tensor(out=T2[:], in0=prs.ap[0:64, :], in1=CSN[:, 1, :],
                                                                op=ALU.mult), [prs, csnB], [t2B])
                            dma("pool", QP[sl_][64:80, :], mk_d[1, :, qsl_], writes=[qpB[sl_]])
                            op("dve", lambda e: e.tensor_tensor(out=QP[sl_][0:64, :], in0=T1[:], in1=T2[:], op=ALU.add),
                               [t1B, t2B], [qpB[sl_]])

                        if 'a2' not in _DBG:
                            prep(0)
                        for hq in range(0 if 'a2' in _DBG else 2 * TB):
                            hh, qb = hq // TB, hq % TB
                            h = 2 * hp + hh
                            sl = hq % 2
                            qsl = slice(qb * 512, (qb + 1) * 512)
                            if True:
                                po, prw = PS[4], PS[5]

                                def s_tile(kt):
                                    pb = PS[kt % 4]
                                    op("pe", lambda e: e.matmul(pb.ap[:], lhsT=KN[:, hh, kt * 128:(kt + 1) * 128],
                                                                rhs=QN[sl][:], start=True, stop=False),
                                       [knB[hh][kt // 4], qnB[sl]], [pb])
                                    op("pe", lambda e: e.matmul(pb.ap[:], lhsT=KPT[:, kt * 128:(kt + 1) * 128],
                                                                rhs=QP[sl][:], start=False, stop=True),
                                       [kptB[kt // 4], qpB[sl]], [pb])
                                    op("act", lambda e: e.activation(out=PT[kt % 5][:], in_=pb.ap[:], func=AF.Exp,
                                                                     scale=MLA_SCALE), [pb], [ptB[kt % 5]])

                                def pv_tile(kt):
                                    op("pe", lambda e: e.matmul(po.ap[:], lhsT=V[:, kt, hh, :], rhs=PT[kt % 5][:],
                                                                start=(kt == 0), stop=(kt == 19)),
                                       [vB[kt], ptB[kt % 5]], [po])
                                    op("pe", lambda e: e.matmul(prw.ap[:], lhsT=onesb, rhs=PT[kt % 5][:],
                                                                start=(kt == 0), stop=(kt == 19)),
                                       [cbB, ptB[kt % 5]], [prw])
                                s_tile(0)
                                s_tile(1)
                                s_tile(2)
                                for kt in range(20):
                                    if kt + 3 < 20:
                                        s_tile(kt + 3)
                                    pv_tile(kt)
                                    if kt == 9 and hq + 1 < 2 * TB:
                                        prep(hq + 1)
                                op("act", lambda e: e.copy(out=RA[0][:], in_=po.ap[:]), [po], [raB[0]])
                                op("act", lambda e: e.copy(out=RA[1][:], in_=prw.ap[:]), [prw], [raB[1]])
                                op("dve", lambda e: e.reciprocal(out=RI[:], in_=RA[1][:]), [raB[1]], [riB])
                                op("dve", lambda e, hh=hh, qsl=qsl: e.tensor_tensor(out=OT[:, hh, qsl], in0=RA[0][:],
                                                                                    in1=RI[:], op=ALU.mult),
                                   [raB[0], riB], [otB[hh][qb]])
                    if 'noattn' not in _DBG:
                        wo_part()
                    k.barrier()
                    k.release([wqB, wkvB, woB, csnB, t1B, t2B, riB] + qnB + qpB + raB + ptB + vB + sum(knB, []) + sum(otB, []))
                k.release(qanB + cktB + kptB)

        def ret_layer(i):
            with ExitStack() as ph:
                H = sb(ph, "rH", [128, 8, NT], BF16)
                Hb = [Buf(H[:, :, t * 512:(t + 1) * 512]) for t in range(TB)]
                WR = sb(ph, "rWR", [128, 8, 1536], BF16)
                wrB = Buf(WR)
                WRO = sb(ph, "rWRO", [128, 4, D], BF16)
                wroB = Buf(WRO)
                OF = sb(ph, "rOF", [128, 16, 512], BF16)
                ofB = [Buf(OF[:, t]) for t in range(16)]
                U = sb(ph, "rU", [128, 2, 512])
                uB = Buf(U)
                UBs = [sb(ph, "rUB%d" % s_, [128, 2, 512], BF16) for s_ in range(2)]
                ubB = [Buf(t) for t in UBs]
                ubC = [[Buf(t[:, c, :]) for c in range(2)] for t in UBs]
                ubi = [0]
                SO = [sb(ph, "rSO%d" % s, [128, 2, 512]) for s in range(2)]
                soB = [Buf(t) for t in SO]
                soC = [[Buf(t[:, c, :]) for c in range(2)] for t in SO]
                QT = sb(ph, "rQT", [128, 2, 512], BF16)
                qtB = Buf(QT)
                KT = sb(ph, "rKT", [128, 2, 512], BF16)
                ktB = Buf(KT)
                KTM = sb(ph, "rKTM", [128, 4, 256], BF16)
                ktmB = [Buf(KTM[:, q]) for q in range(4)]
                VTM = sb(ph, "rVTM", [128, 4, 512], BF16)
                vtmB = [Buf(VTM[:, q]) for q in range(4)]
                PTM = [sb(ph, "rPTM%d" % s, [128, 128], BF16) for s in range(3)]
                ptmB = [Buf(t) for t in PTM]
                OSM = sb(ph, "rOSM", [128, 512])
                osmB = Buf(OSM)
                SG = sb(ph, "rSG", [128, 512])
                sgB = Buf(SG)
                GN = sb(ph, "rGN", [128, 512])
                gnB = Buf(GN)
                YB = [sb(ph, "rYB%d" % s_, [128, 512], BF16) for s_ in range(3)]
                ybB = [Buf(t) for t in YB]
                YT = sb(ph, "rYT", [128, 4, 512], BF16)
                ytB = Buf(YT)
                ST = sb(ph, "rST", [128, 16])
                stB = Buf(ST)
                win = ret_w_in[0].rearrange("(c p) n -> p c n", p=128)
                wout = ret_w_out[0].rearrange("(e p) n -> p e n", p=128)
                nso = 0
                for h in range(4):
                    dma("pool", WR[:, :, 0:256], win[:, :, h * 256:(h + 1) * 256], writes=[wrB])
                    dma("pool", WR[:, :, 256:512], win[:, :, 1024 + h * 256:1024 + (h + 1) * 256], writes=[wrB])
                    dma("pool", WR[:, :, 512:1024], win[:, :, 2048 + h * 512:2048 + (h + 1) * 512], writes=[wrB])
                    dma("pool", WR[:, :, 1024:1536], win[:, :, 4096 + h * 512:4096 + (h + 1) * 512], writes=[wrB])
                    dma("pool", WRO[:], wout[:, 4 * h:4 * h + 4, :], writes=[wroB])
                    dma("sp", GN[:], gng_d[:, h * 512:(h + 1) * 512], writes=[gnB])
                    if h == 0:
                        modulate(ph, i, 0, H, Hb)
                    for d in range(2):
                        dh = d * 4 + h
                        Acol = DEC[:, 0, dh:dh + 1]
                        Bcol = DEC[:, 1, dh:dh + 1]
                        Kcol = DEC[:, 2, dh:dh + 1]
                        Ccol = DEC[:, 3, dh:dh + 1]
                        mask = maskf if d == 0 else maskb
                        dma("sp", U[:], st0_d[d, h].rearrange("(c p) e -> p c e", p=128), writes=[uB])
                        cm0 = c_("cmf", 0) if d == 0 else c_("cmb", 15)
                        op("act", lambda e: e.activation(out=U[:], in_=U[:], func=AF.Identity, scale=cm0),
                           [uB, cstB], [uB])
                        op("act", lambda e: e.copy(out=UBs[ubi[0]][:], in_=U[:]), [uB],
                           [ubB[ubi[0]], ubC[ubi[0]][0], ubC[ubi[0]][1]])
                        scs = range(4) if d == 0 else range(3, -1, -1)
                        pending = []

                        def drain(keep):
                            while sum(1 for k_, _ in pending if k_ == "tr") > keep:
                                pending.pop(0)[1]()
                        for sc in scs:
                            tsl = slice(sc * 512, (sc + 1) * 512)
                            for c in range(2):
                                pb = ps_next()
                                for kc in range(8):
                                    op("pe", lambda e: e.matmul(
                                        pb.ap[:], lhsT=WR[:, kc, 256 + c * 128:256 + (c + 1) * 128], rhs=H[:, kc, tsl],
                                        start=(kc == 0), stop=(kc == 7)), [wrB, Hb[sc]], [pb])
                                op("dve", lambda e: e.tensor_copy(out=KT[:, c, :], in_=pb.ap[:]), [pb], [ktB])
                            for c in range(2):
                                pb = ps_next()
                                for kc in range(8):
                                    op("pe", lambda e: e.matmul(
                                        pb.ap[:], lhsT=WR[:, kc, c * 128:(c + 1) * 128], rhs=H[:, kc, tsl],
                                        start=(kc == 0), stop=(kc == 7)), [wrB, Hb[sc]], [pb])
                                op("act", lambda e: e.copy(out=QT[:, c, :], in_=pb.ap[:]), [pb], [qtB])
                            qs = list(range(4)) if d == 0 else [3, 2, 1, 0]

                            def proj_tm(q):
                                tok = slice(sc * 512 + q * 128, sc * 512 + (q + 1) * 128)
                                pb = ps_next()
                                pbv = pb.ap.bitcast(BF16)
                                for c in range(2):
                                    op("pe", lambda e: e.transpose(
                                        out=pbv[:, c * 128:(c + 1) * 128], in_=KT[:, c, q * 128:(q + 1) * 128],
                                        identity=identb), [ktB, cbB], [pb])
                                op("dve", lambda e: e.tensor_scalar(
                                    out=KTM[:, q, :], in0=pbv[:, 0:256], scalar1=Kcol, scalar2=None, op0=ALU.mult),
                                   [pb, decB], [ktmB[q]])
                                pb2 = ps_next()
                                for kc in range(8):
                                    op("pe", lambda e: e.matmul(
                                        pb2.ap[:], lhsT=H[:, kc, tok], rhs=WR[:, kc, 512:1024],
                                        start=(kc == 0), stop=(kc == 7)), [wrB, Hb[sc]], [pb2])
                                op("act", lambda e: e.copy(out=VTM[:, q, :], in_=pb2.ap[:]), [pb2], [vtmB[q]])

                            def st_tile(q):
                                loc = slice(q * 128, (q + 1) * 128)
                                pm_ = (sc * 4 + q) % 3
                                pb = ps_next()
                                for c in range(2):
                                    op("pe", lambda e: e.matmul(
                                        pb.ap[:, 0:128], lhsT=KT[:, c, loc], rhs=QT[:, c, loc],
                                        start=(c == 0), stop=(c == 1)), [ktB, qtB], [pb])
                                op("dve", lambda e: e.scalar_tensor_tensor(
                                    out=PTM[pm_][:], in0=pb.ap[:, 0:128], scalar=Acol, in1=mask,
                                    op0=ALU.mult, op1=ALU.mult), [pb, decB, cbB], [ptmB[pm_]])

                            proj_tm(qs[0])
                            st_tile(qs[0])
                            proj_tm(qs[1])
                            st_tile(qs[1])
                            for idx, q in enumerate(qs):
                                tt = sc * 4 + q
                                loc = slice(q * 128, (q + 1) * 128)
                                tok = slice(tt * 128, (tt + 1) * 128)
                                pm = tt % 3
                                if idx + 2 < 4:
                                    proj_tm(qs[idx + 2])
                                    st_tile(qs[idx + 2])
                                psts = []
                                for c in range(2):
                                    pst = ps_next()
                                    psts.append(pst)
                                    op("pe", lambda e: e.matmul(
                                        pst.ap[:], lhsT=KTM[:, q, c * 128:(c + 1) * 128], rhs=VTM[:, q, :],
                                        start=True, stop=True), [ktmB[q], vtmB[q]], [pst])
                                ucur = ubi[0]
                                ubi[0] = 1 - ubi[0]
                                so = nso % 2
                                nso += 1
                                for c in range(2):
                                    op("dve", lambda e: e.scalar_tensor_tensor(
                                        out=SO[so][:, c, :], in0=U[:, c, :], scalar=Ccol, in1=psts[c].ap[:],
                                        op0=ALU.mult, op1=ALU.add), [uB, decB, psts[c]], [soB[so], soC[so][c]])
                                if (d == 0 and tt % 2 == 1) or (d == 1 and tt % 2 == 0):
                                    dma("sp", ost_d[tt // 2, d, h].rearrange("(c p) e -> p c e", p=128), SO[so][:],
                                        reads=[soB[so]])
                                nxt = tt + 1 if d == 0 else tt - 1
                                if 0 <= nxt < 16:
                                    cm = c_("cmf" if d == 0 else "cmb", nxt)
                                    for c in range(2):
                                        op("act", lambda e: e.activation(
                                            out=UBs[1 - ucur][:, c, :], in_=SO[so][:, c, :], func=AF.Identity,
                                            scale=cm), [soC[so][c], cstB], [ubC[1 - ucur][c]])
                                    op("dve", lambda e: e.tensor_scalar(
                                        out=U[:], in0=SO[so][:], scalar1=cm, scalar2=None, op0=ALU.mult),
                                       [soB[so], cstB], [uB])
                                if d == 1:
                                    pg = ps_next()
                                    for kc in range(8):
                                        op("pe", lambda e: e.matmul(
                                            pg.ap[:], lhsT=H[:, kc, tok], rhs=WR[:, kc, 1024:1536],
                                            start=(kc == 0), stop=(kc == 7)), [wrB, Hb[sc]], [pg])
                                    op("act", lambda e: e.activation(out=SG[:], in_=pg.ap[:], func=AF.Silu),
                                       [pg], [sgB])
                                po = ps_next()
                                op("pe", lambda e: e.matmul(
                                    po.ap[:], lhsT=PTM[pm][:], rhs=VTM[:, q, :], start=True, stop=False),
                                   [ptmB[pm], vtmB[q]], [po])
                                for c in range(2):
                                    op("pe", lambda e: e.matmul(
                                        po.ap[:], lhsT=QT[:, c, loc], rhs=UBs[ucur][:, c, :], start=False,
                                        stop=(c == 1)), [qtB, ubC[ucur][c]], [po])
                                if d == 0:
                                    op("act", lambda e: e.activation(
                                        out=OF[:, tt, :], in_=po.ap[:], func=AF.Identity, scale=Bcol),
                                       [po, decB], [ofB[tt]])
                                else:
                                    yb = tt % 3
                                    op("dve", lambda e: e.scalar_tensor_tensor(
                                        out=OSM[:], in0=po.ap[:], scalar=Bcol, in1=OF[:, tt, :],
                                        op0=ALU.mult, op1=ALU.add), [po, decB, ofB[tt]], [osmB])
                                    op("dve", lambda e: e.bn_stats(out=ST[:, 0:6], in_=OSM[:]), [osmB], [stB])
                                    op("dve", lambda e: e.bn_aggr(out=ST[:, 8:10], in_=ST[:, 0:6]), [stB], [stB])
                                    op("dve", lambda e: e.tensor_scalar(
                                        out=ST[:, 10:11], in0=ST[:, 9:10], scalar1=EPS, scalar2=None,
                                        op0=ALU.add), [stB], [stB])
                                    op("act", lambda e: e.activation(out=ST[:, 10:11], in_=ST[:, 10:11], func=AF.Sqrt),
                                       [stB], [stB])
                                    op("dve", lambda e: e.reciprocal(out=ST[:, 10:11], in_=ST[:, 10:11]), [stB], [stB])
                                    op("dve", lambda e: e.tensor_scalar(
                                        out=OSM[:], in0=OSM[:], scalar1=ST[:, 8:9], scalar2=ST[:, 10:11],
                                        op0=ALU.subtract, op1=ALU.mult), [osmB, stB], [osmB])
                                    op("dve", lambda e: e.tensor_tensor(out=SG[:], in0=SG[:], in1=GN[:], op=ALU.mult),
                                       [sgB, gnB], [sgB])
                                    op("dve", lambda e: e.tensor_tensor(out=YB[yb][:], in0=OSM[:], in1=SG[:],
                                                                        op=ALU.mult), [osmB, sgB], [ybB[yb]])

                                    def tr(yb=yb, loc=loc):
                                        pt = ps_next()
                                        ptv = pt.ap.bitcast(BF16)
                                        for ec in range(4):
                                            op("pe", lambda e: e.transpose(
                                                out=ptv[:, ec * 128:(ec + 1) * 128],
                                                in_=YB[yb][:, ec * 128:(ec + 1) * 128], identity=identb),
                                               [ybB[yb], cbB], [pt])
                                        op("act", lambda e: e.copy(
                                            out=YT[:, :, loc], in_=ptv[:, 0:512].rearrange("p (e t) -> p e t", e=4)),
                                           [pt], [ytB])
                                    pending.append(("tr", tr))
                                    drain(2)
                            if d == 1:
                                def wout_fn(sc=sc):
                                    for f in range(8):
                                        pb = ps_next()
                                        for ec in range(4):
                                            op("pe", lambda e: e.matmul(
                                                pb.ap[:], lhsT=WRO[:, ec, f * 128:(f + 1) * 128], rhs=YT[:, ec, :],
                                                start=(ec == 0), stop=(ec == 3)), [wroB, ytB], [pb])
                                        resid_add(pb, f, sc, i, 16)
                                pending.append(("w", wout_fn))
                        while pending:
                            pending.pop(0)[1]()
                k.barrier()
                k.release(Hb + [wrB, wroB, uB, qtB, ktB, osmB, sgB, gnB, ytB, stB] + ubB + ybB + ofB + soB + ktmB
                          + vtmB + ptmB)

        for li, i in enumerate(layers):
            kind, j = i % 3, i // 3
            if kind == 0:
                mla_layer(i, j)
            elif kind == 1:
                conv_layer(i)
            else:
                ret_layer(i)
            ffn(i, layers[li + 1] if li + 1 < len(layers) else None, is_last=(li + 1 == len(layers)))

        with ExitStack() as ph:
          if not layers:
            SQ = [sb(ph, "oSQ%d" % s, [128, 512], BF16) for s in range(2)]
            sqB = [Buf(t) for t in SQ]
            RS = sb(ph, "oRS", [128, 512])
            rsB = Buf(RS)
            YF = sb(ph, "oYF", [128, 8, 512])
            yfB = Buf(YF)
            YS = [sb(ph, "oYS%d" % s, [128, D]) for s in range(2)]
            ysB = [Buf(t) for t in YS]
            for tb in range(TB):
                rms_rstd((SQ, sqB), [X[:, fc, tb * 512:(tb + 1) * 512] for fc in range(8)], 8, RS[:], rsB,
                         [Xb[tb]], 1.0 / D)
                for fc in range(8):
                    op("dve", lambda e, fc=fc, tb=tb: e.scalar_tensor_tensor(
                        out=YF[:, fc, :], in0=X[:, fc, tb * 512:(tb + 1) * 512], scalar=c_("gfin", fc), in1=RS[:],
                        op0=ALU.mult, op1=ALU.mult), [Xb[tb], cstB, rsB], [yfB])
                for q in range(4):
                    tt = tb * 4 + q
                    s = tt % 2
                    for half in range(2):
                        pb = ps_next()
                        for r in range(4):
                            fc = half * 4 + r
                            op("pe", lambda e, pb=pb, r=r, fc=fc, q=q: e.transpose(
                                out=pb.ap[:, r * 128:(r + 1) * 128], in_=YF[:, fc, q * 128:(q + 1) * 128],
                                identity=identf), [yfB, cstB], [pb])
                        if half == 0:
                            op("act", lambda e, pb=pb, s=s: e.copy(out=YS[s][:, 0:512], in_=pb.ap[:]), [pb], [ysB[s]])
                        else:
                            op("dve", lambda e, pb=pb, s=s: e.tensor_copy(out=YS[s][:, 512:1024], in_=pb.ap[:]),
                               [pb], [ysB[s]])
                    dma("sp", y_d[tt * 128:(tt + 1) * 128, :], YS[s][:], reads=[ysB[s]])
            k.barrier()
    return nc


def _const_table(cvec, inp, is_sample):
    t = np.zeros((128, NCST), np.float32)

    def put(name, arr):
        arr = np.asarray(arr, np.float32)
        t[:, _CO[name]:_CO[name] + arr.shape[1]] = arr

    def pp(v):
        v = np.asarray(v, np.float32)
        return v.reshape(-1, 128).T

    put("cv", pp(cvec))
    put("adab", np.concatenate([pp(inp["ada_b"][i]) for i in range(4)], axis=1))
    put("gmix", np.concatenate([pp(inp["norm_mix_g"][i]) for i in range(4)], axis=1))
    put("gffn", np.concatenate([pp(inp["norm_ffn_g"][i]) for i in range(4)], axis=1))
    put("gfin", pp(inp["final_norm_g"]))
    put("qng", np.concatenate([pp(inp["mla_q_norm_g"][j]) for j in range(2)], axis=1))
    put("kvg", np.concatenate([pp(inp["mla_kv_norm_g"][j]) for j in range(2)], axis=1))
    put("convw", np.concatenate([pp(inp["conv_w"][0][kk]) for kk in range(3)], axis=1))
    put("lr", np.broadcast_to(np.asarray(inp["ret_log_rate"][0], np.float32).reshape(1, 8), (128, 8)))
    p = np.arange(128, dtype=np.float32)
    coef = np.stack([p + 1, 128 - p, -(p + 1), -(128 - p), -(127 - p), -p, np.full(128, -128.0, np.float32),
                     np.zeros(128, np.float32)], axis=1)
    put("coef", coef)
    if is_sample:
        cmf = np.ones(16, np.float32)
        cmb = np.ones(16, np.float32)
    else:
        cmf = np.array([0.0 if n % 2 == 0 else 1.0 for n in range(16)], np.float32)
        cmb = np.array([0.0 if n % 2 == 1 else 1.0 for n in range(16)], np.float32)
    put("cmf", np.broadcast_to(cmf.reshape(1, 16), (128, 16)))
    put("cmb", np.broadcast_to(cmb.reshape(1, 16), (128, 16)))
    bt = np.zeros((20, 8), np.float32)
    if not is_sample:
        bt[:] = NEG
        for kt in range(4, 20):
            sk = (kt - 4) // 2
            bt[kt, sk] = 0.0
    put("btab", np.broadcast_to(bt.reshape(1, 160), (128, 160)))
    put("identf", np.eye(128, dtype=np.float32))
    put("ones", np.ones((128, 128), np.float32))
    jj = np.arange(128)[:, None]
    ii = np.arange(128)[None, :]
    put("maskf", (jj <= ii).astype(np.float32))
    put("maskb", (jj >= ii).astype(np.float32))
    return t


def _rope_tables(is_sample):
    r = np.zeros((2, 64, NT), np.float32)
    if not is_sample:
        r[0] = 1.0
        return r
    tok = np.arange(NT)
    row = (tok // 64).astype(np.float32)
    col = (tok % 64).astype(np.float32)
    inv = (np.float32(10000.0) ** (-np.arange(16, dtype=np.float32) / np.float32(16))).astype(np.float32)
    for ax, pos in enumerate((row, col)):
        ang = (pos[None, :] * inv[:, None]).astype(np.float32)
        c, s = np.cos(ang), np.sin(ang)
        for hf in range(2):
            p0 = ax * 32 + hf * 16
            r[0, p0:p0 + 16] = c
            r[1, p0:p0 + 16] = -s if hf == 0 else s
    return r


def _mask_factors(is_sample):
    big = 29952.0
    m = np.zeros((2, 16, 2560), np.float32)
    m[0, 15, :] = 1.0
    if not is_sample:
        m[0, 0, :] = -big
        m[1, 0, :2048] = 1.0
        for sq in range(8):
            m[0, 1 + sq, 512 + sq * 256:512 + (sq + 1) * 256] = big
            m[1, 1 + sq, sq * 256:(sq + 1) * 256] = 1.0
    return m


def _conv_masks(is_sample):
    seq = NT if is_sample else 256
    t = np.arange(NT)
    mp = (t % seq != 0).astype(np.float32)
    mn = (t % seq != seq - 1).astype(np.float32)
    return np.ascontiguousarray(np.broadcast_to(np.stack([mp, mn])[:, None, :], (2, 128, NT))).astype(np.float32)


_NC_CACHE = {}
_LAYERS = (0, 1, 2, 3)


def kernel(**inp):
    inp = {k_: np.asarray(v) for k_, v in inp.items()}
    if _LAYERS not in _NC_CACHE:
        _NC_CACHE[_LAYERS] = build(_LAYERS)
    nc = _NC_CACHE[_LAYERS]
    wnames = ["ada_w", "mla_w_a", "mla_w_q_b", "mla_w_kv_b", "mla_w_o", "conv_w_in", "conv_w_out", "ret_w_in",
              "ret_w_out", "ffn_w_in", "ffn_w_out"]
    shared = {n: np.ascontiguousarray(inp[n], dtype=np.float32) for n in wnames}
    gng = np.ascontiguousarray(np.broadcast_to(inp["ret_gn_g"][0].reshape(1, 2048), (128, 2048))).astype(np.float32)
    in_maps = []
    for core in range(8):
        is_s = core < 4
        m = dict(shared)
        if is_s:
            b = core
            m["x"] = np.ascontiguousarray(inp["x_sample"][b])
            cvec = inp["c"][b]
            m["st0"] = np.ascontiguousarray(inp["state_ret"][b, 0])
            m["cckv"] = np.ascontiguousarray(inp["cache_mla_ckv"][b].transpose(0, 2, 1))
            m["ckpe"] = np.ascontiguousarray(inp["cache_mla_kpe"][b].transpose(0, 2, 1))
        else:
            p = core - 4
            m["x"] = np.ascontiguousarray(inp["x_prompt"][8 * p:8 * p + 8].reshape(NT, D))
            cvec = inp["c_ctx"]
            m["st0"] = np.ascontiguousarray(inp["state_ret"][p, 0])
            m["cckv"] = np.ascontiguousarray(inp["cache_mla_ckv"][p].transpose(0, 2, 1))
            m["ckpe"] = np.ascontiguousarray(inp["cache_mla_kpe"][p].transpose(0, 2, 1))
        m["cst"] = _const_table(cvec, inp, is_s)
        m["rope"] = _rope_tables(is_s)
        m["cmask"] = _conv_masks(is_s)
        m["mk"] = _mask_factors(is_s)
        m["gng"] = gng
        in_maps.append(m)
    res = run_bass_kernel_spmd(nc, in_maps, core_ids=list(range(8)))
    R = res.results
    y_sample = np.stack([R[b]["y"] for b in range(4)], axis=0).astype(np.float32)
    y_prompt = np.concatenate([R[4 + p]["y"].reshape(8, 256, D) for p in range(4)], axis=0).astype(np.float32)
    ckv = np.concatenate([R[4 + p]["ockv"].reshape(2, 8, 256, 256).transpose(1, 0, 2, 3) for p in range(4)], axis=0)
    kpe = np.concatenate([R[4 + p]["okpe"].reshape(2, 8, 256, 64).transpose(1, 0, 2, 3) for p in range(4)], axis=0)
    st = np.concatenate([R[4 + p]["ost"].reshape(8, 1, 2, 4, 256, 512) for p in range(4)], axis=0)
    return (y_prompt, y_sample, np.ascontiguousarray(ckv, dtype=np.float32),
            np.ascontiguousarray(kpe, dtype=np.float32), np.ascontiguousarray(st, dtype=np.float32))
```
